# Optimizing a Trainium2 kernel written in Bass

```python
import jax, jax.numpy as jnp
from jax import lax
import numpy as np

D_MODEL = 4096
BATCH = 4
SEQ = 4096
DEPTH = 1

EPS = 1e-6
PLE_DIM = 256
MLA_HEADS = 16
QK_NOPE = 128
QK_ROPE = 64
QK_HEAD = QK_NOPE + QK_ROPE
V_HEAD = 128
Q_LORA = 768
KV_LORA = 512
ROPE_THETA = 10000.0
Q_BLOCK = 128
HG_HEADS = 16
HG_EXPAND = 128
HG_HEAD_V = 128
HG_FDIM = HG_HEADS * HG_EXPAND
HG_VDIM = HG_HEADS * HG_HEAD_V
CHUNK = 64
MIX_WIDTH = MLA_HEADS * V_HEAD + HG_VDIM
D_FF = 4 * D_MODEL
IN_WIDTH = Q_LORA + KV_LORA + QK_ROPE + HG_FDIM + HG_FDIM + HG_VDIM + HG_VDIM
IN_SPLITS = (
    Q_LORA,
    Q_LORA + KV_LORA,
    Q_LORA + KV_LORA + QK_ROPE,
    Q_LORA + KV_LORA + QK_ROPE + HG_FDIM,
    Q_LORA + KV_LORA + QK_ROPE + 2 * HG_FDIM,
    Q_LORA + KV_LORA + QK_ROPE + 2 * HG_FDIM + HG_VDIM,
)

kernel_name = "hymba_mla_hgrn2_relu2_ple"


def rms_norm(x, g):
    xf = x.astype(jnp.float32)
    y = xf * lax.rsqrt(jnp.mean(xf * xf, axis=-1, keepdims=True) + EPS)
    return (y * g.astype(jnp.float32)).astype(x.dtype)


def rope_tables(positions):
    inv_freq = ROPE_THETA ** (-jnp.arange(0, QK_ROPE, 2, dtype=jnp.float32) / QK_ROPE)
    ang = positions.astype(jnp.float32)[..., None] * inv_freq
    return jnp.cos(ang), jnp.sin(ang)


def apply_rope(t, cos, sin):
    tf = t.astype(jnp.float32)
    t1, t2 = jnp.split(tf, 2, axis=-1)
    return jnp.concatenate([t1 * cos - t2 * sin, t2 * cos + t1 * sin], axis=-1).astype(t.dtype)


def causal_block_attention(q, k, v):
    B, S, H, _ = q.shape
    nb = S // Q_BLOCK
    scale = QK_HEAD ** -0.5
    qb = q.reshape(B, nb, Q_BLOCK, H, QK_HEAD).transpose(1, 0, 2, 3, 4)
    key_pos = jnp.arange(S)

    def one_block(args):
        qi, blk = args
        s = jnp.einsum('bqhd,bkhd->bhqk', qi, k, preferred_element_type=jnp.float32) * scale
        q_pos = blk * Q_BLOCK + jnp.arange(Q_BLOCK)
        s = jnp.where(key_pos[None, :] <= q_pos[:, None], s, -jnp.inf)
        pr = jax.nn.softmax(s, axis=-1).astype(v.dtype)
        return jnp.einsum('bhqk,bkhd->bqhd', pr, v)

    out = lax.map(one_block, (qb, jnp.arange(nb)))
    return out.transpose(1, 0, 2, 3, 4).reshape(B, S, H, V_HEAD)


def hgrn2_chunked(q, k, v, logf):
    B, S, H, dk = q.shape
    dv = v.shape[-1]
    nc = S // CHUNK

    def to_chunks(t):
        return t.reshape(B, nc, CHUNK, H, t.shape[-1]).transpose(1, 0, 3, 2, 4)

    causal = jnp.tril(jnp.ones((CHUNK, CHUNK), dtype=bool))

    def step(state, inp):
        qc, kc, vc, gc = inp
        b = jnp.cumsum(gc, axis=2)
        o_inter = jnp.einsum('bhtk,bhkv->bhtv', qc * jnp.exp(b), state)
        diff = b[:, :, :, None, :] - b[:, :, None, :, :]
        decay = jnp.exp(jnp.where(causal[:, :, None], diff, -jnp.inf))
        a = jnp.einsum('bhtk,bhtsk,bhsk->bhts', qc, decay, kc)
        o = o_inter + jnp.einsum('bhts,bhsv->bhtv', a, vc)
        b_last = b[:, :, -1:, :]
        new_state = jnp.exp(b_last[:, :, 0, :])[..., None] * state + jnp.einsum(
            'bhsk,bhsv->bhkv', kc * jnp.exp(b_last - b), vc)
        return new_state, o

    s0 = jnp.zeros((B, H, dk, dv), jnp.float32)
    _, o = lax.scan(step, s0, (to_chunks(q), to_chunks(k), to_chunks(v), to_chunks(logf)))
    return o.transpose(1, 0, 3, 2, 4).reshape(B, S, H, dv)


def setup_inputs(seed: int = 0) -> dict:
    key = jax.random.key(seed)
    ks = jax.random.split(key, 24)
    f32 = jnp.float32

    def w(k, shape, fan_in):
        return jax.random.normal(k, shape, f32) * (fan_in ** -0.5)

    def gain(k, shape):
        return 1.0 + 0.02 * jax.random.normal(k, shape, f32)

    x = jax.random.normal(ks[0], (BATCH, SEQ, D_MODEL), f32)
    p = jax.random.normal(ks[1], (DEPTH, BATCH, SEQ, PLE_DIM), f32)
    positions = (jax.random.randint(ks[2], (BATCH, 1), 0, 1024, jnp.int32)
                 + jnp.arange(SEQ, dtype=jnp.int32)[None, :])
    return {
        "x": x,
        "p": p,
        "positions": positions,
        "norm_mix": gain(ks[3], (DEPTH, D_MODEL)),
        "w_in": w(ks[4], (DEPTH, D_MODEL, IN_WIDTH), D_MODEL),
        "q_a_norm": gain(ks[5], (DEPTH, Q_LORA)),
        "kv_a_norm": gain(ks[6], (DEPTH, KV_LORA)),
        "w_uq": w(ks[7], (DEPTH, Q_LORA, MLA_HEADS * QK_HEAD), Q_LORA),
        "w_ukv": w(ks[8], (DEPTH, KV_LORA, MLA_HEADS * (QK_NOPE + V_HEAD)), KV_LORA),
        "hg_lower_bound": 0.5 * jax.random.normal(ks[9], (DEPTH + 1, HG_FDIM), f32),
        "hg_out_norm": gain(ks[10], (DEPTH, HG_VDIM)),
        "w_o": w(ks[11], (DEPTH, MIX_WIDTH, D_MODEL), MIX_WIDTH),
        "norm_mlp": gain(ks[12], (DEPTH, D_MODEL)),
        "w_up": w(ks[13], (DEPTH, D_MODEL, D_FF), D_MODEL),
        "w_down": w(ks[14], (DEPTH, D_FF, D_MODEL), D_FF),
        "norm_ple": gain(ks[15], (DEPTH, D_MODEL)),
        "w_ple_gate": w(ks[16], (DEPTH, D_MODEL, D_MODEL), D_MODEL),
        "w_ple": w(ks[17], (DEPTH, PLE_DIM, D_MODEL), PLE_DIM),
        "ple_post_norm": gain(ks[18], (DEPTH, D_MODEL)),
        "final_norm": gain(ks[19], (D_MODEL,)),
    }


def reference(x, p, positions, norm_mix, w_in, q_a_norm, kv_a_norm, w_uq, w_ukv,
              hg_lower_bound, hg_out_norm, w_o, norm_mlp, w_up, w_down,
              norm_ple, w_ple_gate, w_ple, ple_post_norm, final_norm):
    B, S, _ = x.shape
    cos, sin = rope_tables(positions)
    lb_all = jnp.cumsum(jax.nn.softmax(hg_lower_bound.astype(jnp.float32), axis=0), axis=0)
    h = x
    for i in range(DEPTH):
        u = rms_norm(h, norm_mix[i])
        proj = u @ w_in[i]
        c_q, c_kv, k_r, hq, hf, hi, hg = jnp.split(proj, IN_SPLITS, axis=-1)

        q = (rms_norm(c_q, q_a_norm[i]) @ w_uq[i]).reshape(B, S, MLA_HEADS, QK_HEAD)
        q_nope, q_rope = jnp.split(q, [QK_NOPE], axis=-1)
        q = jnp.concatenate([q_nope, apply_rope(q_rope, cos[:, :, None], sin[:, :, None])], axis=-1)
        kv = (rms_norm(c_kv, kv_a_norm[i]) @ w_ukv[i]).reshape(B, S, MLA_HEADS, QK_NOPE + V_HEAD)
        k_nope, v = jnp.split(kv, [QK_NOPE], axis=-1)
        k_rope = apply_rope(k_r, cos, sin)
        k = jnp.concatenate(
            [k_nope, jnp.broadcast_to(k_rope[:, :, None, :], (B, S, MLA_HEADS, QK_ROPE))], axis=-1)
        o_mla = causal_block_attention(q, k, v).reshape(B, S, MLA_HEADS * V_HEAD)

        lb = lb_all[i]
        zf = hf.astype(jnp.float32)
        f = lb + (1.0 - lb) * jax.nn.sigmoid(zf)
        k_h = ((1.0 - lb) * jax.nn.sigmoid(-zf)).reshape(B, S, HG_HEADS, HG_EXPAND)
        logf = jnp.log(f).reshape(B, S, HG_HEADS, HG_EXPAND)
        q_h = jax.nn.silu(hq.astype(jnp.float32)).reshape(B, S, HG_HEADS, HG_EXPAND)
        v_h = hi.astype(jnp.float32).reshape(B, S, HG_HEADS, HG_HEAD_V)
        o_h = hgrn2_chunked(q_h, k_h, logf=logf, v=v_h)
        o_h = rms_norm(o_h, hg_out_norm[i].reshape(HG_HEADS, HG_HEAD_V))
        o_h = o_h * jax.nn.silu(hg.astype(jnp.float32)).reshape(B, S, HG_HEADS, HG_HEAD_V)
        o_h = o_h.reshape(B, S, HG_VDIM).astype(x.dtype)

        h = h + jnp.concatenate([o_mla, o_h], axis=-1) @ w_o[i]

        hidden = jnp.square(jax.nn.relu(rms_norm(h, norm_mlp[i]) @ w_up[i]))
        h = h + hidden @ w_down[i]

        gate = jax.nn.sigmoid(rms_norm(h, norm_ple[i]) @ w_ple_gate[i])
        e = rms_norm(p[i] @ w_ple[i], ple_post_norm[i])
        h = h + gate * e
    return rms_norm(h, final_norm)
```

```python
import math
from contextlib import ExitStack
import numpy as np
import ml_dtypes
import concourse.bass as bass
import concourse.mybir as mybir
from concourse.bass_utils import run_bass_kernel_spmd

F32 = mybir.dt.float32
BF16 = mybir.dt.bfloat16
I32 = mybir.dt.int32
AF = mybir.ActivationFunctionType
ALU = mybir.AluOpType
AX = mybir.AxisListType
ENGS = ("pe", "act", "dve", "pool", "sp")
EPS = 1e-6
PI = math.pi


class Buf:
    __slots__ = ("name", "t", "w", "rs", "sem_w", "cnt_w", "sem_r", "cnt_r", "excl")

    def __init__(self, name, t=None):
        self.excl = False
        self.name = name
        self.t = t
        self.w = None
        self.rs = []
        self.sem_w = None
        self.cnt_w = 0
        self.sem_r = None
        self.cnt_r = 0


class Op:
    __slots__ = ("eng", "fn", "deps", "sig", "sigval", "is_dma", "dsem", "dval")

    def __init__(self, eng, fn):
        self.eng = eng
        self.fn = fn
        self.deps = []
        self.sig = False
        self.sigval = 0
        self.is_dma = False
        self.dsem = None
        self.dval = 0


class Sched:
    def __init__(self, nc, stack):
        self.nc = nc
        self.stack = stack
        self.top = stack
        self.streams = {e: [] for e in ENGS}
        self.esem = {e: stack.enter_context(nc.semaphore("es_" + e)) for e in ENGS}
        self.nsem = 0
        self.base = {}
        self.bar = {}
        self.nbuf = 0

    def new_sem(self, name):
        self.nsem += 1
        return self.top.enter_context(self.nc.semaphore("ds%d" % self.nsem))

    def sbuf(self, name, shape, dt, top=False):
        self.nbuf += 1
        st = self.top if top else self.stack
        t = st.enter_context(self.nc.sbuf_tensor("%s_%d" % (name, self.nbuf), list(shape), dt))
        return Buf(name, t)

    def psum(self, name, shape, dt):
        self.nbuf += 1
        t = self.stack.enter_context(self.nc.psum_tensor("%s_%d" % (name, self.nbuf), list(shape), dt))
        b = Buf(name, t)
        b.excl = True
        return b

    def _deps(self, op, reads, writes):
        deps = []
        for r in reads:
            if r.w is not None:
                deps.append(r.w)
            if r.excl:
                deps.extend(p for p in r.rs if p.eng != op.eng)
        for w in writes:
            if w.w is not None:
                deps.append(w.w)
            deps.extend(w.rs)
        op.deps = [d for d in deps if d is not op]
        for r in reads:
            r.rs.append(op)
        for w in writes:
            w.w = op
            w.rs = []

    def op(self, eng, fn, reads=(), writes=()):
        o = Op(eng, fn)
        self._deps(o, reads, writes)
        self.streams[eng].append(o)
        return o

    def dma(self, q, out_ap, in_ap, reads=(), writes=(), sem_buf=None, kind="w"):
        def fn(e):
            return e.dma_start(out=out_ap, in_=in_ap)
        o = Op(q, fn)
        o.is_dma = True
        if kind == "w":
            if sem_buf.sem_w is None:
                sem_buf.sem_w = self.new_sem(sem_buf.name)
            sem_buf.cnt_w += 16
            o.dsem, o.dval = sem_buf.sem_w, sem_buf.cnt_w
        else:
            if sem_buf.sem_r is None:
                sem_buf.sem_r = self.new_sem(sem_buf.name)
            sem_buf.cnt_r += 16
            o.dsem, o.dval = sem_buf.sem_r, sem_buf.cnt_r
        self._deps(o, reads, writes)
        self.streams[q].append(o)
        return o

    def flush(self, final_waits=()):
        r = self.finalize(final_waits)
        pend = []
        for e in ENGS:
            lastc = None
            for o in self.streams[e]:
                if o.is_dma:
                    pend.append(o)
                else:
                    lastc = o
            if lastc is not None:
                pend.append(lastc)
        self.base = {e: sum(1 for o in self.streams[e] if o.sig and not o.is_dma) + self.base.get(e, 0)
                     for e in ENGS}
        self.streams = {e: [] for e in ENGS}
        old = self.bar
        self.bar = {e: list(pend) + list(old.get(e, [])) for e in ENGS}
        return r

    def finalize(self, final_waits=()):
        for e in ENGS:
            if self.bar.get(e) and self.streams[e]:
                o0 = self.streams[e][0]
                o0.deps = list(o0.deps) + [d for d in self.bar[e] if d is not o0]
                self.bar[e] = []
        for e in ENGS:
            for o in self.streams[e]:
                for d in o.deps:
                    if not d.is_dma:
                        if d.eng == "pe" and o.eng == "pe" and not o.is_dma:
                            continue
                        d.sig = True
        for o in final_waits:
            if not o.is_dma:
                o.sig = True
        for e in ENGS:
            for o in reversed(self.streams[e]):
                if not o.is_dma:
                    o.sig = True
                    break
        for e in ENGS:
            c = self.base.get(e, 0)
            for o in self.streams[e]:
                if o.sig and not o.is_dma:
                    c += 1
                    o.sigval = c
        streams = self.streams
        esem = self.esem
        stats = {e: [0, 0] for e in ENGS}

        proto = {e: [] for e in ENGS}
        self.proto = proto

        def emit(e, eng):
            waited = {}
            for o in streams[e]:
                mywaits = []
                need = {}
                for d in o.deps:
                    if d.is_dma:
                        key, val = d.dsem, d.dval
                    else:
                        if d.eng == "pe" and e == "pe" and not o.is_dma:
                            continue
                        key, val = esem[d.eng], d.sigval
                    if val > need.get(key, 0):
                        need[key] = val
                for key, val in need.items():
                    if val > waited.get(key, 0):
                        eng.wait_ge(key, val)
                        waited[key] = val
                        stats[e][1] += 1
                        mywaits.append((id(key), val))
                proto[e].append((mywaits, (id(o.dsem), 16) if o.is_dma else ((id(esem[e]), 1) if o.sig else None)))
                ins = o.fn(eng)
                stats[e][0] += 1
                if o.is_dma:
                    ins.then_inc(o.dsem, 16)
                elif o.sig:
                    ins.then_inc(esem[e], 1)
            if e == "sp":
                for o in final_waits:
                    if o.is_dma:
                        eng.wait_ge(o.dsem, o.dval)
                    else:
                        eng.wait_ge(esem[o.eng], o.sigval)

        with self.nc.Block() as block:
            @block.tensor
            def _(eng):
                emit("pe", eng)

            @block.scalar
            def _(eng):
                emit("act", eng)

            @block.vector
            def _(eng):
                emit("dve", eng)

            @block.gpsimd
            def _(eng):
                emit("pool", eng)

            @block.sync
            def _(eng):
                emit("sp", eng)
        if not hasattr(self, "semval"):
            self.semval = {}
        semval = self.semval
        ptr = {e: 0 for e in ENGS}
        progress = True
        while progress:
            progress = False
            for e in ENGS:
                while ptr[e] < len(proto[e]):
                    waits, inc = proto[e][ptr[e]]
                    if all(semval.get(k, 0) >= v for k, v in waits):
                        if inc is not None:
                            semval[inc[0]] = semval.get(inc[0], 0) + inc[1]
                        ptr[e] += 1
                        progress = True
                    else:
                        break
        stuck = {e: (ptr[e], len(proto[e])) for e in ENGS if ptr[e] < len(proto[e])}
        if stuck:
            print("DEADLOCK in emitted protocol:", stuck)
            for e in stuck:
                waits, inc = proto[e][ptr[e]]
                print("  ", e, "waiting", [(k, v, semval.get(k, 0)) for k, v in waits])
        return stats


class Ring:
    def __init__(self, S, n, name):
        self.S = S
        self.slots = [S.sbuf("%s%d" % (name, i), [128, 4096], BF16) for i in range(n)]
        self.i = 0

    def load(self, src, width=4096):
        sl = self.slots[self.i % len(self.slots)]
        self.i += 1
        b = min(width, 2048)
        self.S.dma("pool", sl.t[:, 0:width].rearrange("p (a b) -> p a b", b=b),
                   src.rearrange("p (a b) -> p a b", b=b), writes=[sl], sem_buf=sl)
        return sl


LAST_DRAM = []
DBG_COPY = False
SMALL = ()
import os
KN_SUB = int(os.environ.get('KN_SUB', '4'))
KN_EVAC = os.environ.get('KN_EVAC', 'dve')
KN_TPSEP = int(os.environ.get('KN_TPSEP', '0'))
KN_NCH = int(os.environ.get('KN_NCH', '32'))
KN_H = int(os.environ.get('KN_H', '9'))
KN_LEAD = int(os.environ.get('KN_LEAD', '6'))
KN_RING = int(os.environ.get('KN_RING', '3'))
KN_ESTEP = int(os.environ.get('KN_ESTEP', '1'))
KN_RINGC = int(os.environ.get('KN_RINGC', '5'))
KN_CQ = int(os.environ.get('KN_CQ', '2048'))
NTILE_PRE = 4
NTILE_OWN = 4
QSCALE = 192 ** -0.5


def build_program(phases="ABC", alim=0, tiles=None):
    nc = bass.Bass("TRN2", target_bir_lowering=False)

    LAST_DRAM.clear()

    def dram(name, shape, dt, kind="ExternalInput"):
        if SMALL and name in SMALL:
            shape = [1] + list(shape[1:])
        if kind == "ExternalInput":
            LAST_DRAM.append((name, tuple(shape), "bf16" if dt == BF16 else ("i32" if dt == I32 else "f32")))
        return nc.dram_tensor(name, list(shape), dt, kind=kind).ap()

    xs = dram("xs", [4096, 4096], F32)
    posr = dram("posr", [64, 4096], I32)
    pp = dram("pp", [2048, 256], F32)
    pbias_d = dram("pbias", [128, 1], F32)
    w_in_fm = dram("w_in_fm", [33, 128, 4096], F32)
    w_in_tm = dram("w_in_tm", [16, 128, 4096], F32)
    w_in_tmh = dram("w_in_tmh", [64, 128, 2048], F32)
    w_in_tmp = dram("w_in_tmp", [64, 128, 1024], F32)
    w_attn = dram("w_attn", [16, 128, 4096], F32)
    w_o_tm = dram("w_o_tm", [32, 128, 4096], F32)
    w_up_fm = dram("w_up_fm", [128, 128, 4096], F32)
    w_dn_tm = dram("w_dn_tm", [128, 128, 4096], F32)
    w_pg_tm = dram("w_pg_tm", [32, 128, 4096], F32)
    w_pe_t = dram("w_pe_t", [2, 128, 4096], F32)
    g_mix_d = dram("g_mix", [128, 32], F32)
    g_mlp_d = dram("g_mlp", [128, 32], F32)
    g_ple_d = dram("g_ple", [128, 32], F32)
    g_qa_d = dram("g_qa", [128, 6], F32)
    g_kva_d = dram("g_kva", [128, 4], F32)
    g_hg_d = dram("g_hg", [128, 16], F32)
    lbraw_d = dram("lbraw", [128, 32], F32)
    g_post_d = dram("g_post", [128, 4096], F32)
    g_fin_d = dram("g_fin", [128, 4096], F32)
    ident_d = dram("ident", [128, 128], F32)
    ones_d = dram("ones", [128, 128], BF16)
    maskbd_d = dram("maskbd", [128, 128], BF16)
    cme_d = dram("cme", [128, 512], BF16)
    cmo_d = dram("cmo", [128, 512], BF16)
    rme_d = dram("rme", [128, 2], F32)
    rmask_d = dram("rmask", [128, 512], F32)
    cmask_d = dram("cmask", [128, 2048], BF16)
    invf_d = dram("invf", [64, 2], F32)
    out_d = dram("out", [2048, 4096], F32, kind="ExternalOutput")
    mix_d = dram("mixT", [4096, 2048], BF16, kind="Internal")

    with ExitStack() as top:
        S = Sched(nc, top)
        MIX = Buf("MIX")

        def const(name, shape, dt, src, top_=True):
            b = S.sbuf(name, shape, dt, top=top_)
            S.dma("sp", b.t[:], src, writes=[b], sem_buf=b)
            return b

        ident = const("ident", [128, 128], F32, ident_d)
        invf = const("invf", [64, 2], F32, invf_d)
        mid = ExitStack()
        S.stack = mid
        ckvnT = S.sbuf("ckvnT", [128, 4, 4096], BF16)
        krT = S.sbuf("krT", [64, 4096], BF16)
        cqnT = S.sbuf("cqnT", [128, 6, KN_CQ], BF16)

        def rstd_from_ss(rstd, ss, dim):
            S.op("dve", lambda e: e.tensor_scalar(out=rstd.t[:], in0=ss.t[:], scalar1=1.0 / dim, scalar2=EPS,
                                                  op0=ALU.mult, op1=ALU.add), reads=[ss], writes=[rstd])
            S.op("act", lambda e: e.activation(out=rstd.t[:], in_=rstd.t[:], func=AF.Sqrt), reads=[rstd], writes=[rstd])
            S.op("dve", lambda e: e.reciprocal(out=rstd.t[:], in_=rstd.t[:]), reads=[rstd], writes=[rstd])

        cnt = [0]

        def evac_scaled(out_ap, in_ap, scale_ap, reads, writes):
            if KN_EVAC == "act":
                S.op("act", lambda e: e.copy(out=out_ap, in_=in_ap), reads=reads, writes=writes)
            elif DBG_COPY:
                S.op("dve", lambda e: e.tensor_copy(out=out_ap, in_=in_ap), reads=reads, writes=writes)
            else:
                S.op("dve", lambda e: e.tensor_scalar(out=out_ap, in0=in_ap, scalar1=scale_ap, scalar2=None,
                                                      op0=ALU.mult), reads=reads, writes=writes)

        def copy_any(out_ap, in_ap, reads, writes):
            cnt[0] += 1
            if cnt[0] % 2:
                S.op("act", lambda e: e.copy(out=out_ap, in_=in_ap), reads=reads, writes=writes)
            else:
                S.op("dve", lambda e: e.tensor_copy(out=out_ap, in_=in_ap), reads=reads, writes=writes)

        def norm_rows(src, src_ap, width, yb, ss, rstd):
            S.op("dve", lambda e: e.memset(ss.t[:], 0.0), writes=[ss])
            S.op("act", lambda e: e.activation(out=yb.t[:, 0:width], in_=src_ap, func=AF.Square, accum_out=ss.t[:]),
                 reads=[src, ss], writes=[yb, ss])
            rstd_from_ss(rstd, ss, width)
            S.op("dve", lambda e: e.tensor_scalar(out=yb.t[:, 0:width], in0=src_ap, scalar1=rstd.t[:], scalar2=None,
                                                  op0=ALU.mult), reads=[src, rstd], writes=[yb])

        def transpose_out(yb, nchunk, tps, dst_fn, gain, dst_bufs, noevac=False):
            for c0 in range(0, nchunk, 4):
                tp = tps[(c0 // 4) % 2]
                n = min(4, nchunk - c0)
                for j in range(n):
                    c = c0 + j
                    S.op("pe", lambda e, c=c, j=j, tp=tp: e.transpose(tp.t[:, j * 128:(j + 1) * 128],
                                                                      yb.t[:, c * 128:(c + 1) * 128], ident.t[:]),
                         reads=[yb, ident], writes=[tp])
                for j in range(n):
                    if noevac:
                        break
                    c = c0 + j
                    evac_scaled(dst_fn(c), tp.t[:, j * 128:(j + 1) * 128], gain.t[:, c:c + 1],
                                reads=[tp, gain], writes=dst_bufs(c))

        def rope_gen(src, posi_b, posi_ap, ang_b, ang_ap, tcs_b, tcs_ap, tsn_b, tsn_ap, cosb, cos_ap, sinb, sin_ap):
            S.dma("sp", posi_ap, src, writes=[posi_b], sem_buf=posi_b)
            S.op("dve", lambda e: e.tensor_copy(out=ang_ap, in_=posi_ap), reads=[posi_b], writes=[ang_b])
            S.op("dve", lambda e: e.tensor_scalar(out=ang_ap, in0=ang_ap, scalar1=invf.t[:, 0:1], scalar2=None,
                                                  op0=ALU.mult), reads=[ang_b, invf], writes=[ang_b])
            for which in (0, 1):
                shift = 0.5 * PI if which == 0 else 0.0
                S.op("dve", lambda e, shift=shift: e.tensor_scalar(out=tcs_ap, in0=ang_ap, scalar1=shift, scalar2=None, op0=ALU.add),
                     reads=[ang_b], writes=[tcs_b])
                S.op("dve", lambda e: e.tensor_scalar(out=tsn_ap, in0=tcs_ap, scalar1=1.0 / (2 * PI), scalar2=None, op0=ALU.mult),
                     reads=[tcs_b], writes=[tsn_b])
                S.op("dve", lambda e: e.tensor_copy(out=posi_ap, in_=tsn_ap), reads=[tsn_b], writes=[posi_b])
                S.op("dve", lambda e: e.tensor_copy(out=tsn_ap, in_=posi_ap), reads=[posi_b], writes=[tsn_b])
                S.op("dve", lambda e: e.scalar_tensor_tensor(out=tcs_ap, in0=tsn_ap, scalar=-2 * PI, in1=tcs_ap, op0=ALU.mult, op1=ALU.add),
                     reads=[tsn_b, tcs_b], writes=[tcs_b])
                S.op("dve", lambda e: e.tensor_scalar(out=tsn_ap, in0=tcs_ap, scalar1=PI, scalar2=-2 * PI, op0=ALU.is_gt, op1=ALU.mult),
                     reads=[tcs_b], writes=[tsn_b])
                S.op("dve", lambda e: e.tensor_tensor(out=tcs_ap, in0=tcs_ap, in1=tsn_ap, op=ALU.add), reads=[tcs_b, tsn_b], writes=[tcs_b])
                S.op("dve", lambda e: e.tensor_scalar(out=tcs_ap, in0=tcs_ap, scalar1=-PI, scalar2=PI, op0=ALU.max, op1=ALU.min),
                     reads=[tcs_b], writes=[tcs_b])
                if which == 0:
                    S.op("act", lambda e: e.activation(out=cos_ap, in_=tcs_ap, func=AF.Sin), reads=[tcs_b], writes=[cosb])
                else:
                    S.op("act", lambda e: e.activation(out=sin_ap, in_=tcs_ap, func=AF.Sin, scale=invf.t[:, 1:2]),
                         reads=[tcs_b, invf], writes=[sinb])

        if "A" in phases:
            with ExitStack() as pa:
                S.stack = pa
                g_mix = const("g_mix", [128, 32], F32, g_mix_d, False)
                g_qa = const("g_qa", [128, 6], F32, g_qa_d, False)
                g_kva = const("g_kva", [128, 4], F32, g_kva_d, False)
                g_hg = const("g_hg", [128, 16], F32, g_hg_d, False)
                lbraw = const("lbraw", [128, 32], F32, lbraw_d, False)
                maskbd = const("maskbd", [128, 128], BF16, maskbd_d, False)
                cme = const("cme", [128, 512], BF16, cme_d, False)
                cmo = const("cmo", [128, 512], BF16, cmo_d, False)
                rme = const("rme", [128, 2], F32, rme_d, False)
                rmask = const("rmask", [128, 512], F32, rmask_d, False)
                ring = Ring(S, KN_RING, "wra")
                xsb = S.sbuf("xsb", [128, 4096], F32)
                yb = S.sbuf("yb", [128, 4096], F32)
                uT = S.sbuf("uT", [128, 32, 512], BF16)
                uTp = [Buf("uTp%d" % i) for i in range(4)]
                class _CC:
                    def __init__(self, ap):
                        self.ap = ap
                    def __getitem__(self, k):
                        return self.ap[k]
                cc = [xsb, xsb, xsb, yb]
                ccv = [_CC(xsb.t[:, 0:1280]), _CC(xsb.t[:, 1280:2560]), _CC(xsb.t[:, 2560:3840]), _CC(yb.t[:, 0:1280])]
                ss = S.sbuf("ss", [128, 1], F32)
                rstd = S.sbuf("rstd", [128, 1], F32)
                tm = [S.psum("tm%d" % i, [128, 512], F32) for i in range(4)]
                fm = [S.psum("fm%d" % i, [128, 512], F32) for i in range(2)]
                tpt = S.psum("tp", [128, 512], F32)
                tpt2 = S.psum("tp2", [128, 512], F32)
                tps = [Buf("tpa", None), Buf("tpb", None)]
                tps[0].excl = tps[1].excl = True
                class _V:
                    def __init__(self, t, off):
                        self.t_, self.off = t, off
                    def __getitem__(self, k):
                        rows, cols = k
                        return self.t_[rows, self.off + cols.start:self.off + cols.stop]
                tps[0].t = _V(tpt.t, 0)
                tps[1].t = _V(tpt2.t, 0)
                smA = smS = tps[0]
                smO = smT1 = smT2 = tps[1]
                apA = tpt.t[:, 0:128]
                apS = tpt.t[:, 128:256]
                apO = tpt2.t[:, 0:128]
                apT1 = tpt2.t[:, 128:256]
                apT2 = tpt2.t[:, 256:384]
                lb = S.sbuf("lb", [128, 16], F32)
                oml = S.sbuf("oml", [128, 16], F32)
                S.op("dve", lambda e: e.tensor_tensor(out=lb.t[:], in0=lbraw.t[:, 0:16], in1=lbraw.t[:, 16:32],
                                                      op=ALU.subtract), reads=[lbraw], writes=[lb])
                S.op("act", lambda e: e.activation(out=lb.t[:], in_=lb.t[:], func=AF.Sigmoid), reads=[lb], writes=[lb])
                S.op("dve", lambda e: e.tensor_scalar(out=oml.t[:], in0=lb.t[:], scalar1=-1.0, scalar2=1.0,
                                                      op0=ALU.mult, op1=ALU.add), reads=[lb], writes=[oml])
                Sst = [S.sbuf("S%d" % h, [128, 128], F32) for h in range(16)]
                Seb2 = [S.sbuf("Seb%d" % h, [128, 128], BF16) for h in range(2)]
                Sob = S.sbuf("Sob", [128, 128], BF16)
                for h in range(16):
                    S.op("dve", lambda e, h=h: e.memset(Sst[h].t[:], 0.0), writes=[Sst[h]])
                def f32t(n):
                    return S.sbuf(n, [128, 512], F32)
                def b16t(n):
                    return S.sbuf(n, [128, 512], BF16)
                escr = S.sbuf("escr", [128, 3072], F32)
                t_sg, t_lf, t_bb, t_qs, t_ex, t_kk = [Buf(n_, escr.t[:, i_ * 512:(i_ + 1) * 512]) for i_, n_ in enumerate(("sg", "lf", "bb", "qs", "ex", "kk"))]
                t_Am = b16t("Am")
                t_qe2 = [b16t("qe%d" % i) for i in range(2)]
                t_qee2 = [b16t("qee%d" % i) for i in range(2)]
                t_qeo2 = [b16t("qeo%d" % i) for i in range(2)]
                t_ke2 = [b16t("ke%d" % i) for i in range(2)]
                t_ke32 = [f32t("ke3%d" % i) for i in range(2)]
                elast2 = [S.sbuf("elast%d" % i, [128, 8], F32) for i in range(2)]
                Vv = [S.sbuf("Vv%d" % i, [128, 4, 128], BF16) for i in range(2)]
                Ve = [S.sbuf("Ve%d" % i, [128, 4, 128], BF16) for i in range(2)]
                Vo = [S.sbuf("Vo%d" % i, [128, 4, 128], BF16) for i in range(2)]
                gs = [S.sbuf("gs%d" % i, [128, 4, 128], F32) for i in range(2)]
                onesf = S.sbuf("onesf", [128, 1], F32)
                S.op("dve", lambda e: e.memset(onesf.t[:], 1.0), writes=[onesf])
                ke3T4 = S.sbuf("ke3T4", [128, 512], BF16)
                ke3T = S.sbuf("ke3T", [128, 128], BF16)
                onb = S.sbuf("onb", [128, 128], F32)
                oTs = [S.sbuf("oT%d" % i, [128, 512], BF16) for i in range(2)]
                sso = S.sbuf("sso", [128, 1], F32)
                epst = S.sbuf("epst", [128, 1], F32)
                S.op("dve", lambda e: e.memset(epst.t[:], EPS), writes=[epst])
                rso = S.sbuf("rso", [128, 1], F32)
                junk = S.sbuf("junk", [128, 128], F32)
                class _T:
                    def __init__(self, ap):
                        self.ap = ap
                    def __getitem__(self, k):
                        return self.ap[k]
                def _alias(b, dt=None):
                    nb_ = Buf(b.name)
                    nb_.__class__ = Buf
                    return b
                posi_b, ang_b, tcs_b, tsn_b, cosk, sink = t_kk, t_sg, t_lf, t_bb, t_qs, t_ex
                posi_ap = t_kk.t[0:64, :].bitcast(I32)
                ang_ap, tcs_ap, tsn_ap = t_sg.t[0:64, :], t_lf.t[0:64, :], t_bb.t[0:64, :]
                cosk_ap, sink_ap = t_qs.t[0:64, :], t_ex.t[0:64, :]
                ybc = Buf("ybc", escr.t[:, 0:1280])

                def rope_tables(tokbase, cos_ap, sin_ap, cosb, sinb):
                    rope_gen(posr[:, tokbase:tokbase + 512], posi_b, posi_ap, ang_b, ang_ap, tcs_b, tcs_ap, tsn_b, tsn_ap,
                             cosb, cos_ap, sinb, sin_ap)

                def gemm_tm(tiles, width, evac):
                    for q in range(4):
                        sl = ring.load(tiles[q], 8 * width)
                        for s in range(4):
                            for kk in range(8):
                                kc = q * 8 + kk
                                S.op("pe", lambda e, s=s, kc=kc, kk=kk, sl=sl, q=q: e.matmul(
                                    tm[s].t[:, 0:width], lhsT=uT.t[:, kc, s * 128:(s + 1) * 128],
                                    rhs=sl.t[:, kk * width:(kk + 1) * width],
                                    start=(kc == 0), stop=(kc == 31)), reads=[uTp[q], sl], writes=[tm[s]])
                    for s in range(4):
                        evac(s)

                def gemm_fm(tile, outs):
                    sl = ring.load(tile)
                    for (pb, pap, c0, ncol) in outs:
                        for kc in range(32):
                            S.op("pe", lambda e, kc=kc, pap=pap, c0=c0, ncol=ncol, sl=sl: e.matmul(
                                pap, lhsT=sl.t[:, kc * 128 + c0: kc * 128 + c0 + ncol], rhs=uT.t[:, kc, :],
                                start=(kc == 0), stop=(kc == 31)), reads=[uTp[kc // 8], sl], writes=[pb])

                for T in (tiles if tiles is not None else range(NTILE_PRE + NTILE_OWN)):
                    own = T >= NTILE_PRE
                    tok0 = T * 512
                    otok0 = (T - NTILE_PRE) * 512
                    for s in range(KN_SUB):
                        r0 = tok0 + s * 128
                        S.dma("sp", xsb.t[:], xs[r0:r0 + 128, :], writes=[xsb], sem_buf=xsb)
                        norm_rows(xsb, xsb.t[:], 4096, yb, ss, rstd)
                        if alim == 11:
                            continue
                        transpose_out(yb, KN_NCH, tps, lambda c, s=s: uT.t[:, c, s * 128:(s + 1) * 128], g_mix,
                                      (lambda c: []) if alim == 13 else (lambda c: [uTp[c // 8]]), noevac=(alim == 12))
                    if alim in (1, 11, 12, 13):
                        continue
                    groups = [(2, 1024, 256)] if not own else [(0, 0, 512), (1, 512, 256), (2, 768 + 256, 256)]
                    glist = ([0, 1] if own else []) + [2, 3]
                    for g in glist:
                        coff = {0: 0, 1: 512, 2: 768, 3: 1024}[g]
                        width = 512 if g == 0 else 256
                        def ev(s, coff=coff, width=width):
                            copy_any(ccv[s][:, coff:coff + width], tm[s].t[:, 0:width], [tm[s]], [cc[s]])
                        gemm_tm([w_in_tm[g * 4 + q][:, 0:8 * width] for q in range(4)], width, ev)
                    for s in range(4):
                        if own:
                            norm_rows(cc[s], ccv[s][:, 0:768], 768, ybc, ss, rstd)
                            transpose_out(ybc, 6, tps,
                                          lambda c, s=s: cqnT.t[:, c, otok0 + s * 128: otok0 + (s + 1) * 128], g_qa,
                                          lambda c: [cqnT])
                        S.op("dve", lambda e: e.memset(ss.t[:], 0.0), writes=[ss])
                        S.op("act", lambda e, s=s: e.activation(out=ybc.t[:, 0:512], in_=ccv[s][:, 768:1280], func=AF.Square,
                                                                accum_out=ss.t[:]), reads=[cc[s], ss], writes=[ybc, ss])
                        rstd_from_ss(rstd, ss, 512)
                        S.op("dve", lambda e, s=s: e.tensor_scalar(out=ybc.t[:, 0:512], in0=ccv[s][:, 768:1280], scalar1=rstd.t[:],
                                                                   scalar2=None, op0=ALU.mult), reads=[cc[s], rstd], writes=[ybc])
                        transpose_out(ybc, 4, tps,
                                      lambda c, s=s: ckvnT.t[:, c, tok0 + s * 128: tok0 + (s + 1) * 128], g_kva,
                                      lambda c: [ckvnT])
                    if alim == 2:
                        continue
                    gemm_fm(w_in_fm[0], [(fm[0], fm[0].t[0:64, :], 0, 64), (fm[1], fm[1].t[0:64, :], 64, 64)])
                    rope_tables(tok0, cosk_ap, sink_ap, cosk, sink)
                    S.op("dve", lambda e: e.tensor_tensor(out=tcs_ap, in0=fm[0].t[0:64, :], in1=cosk_ap, op=ALU.mult),
                         reads=[fm[0], cosk], writes=[tcs_b])
                    S.op("dve", lambda e: e.tensor_tensor(out=tsn_ap, in0=fm[1].t[0:64, :], in1=sink_ap, op=ALU.mult),
                         reads=[fm[1], sink], writes=[tsn_b])
                    S.op("dve", lambda e, tok0=tok0: e.tensor_tensor(out=krT.t[:, tok0:tok0 + 512], in0=tcs_ap, in1=tsn_ap,
                                                                     op=ALU.add), reads=[tcs_b, tsn_b], writes=[krT])
                    if alim == 3:
                        continue
                    nh = 16 if alim == 0 else 2

                    def G_gen(h, own=own):
                        vs = h % 2
                        fms = ([(1 + 2 * h, fm[0])] if own else []) + [(2 + 2 * h, fm[1])]
                        for (ti, fb_) in fms:
                            sl = ring.load(w_in_fm[ti])
                            for kc in range(32):
                                S.op("pe", lambda e, kc=kc, fb_=fb_, sl=sl: e.matmul(
                                    fb_.t[:], lhsT=sl.t[:, kc * 128:(kc + 1) * 128], rhs=uT.t[:, kc, :],
                                    start=(kc == 0), stop=(kc == 31)), reads=[uTp[kc // 8], sl], writes=[fb_])
                                if kc % 8 == 7:
                                    yield "fm"
                        if own:
                            tiles = [w_in_tmh[h * 4 + q] for q in range(4)]
                            width = 256
                        else:
                            tiles = [w_in_tmp[h * 4 + q] for q in range(4)]
                            width = 128
                        n = 0
                        for q in range(4):
                            sl = ring.load(tiles[q], 8 * width)
                            for s in range(4):
                                for kk in range(8):
                                    kc = q * 8 + kk
                                    S.op("pe", lambda e, s=s, kc=kc, kk=kk, sl=sl, width=width: e.matmul(
                                        tm[s].t[:, 0:width], lhsT=uT.t[:, kc, s * 128:(s + 1) * 128],
                                        rhs=sl.t[:, kk * width:(kk + 1) * width],
                                        start=(kc == 0), stop=(kc == 31)), reads=[uTp[q], sl], writes=[tm[s]])
                                    n += 1
                                    if n % 8 == 0 and n < 128:
                                        yield "tm"
                        for s in range(4):
                            vsrc = tm[s].t[:, 0:128]
                            if own:
                                S.op("act", lambda e, s=s, vs=vs, vsrc=vsrc: e.copy(out=Vv[vs].t[:, s, :], in_=vsrc),
                                     reads=[tm[s]], writes=[Vv[vs]])
                                S.op("act", lambda e, s=s, vs=vs: e.activation(out=gs[vs].t[:, s, :], in_=tm[s].t[:, 128:256], func=AF.Silu),
                                     reads=[tm[s]], writes=[gs[vs]])
                            else:
                                copy_any(Vv[vs].t[:, s, :], vsrc, [tm[s]], [Vv[vs]])
                                continue
                            S.op("dve", lambda e, s=s, vs=vs, vsrc=vsrc: e.tensor_scalar(out=Ve[vs].t[:, s, :], in0=vsrc, scalar1=rme.t[:, 0:1],
                                                                                         scalar2=None, op0=ALU.mult), reads=[tm[s], rme], writes=[Ve[vs]])
                            S.op("dve", lambda e, s=s, vs=vs, vsrc=vsrc: e.tensor_scalar(out=Vo[vs].t[:, s, :], in0=vsrc, scalar1=rme.t[:, 1:2],
                                                                                         scalar2=None, op0=ALU.mult), reads=[tm[s], rme], writes=[Vo[vs]])
                        yield "tm"

                    def E_gen(h, own=own):
                        par = h % 2
                        t_qe, t_qee, t_qeo, t_ke, t_ke3, elast = t_qe2[par], t_qee2[par], t_qeo2[par], t_ke2[par], t_ke32[par], elast2[par]
                        S.op("act", lambda e: e.activation(out=t_sg.t[:], in_=fm[1].t[:], func=AF.Sigmoid), reads=[fm[1]], writes=[t_sg])
                        if own:
                            yield
                            S.op("act", lambda e: e.activation(out=t_qs.t[:], in_=fm[0].t[:], func=AF.Silu), reads=[fm[0]], writes=[t_qs])
                        yield
                        S.op("dve", lambda e, h=h: e.tensor_scalar(out=t_sg.t[:], in0=t_sg.t[:], scalar1=oml.t[:, h:h + 1], scalar2=lb.t[:, h:h + 1],
                                                                   op0=ALU.mult, op1=ALU.add), reads=[t_sg, oml, lb], writes=[t_sg])
                        yield
                        S.op("act", lambda e: e.activation(out=t_lf.t[:], in_=t_sg.t[:], func=AF.Ln), reads=[t_sg], writes=[t_lf])
                        yield
                        S.op("dve", lambda e: e.tensor_scalar(out=t_kk.t[:], in0=t_sg.t[:], scalar1=-1.0, scalar2=1.0,
                                                              op0=ALU.mult, op1=ALU.add), reads=[t_sg], writes=[t_kk])
                        if not own:
                            yield
                            S.op("dve", lambda e: e.tensor_tensor_scan(out=t_bb.t[:], data0=onesf.t[:, 0:1].to_broadcast([128, 512]), data1=t_lf.t[:], initial=0.0,
                                                                       op0=ALU.mult, op1=ALU.add), reads=[onesf, t_lf], writes=[t_bb])
                            yield
                            S.op("act", lambda e: e.activation(out=elast.t[:, 0:1], in_=t_bb.t[:, 511:512], func=AF.Exp), reads=[t_bb], writes=[elast])
                            yield
                            S.op("dve", lambda e: e.tensor_scalar(out=t_ex.t[:], in0=t_bb.t[:], scalar1=t_bb.t[:, 511:512], scalar2=-1.0,
                                                                  op0=ALU.subtract, op1=ALU.mult), reads=[t_bb], writes=[t_ex])
                            yield
                            S.op("act", lambda e: e.activation(out=t_ex.t[:], in_=t_ex.t[:], func=AF.Exp), reads=[t_ex], writes=[t_ex])
                            yield
                            S.op("dve", lambda e: e.tensor_tensor(out=t_ke3.t[:], in0=t_kk.t[:], in1=t_ex.t[:], op=ALU.mult),
                                 reads=[t_kk, t_ex], writes=[t_ke3])
                            yield
                            return
                        yield
                        S.op("dve", lambda e: e.tensor_tensor_scan(out=t_bb.t[:], data0=rmask.t[:], data1=t_lf.t[:], initial=0.0,
                                                                   op0=ALU.mult, op1=ALU.add), reads=[rmask, t_lf], writes=[t_bb])
                        bbv = t_bb.t[:].rearrange("p (c t) -> p c t", t=64)
                        yield
                        S.op("act", lambda e, bbv=bbv: e.activation(out=elast.t[:], in_=bbv[:, :, 63], func=AF.Exp), reads=[t_bb], writes=[elast])
                        yield
                        S.op("dve", lambda e, bbv=bbv: e.tensor_tensor(out=t_ex.t[:].rearrange("p (c t) -> p c t", t=64),
                                                                       in0=bbv[:, :, 63:64].to_broadcast([128, 8, 64]), in1=bbv, op=ALU.subtract),
                             reads=[t_bb], writes=[t_ex])
                        yield
                        S.op("act", lambda e: e.activation(out=t_ex.t[:], in_=t_ex.t[:], func=AF.Exp), reads=[t_ex], writes=[t_ex])
                        yield
                        S.op("dve", lambda e: e.tensor_tensor(out=t_ke3.t[:], in0=t_kk.t[:], in1=t_ex.t[:], op=ALU.mult),
                             reads=[t_kk, t_ex], writes=[t_ke3])
                        if own:
                            yield
                            S.op("act", lambda e: e.activation(out=t_ex.t[:], in_=t_bb.t[:], func=AF.Exp), reads=[t_bb, t_ke3], writes=[t_ex])
                            yield
                            S.op("dve", lambda e: e.tensor_tensor(out=t_qe.t[:], in0=t_qs.t[:], in1=t_ex.t[:], op=ALU.mult),
                                 reads=[t_qs, t_ex], writes=[t_qe])
                            yield
                            S.op("act", lambda e: e.activation(out=t_ex.t[:], in_=t_bb.t[:], func=AF.Exp, scale=-1.0), reads=[t_bb], writes=[t_ex])
                            yield
                            S.op("dve", lambda e: e.tensor_tensor(out=t_qee.t[:], in0=t_qe.t[:], in1=cme.t[:], op=ALU.mult),
                                 reads=[t_qe, cme], writes=[t_qee])
                            yield
                            S.op("dve", lambda e: e.tensor_tensor(out=t_qeo.t[:], in0=t_qe.t[:], in1=cmo.t[:], op=ALU.mult),
                                 reads=[t_qe, cmo], writes=[t_qeo])
                            yield
                            S.op("dve", lambda e: e.tensor_tensor(out=t_ke.t[:], in0=t_kk.t[:], in1=t_ex.t[:], op=ALU.mult),
                                 reads=[t_kk, t_ex], writes=[t_ke])

                    def P_gen(h, own=own, otok0=otok0):
                        vs = h % 2
                        oT = oTs[h % 2]
                        par = h % 2
                        t_qe, t_qee, t_qeo, t_ke, t_ke3, elast = t_qe2[par], t_qee2[par], t_qeo2[par], t_ke2[par], t_ke32[par], elast2[par]
                        Sebh = Seb2[par]
                        if own:
                            S.op("act", lambda e, h=h: e.copy(out=Sebh.t[:], in_=Sst[h].t[:]), reads=[Sst[h]], writes=[Sebh])
                        if not own:
                            for pr in range(4):
                                S.op("pe", lambda e, pr=pr: e.transpose(tpt2.t[:, pr * 128:(pr + 1) * 128], t_ke3.t[:, pr * 128:(pr + 1) * 128], ident.t[:]),
                                     reads=[t_ke3, ident], writes=[smT1])
                            S.op("act", lambda e: e.copy(out=ke3T4.t[:], in_=tpt2.t[:]), reads=[smT1], writes=[ke3T4])
                            yield
                            for pr in range(4):
                                S.op("pe", lambda e, pr=pr, vs=vs: e.matmul(apS, lhsT=ke3T4.t[:, pr * 128:(pr + 1) * 128], rhs=Vv[vs].t[:, pr, :],
                                                                            start=(pr == 0), stop=(pr == 3)), reads=[ke3T4, Vv[vs]], writes=[smS])
                            S.op("dve", lambda e, h=h: e.scalar_tensor_tensor(out=Sst[h].t[:], in0=Sst[h].t[:], scalar=elast.t[:, 0:1],
                                                                              in1=apS, op0=ALU.mult, op1=ALU.add),
                                 reads=[Sst[h], elast, smS], writes=[Sst[h]])
                            yield
                            return
                        for pr in range(4):
                            cs = slice(pr * 128, pr * 128 + 128)
                            if own:
                                S.op("pe", lambda e, cs=cs: e.matmul(apA, lhsT=t_ke.t[:, cs], rhs=t_qe.t[:, cs], start=True, stop=True),
                                     reads=[t_ke, t_qe], writes=[smA])
                                S.op("dve", lambda e, cs=cs: e.tensor_tensor(out=t_Am.t[:, cs], in0=apA, in1=maskbd.t[:], op=ALU.mult),
                                     reads=[smA, maskbd], writes=[t_Am])
                            S.op("pe", lambda e, cs=cs: e.transpose(apT1, t_ke3.t[:, cs], ident.t[:]), reads=[t_ke3, ident], writes=[smT1])
                            S.op("act", lambda e: e.copy(out=ke3T.t[:], in_=apT1), reads=[smT1], writes=[ke3T])
                            yield
                            S.op("pe", lambda e, pr=pr, vs=vs: e.matmul(apS, lhsT=ke3T.t[:], rhs=Ve[vs].t[:, pr, :], start=True, stop=True),
                                 reads=[ke3T, Ve[vs]], writes=[smS])
                            S.op("dve", lambda e, h=h, pr=pr: e.scalar_tensor_tensor(out=Sst[h].t[:], in0=Sst[h].t[:], scalar=elast.t[:, 2 * pr:2 * pr + 1],
                                                                                     in1=apS, op0=ALU.mult, op1=ALU.add),
                                 reads=[Sst[h], elast, smS], writes=[Sst[h]])
                            if own:
                                S.op("act", lambda e, h=h: e.copy(out=Sob.t[:], in_=Sst[h].t[:]), reads=[Sst[h]], writes=[Sob])
                            yield
                            if own:
                                S.op("pe", lambda e, cs=cs, pr=pr, vs=vs: e.matmul(apO, lhsT=t_Am.t[:, cs], rhs=Vv[vs].t[:, pr, :], start=True, stop=False),
                                     reads=[t_Am, Vv[vs]], writes=[smO])
                                S.op("pe", lambda e, cs=cs, h=h: e.matmul(apO, lhsT=t_qee.t[:, cs], rhs=Sebh.t[:], start=False, stop=False),
                                     reads=[t_qee, Sebh], writes=[smO])
                                S.op("pe", lambda e, cs=cs: e.matmul(apO, lhsT=t_qeo.t[:, cs], rhs=Sob.t[:], start=False, stop=True),
                                     reads=[t_qeo, Sob], writes=[smO])
                            S.op("pe", lambda e, pr=pr, vs=vs: e.matmul(apS, lhsT=ke3T.t[:], rhs=Vo[vs].t[:, pr, :], start=True, stop=True),
                                 reads=[ke3T, Vo[vs]], writes=[smS])
                            S.op("dve", lambda e, h=h, pr=pr: e.scalar_tensor_tensor(out=Sst[h].t[:], in0=Sst[h].t[:], scalar=elast.t[:, 2 * pr + 1:2 * pr + 2],
                                                                                     in1=apS, op0=ALU.mult, op1=ALU.add),
                                 reads=[Sst[h], elast, smS], writes=[Sst[h]])
                            if own:
                                S.op("act", lambda e, h=h: e.copy(out=Sebh.t[:], in_=Sst[h].t[:]), reads=[Sst[h]], writes=[Sebh])
                                S.op("dve", lambda e: e.memset(sso.t[:], 0.0), writes=[sso])
                                S.op("act", lambda e: e.activation(out=junk.t[:], in_=apO, func=AF.Square, accum_out=sso.t[:]),
                                     reads=[smO, sso], writes=[junk, sso])
                                S.op("act", lambda e: e.activation(out=rso.t[:], in_=sso.t[:], func=AF.Sqrt, bias=epst.t[:], scale=1.0 / 128),
                                     reads=[sso, epst], writes=[rso])
                                S.op("dve", lambda e: e.reciprocal(out=rso.t[:], in_=rso.t[:]), reads=[rso], writes=[rso])
                                S.op("dve", lambda e, pr=pr, vs=vs: e.scalar_tensor_tensor(out=onb.t[:], in0=apO, scalar=rso.t[:], in1=gs[vs].t[:, pr, :],
                                                                                           op0=ALU.mult, op1=ALU.mult), reads=[smO, rso, gs[vs]], writes=[onb])
                            yield
                            if own:
                                yield
                                S.op("pe", lambda e: e.transpose(apT2, onb.t[:], ident.t[:]), reads=[onb, ident], writes=[smT2])
                                S.op("dve", lambda e, cs=cs, h=h, oT=oT: e.tensor_scalar(out=oT.t[:, cs], in0=apT2, scalar1=g_hg.t[:, h:h + 1], scalar2=None, op0=ALU.mult),
                                     reads=[smT2, g_hg], writes=[oT])
                                yield
                        if own:
                            S.dma("sp", mix_d[2048 + h * 128: 2048 + (h + 1) * 128, otok0:otok0 + 512], oT.t[:],
                                  reads=[oT], writes=[MIX], sem_buf=oT, kind="r")

                    def drain(g):
                        for _ in g:
                            pass

                    def interleave3(g, p, e_, nfm):
                        gi = iter(g) if g is not None else None
                        pi = iter(p) if p is not None else None
                        ei = iter(e_) if e_ is not None else None
                        gcount = 0
                        while gi is not None or pi is not None or ei is not None:
                            if gi is not None:
                                if next(gi, None) is None:
                                    gi = None
                                gcount += 1
                            if pi is not None and next(pi, "end") == "end":
                                pi = None
                            if ei is not None and (gcount > nfm or gi is None):
                                for _ in range(KN_ESTEP):
                                    if next(ei, "end") == "end":
                                        ei = None
                                        break

                    drain(G_gen(0))
                    drain(E_gen(0))
                    nfm = 8 if own else 4
                    for i in range(nh):
                        if i + 1 < nh:
                            interleave3(G_gen(i + 1), P_gen(i), E_gen(i + 1), nfm)
                        else:
                            drain(P_gen(i))
                print("phase A", S.flush())

        if "B" in phases:
            with ExitStack() as pb_:
                S.stack = pb_
                ones = const("ones", [128, 128], BF16, ones_d, False)
                cmask = const("cmask", [128, 2048], BF16, cmask_d, False)
                pbias = const("pbias", [128, 1], F32, pbias_d, False)
                ring = Ring(S, 2, "wrb")
                cosq = S.sbuf("cosq", [64, 2048], F32)
                sinq = S.sbuf("sinq", [64, 2048], F32)
                posi = S.sbuf("posi", [64, 512], I32)
                ang = S.sbuf("ang", [64, 512], F32)
                tcs = S.sbuf("tcs", [64, 512], F32)
                tsn = S.sbuf("tsn", [64, 512], F32)
                for qt in range(4):
                    tb = 2048 + qt * 512
                    cs = slice(qt * 512, qt * 512 + 512)
                    rope_gen(posr[:, tb:tb + 512], posi, posi.t[:], ang, ang.t[:], tcs, tcs.t[:], tsn, tsn.t[:],
                             cosq, cosq.t[:, cs], sinq, sinq.t[:, cs])
                KT = S.sbuf("KT", [128, 4096], BF16)
                Vh = S.sbuf("Vh", [128, 32, 128], BF16)
                QN = S.sbuf("QN", [128, 2048], BF16)
                QR = S.sbuf("QR", [64, 2048], BF16)
                sq = S.sbuf("sq", [128, 512], BF16)
                PTs = [S.sbuf("PT%d" % i, [128, 512], BF16) for i in range(3)]
                rden = S.sbuf("rden", [128, 512], F32)
                OTs = [S.sbuf("OT%d" % i, [128, 512], BF16) for i in range(2)]
                red = S.sbuf("red", [128, 1], F32)
                krmax = S.sbuf("krmax", [128, 1], F32)
                knmax = S.sbuf("knmax", [128, 1], F32)
                qnmax = S.sbuf("qnmax", [128, 1], F32)
                qrmax = S.sbuf("qrmax", [128, 1], F32)
                negB = S.sbuf("negB", [128, 1], F32)
                negBp = S.sbuf("negBp", [128, 1], F32)
                stp = [S.psum("st%d" % i, [128, 512], F32) for i in range(2)]
                otp = S.psum("otp", [128, 512], F32)
                dnp = S.psum("dnp", [128, 512], F32)
                fm = [S.psum("fmb%d" % i, [128, 512], F32) for i in range(2)]
                vps = S.psum("vps", [128, 512], F32)
                nb = S.psum("nb", [128, 512], F32)

                sq2 = S.sbuf("sq2", [128, 512], BF16)
                sqs = [sq, sq2]
                pend = []
                sqn = [0]

                def sqmax_flush(keep=0):
                    while len(pend) > keep:
                        sqb, nrows, acc = pend.pop(0)
                        S.op("pe", lambda e, sqb=sqb, nrows=nrows: e.matmul(nb.t[:], lhsT=ones.t[0:nrows, :], rhs=sqb.t[0:nrows, :], start=True, stop=True),
                             reads=[ones, sqb], writes=[nb])
                        S.op("dve", lambda e: e.tensor_reduce(out=red.t[:], in_=nb.t[:], axis=AX.X, op=ALU.max), reads=[nb], writes=[red])
                        S.op("dve", lambda e, acc=acc: e.tensor_tensor(out=acc.t[:], in0=acc.t[:], in1=red.t[:], op=ALU.max), reads=[acc, red], writes=[acc])

                def sqmax(src_buf, src_ap, nrows, acc):
                    sqmax_flush(0)
                    sqb = sqs[sqn[0] % 2]
                    sqn[0] += 1
                    S.op("dve", lambda e: e.tensor_tensor(out=sqb.t[0:nrows, :], in0=src_ap, in1=src_ap, op=ALU.mult),
                         reads=[src_buf], writes=[sqb])
                    pend.append((sqb, nrows, acc))

                S.op("dve", lambda e: e.memset(krmax.t[:], 0.0), writes=[krmax])
                for kt in range(8):
                    sqmax(krT, krT.t[:, kt * 512:(kt + 1) * 512], 64, krmax)
                sqmax_flush(0)

                for h in range(16):
                    sl = ring.load(w_attn[h][:, 0:2560], 2560) if False else ring.load(w_attn[h])
                    S.op("dve", lambda e: e.memset(knmax.t[:], 0.0), writes=[knmax])
                    S.op("dve", lambda e: e.memset(qnmax.t[:], 0.0), writes=[qnmax])
                    S.op("dve", lambda e: e.memset(qrmax.t[:], 0.0), writes=[qrmax])
                    for kt in range(8):
                        f = fm[kt % 2]
                        ks = slice(kt * 512, kt * 512 + 512)
                        for kc in range(4):
                            S.op("pe", lambda e, kc=kc, ks=ks, f=f, sl=sl: e.matmul(f.t[:], lhsT=sl.t[:, 1536 + kc * 128: 1536 + (kc + 1) * 128],
                                                                                   rhs=ckvnT.t[:, kc, ks], start=(kc == 0), stop=(kc == 3)),
                                 reads=[sl, ckvnT], writes=[f])
                        copy_any(KT.t[:, ks], f.t[:], [f], [KT])
                        sqmax(KT, KT.t[:, ks], 128, knmax)
                    for sg_ in range(8):
                        for j in range(4):
                            st_ = sg_ * 4 + j
                            for kc in range(4):
                                S.op("pe", lambda e, kc=kc, st_=st_, j=j, sl=sl: e.matmul(vps.t[:, j * 128:(j + 1) * 128],
                                                                                         lhsT=ckvnT.t[:, kc, st_ * 128:(st_ + 1) * 128],
                                                                                         rhs=sl.t[:, 2048 + kc * 128: 2048 + (kc + 1) * 128],
                                                                                         start=(kc == 0), stop=(kc == 3)),
                                     reads=[sl, ckvnT], writes=[vps])
                        copy_any(Vh.t[:, sg_ * 4:(sg_ + 1) * 4, :], vps.t[:].rearrange("p (a b) -> p a b", b=128), [vps], [Vh])
                    for qt in range(4):
                        cs = slice(qt * 512, qt * 512 + 512)
                        f = fm[0]
                        for kc in range(6):
                            S.op("pe", lambda e, kc=kc, cs=cs, sl=sl: e.matmul(fm[0].t[:], lhsT=sl.t[:, kc * 128:(kc + 1) * 128], rhs=cqnT.t[:, kc, cs],
                                                                              start=(kc == 0), stop=(kc == 5)), reads=[sl, cqnT], writes=[fm[0]])
                        copy_any(QN.t[:, cs], fm[0].t[:], [fm[0]], [QN])
                        sqmax(QN, QN.t[:, cs], 128, qnmax)
                        for kc in range(6):
                            S.op("pe", lambda e, kc=kc, cs=cs, sl=sl: e.matmul(fm[1].t[0:64, :], lhsT=sl.t[:, 768 + kc * 128: 768 + kc * 128 + 64],
                                                                              rhs=cqnT.t[:, kc, cs], start=(kc == 0), stop=(kc == 5)),
                                 reads=[sl, cqnT], writes=[fm[1]])
                        S.op("dve", lambda e, cs=cs: e.tensor_tensor(out=tcs.t[:], in0=fm[1].t[0:64, :], in1=cosq.t[:, cs], op=ALU.mult),
                             reads=[fm[1], cosq], writes=[tcs])
                        for kc in range(6):
                            S.op("pe", lambda e, kc=kc, cs=cs, sl=sl: e.matmul(fm[1].t[0:64, :], lhsT=sl.t[:, 768 + kc * 128 + 64: 768 + kc * 128 + 128],
                                                                              rhs=cqnT.t[:, kc, cs], start=(kc == 0), stop=(kc == 5)),
                                 reads=[sl, cqnT], writes=[fm[1]])
                        S.op("dve", lambda e, cs=cs: e.tensor_tensor(out=tsn.t[:], in0=fm[1].t[0:64, :], in1=sinq.t[:, cs], op=ALU.mult),
                             reads=[fm[1], sinq], writes=[tsn])
                        S.op("dve", lambda e, cs=cs: e.tensor_tensor(out=QR.t[:, cs], in0=tcs.t[:], in1=tsn.t[:], op=ALU.add),
                             reads=[tcs, tsn], writes=[QR])
                        sqmax(QR, QR.t[:, cs], 64, qrmax)
                    sqmax_flush(0)
                    S.op("dve", lambda e: e.tensor_tensor(out=qnmax.t[:], in0=qnmax.t[:], in1=qrmax.t[:], op=ALU.add), reads=[qnmax, qrmax], writes=[qnmax])
                    S.op("dve", lambda e: e.tensor_tensor(out=knmax.t[:], in0=knmax.t[:], in1=krmax.t[:], op=ALU.add), reads=[knmax, krmax], writes=[knmax])
                    S.op("dve", lambda e: e.tensor_tensor(out=negB.t[:], in0=qnmax.t[:], in1=knmax.t[:], op=ALU.mult), reads=[qnmax, knmax], writes=[negB])
                    S.op("act", lambda e: e.activation(out=negB.t[:], in_=negB.t[:], func=AF.Sqrt), reads=[negB], writes=[negB])
                    S.op("dve", lambda e: e.tensor_scalar(out=negB.t[:], in0=negB.t[:], scalar1=-QSCALE, scalar2=None, op0=ALU.mult),
                         reads=[negB], writes=[negB])
                    S.op("dve", lambda e: e.tensor_tensor(out=negBp.t[:], in0=negB.t[:], in1=pbias.t[:], op=ALU.add), reads=[negB, pbias], writes=[negBp])
                    for qt in range(4):
                        cs = slice(qt * 512, qt * 512 + 512)
                        nkb = 16 + 4 * qt + 4
                        OT = OTs[qt % 2]

                        def qk(kb, cs=cs, qt=qt):
                            st_ = stp[kb % 2]
                            ks = slice(kb * 128, kb * 128 + 128)
                            S.op("pe", lambda e: e.matmul(st_.t[:], lhsT=KT.t[:, ks], rhs=QN.t[:, cs], start=True, stop=False),
                                 reads=[KT, QN], writes=[st_])
                            S.op("pe", lambda e: e.matmul(st_.t[:], lhsT=krT.t[:, ks], rhs=QR.t[:, cs], start=False, stop=True),
                                 reads=[krT, QR], writes=[st_])
                            PT = PTs[kb % 3]
                            bias = negBp if kb < 16 else negB
                            S.op("act", lambda e: e.activation(out=PT.t[:], in_=st_.t[:], func=AF.Exp, bias=bias.t[:], scale=QSCALE),
                                 reads=[st_, bias], writes=[PT])
                            j = kb - (16 + 4 * qt)
                            if j >= 0:
                                S.op("dve", lambda e: e.tensor_tensor(out=PT.t[:], in0=PT.t[:], in1=cmask.t[:, j * 512:(j + 1) * 512], op=ALU.mult),
                                     reads=[PT, cmask], writes=[PT])

                        def pv(kb, nkb=nkb):
                            PT = PTs[kb % 3]
                            S.op("pe", lambda e: e.matmul(otp.t[:], lhsT=Vh.t[:, kb, :], rhs=PT.t[:], start=(kb == 0), stop=(kb == nkb - 1)),
                                 reads=[Vh, PT], writes=[otp])
                            S.op("pe", lambda e: e.matmul(dnp.t[:], lhsT=ones.t[:], rhs=PT.t[:], start=(kb == 0), stop=(kb == nkb - 1)),
                                 reads=[ones, PT], writes=[dnp])

                        qk(0)
                        for kb in range(nkb):
                            if kb + 1 < nkb:
                                qk(kb + 1)
                            pv(kb)
                        S.op("dve", lambda e: e.reciprocal(out=rden.t[:], in_=dnp.t[:]), reads=[dnp], writes=[rden])
                        S.op("dve", lambda e, OT=OT: e.tensor_tensor(out=OT.t[:], in0=otp.t[:], in1=rden.t[:], op=ALU.mult),
                             reads=[otp, rden], writes=[OT])
                        S.dma("sp", mix_d[h * 128:(h + 1) * 128, qt * 512:(qt + 1) * 512], OT.t[:], reads=[OT], writes=[MIX],
                              sem_buf=OT, kind="r")
                print("phase B", S.flush())

        mid.close()
        finals = []
        if "C" in phases:
            with ExitStack() as pc:
                S.stack = pc
                g_mlp = const("g_mlp", [128, 32], F32, g_mlp_d, False)
                g_ple = const("g_ple", [128, 32], F32, g_ple_d, False)
                gbc = S.sbuf("gbc", [128, 4096], F32)
                g_post = g_fin = gbc
                ring = Ring(S, KN_RINGC, "wrc")
                hacc = [S.sbuf("hacc%d" % s, [128, 4096], F32) for s in range(4)]
                actT = S.sbuf("actT", [128, 32, 512], BF16)
                aTp = [Buf("aTp%d" % i) for i in range(4)]
                hid = S.sbuf("hid", [128, 8, 512], BF16)
                yb = S.sbuf("ybC", [128, 4096], F32)
                wpe = S.sbuf("wpe", [128, 8192], BF16)
                for t in range(2):
                    S.dma("pool", wpe.t[:, t * 4096:(t + 1) * 4096].rearrange("p (a b) -> p a b", b=2048),
                          w_pe_t[t].rearrange("p (a b) -> p a b", b=2048), writes=[wpe], sem_buf=wpe)
                tmpr = [S.sbuf("tmpr%d" % i, [128, 512], F32) for i in range(2)]
                tmpg = S.sbuf("tmpg", [128, 512], F32)
                tmpe = S.sbuf("tmpe", [128, 512], F32)
                pf = S.sbuf("pf", [128, 256], F32)
                pbf = S.sbuf("pbf", [128, 256], BF16)
                pT = S.sbuf("pT", [128, 2, 512], BF16)
                ss = S.sbuf("ssC", [128, 1], F32)
                rstd = S.sbuf("rstdC", [128, 1], F32)
                sse = [S.sbuf("sse%d" % s, [128, 8], F32) for s in range(4)]
                rse = [S.sbuf("rse%d" % s, [128, 1], F32) for s in range(4)]
                tm = [S.psum("tmc%d" % i, [128, 512], F32) for i in range(4)]
                fm = [S.psum("fmc%d" % i, [128, 512], F32) for i in range(2)]
                tpt = S.psum("tpc", [128, 512], F32)
                tptb = S.psum("tpcb", [128, 512], F32)

                class _V2:
                    def __init__(self, t, off):
                        self.t_, self.off = t, off
                    def __getitem__(self, k):
                        rows, cols = k
                        return self.t_[rows, self.off + cols.start:self.off + cols.stop]
                tps = [Buf("tpa"), Buf("tpb")]
                tps[0].excl = tps[1].excl = True
                tps[0].t = _V2(tpt.t, 0)
                tps[1].t = _V2(tptb.t, 0)
                ones_c = const("ones_c", [128, 128], BF16, ones_d, False)

                def gemm_tm_c(tiles, evac):
                    for q in range(4):
                        sl = ring.load(tiles[q])
                        for s in range(4):
                            for kk in range(8):
                                kc = q * 8 + kk
                                S.op("pe", lambda e, s=s, kc=kc, kk=kk, sl=sl: e.matmul(
                                    tm[s].t[:], lhsT=actT.t[:, kc, s * 128:(s + 1) * 128], rhs=sl.t[:, kk * 512:(kk + 1) * 512],
                                    start=(kc == 0), stop=(kc == 31)), reads=[aTp[q], sl], writes=[tm[s]])
                    for s in range(4):
                        evac(s)

                def add_into(s, g, src_buf, src_ap):
                    S.op("dve", lambda e: e.tensor_tensor(out=hacc[s].t[:, g * 512:(g + 1) * 512], in0=hacc[s].t[:, g * 512:(g + 1) * 512],
                                                          in1=src_ap, op=ALU.add), reads=[hacc[s], src_buf], writes=[hacc[s]])

                for it in range(NTILE_OWN):
                    tok0 = it * 512
                    for s in range(4):
                        r0 = 2048 + tok0 + s * 128
                        S.dma("sp", hacc[s].t[:], xs[r0:r0 + 128, :], writes=[hacc[s]], sem_buf=hacc[s])
                    mixv = mix_d.rearrange("(c p) t -> p c t", p=128)
                    for j in range(4):
                        S.dma("sp", actT.t[:, 8 * j:8 * j + 8, :], mixv[:, 8 * j:8 * j + 8, tok0:tok0 + 512], reads=[MIX], writes=[aTp[j]],
                              sem_buf=aTp[j])
                    for g in range(8):
                        gemm_tm_c([w_o_tm[g * 4 + q] for q in range(4)], lambda s, g=g: add_into(s, g, tm[s], tm[s].t[:]))
                    for s in range(4):
                        norm_rows(hacc[s], hacc[s].t[:], 4096, yb, ss, rstd)
                        transpose_out(yb, 32, tps, lambda c, s=s: actT.t[:, c, s * 128:(s + 1) * 128], g_mlp, lambda c: [aTp[c // 8]])
                    for fb in range(16):
                        for fc in range(8):
                            sl = ring.load(w_up_fm[fb * 8 + fc])
                            f = fm[fc % 2]
                            for kc in range(32):
                                S.op("pe", lambda e, kc=kc, f=f, sl=sl: e.matmul(f.t[:], lhsT=sl.t[:, kc * 128:(kc + 1) * 128], rhs=actT.t[:, kc, :],
                                                                                start=(kc == 0), stop=(kc == 31)), reads=[aTp[kc // 8], sl], writes=[f])
                            tr = tmpr[fc % 2]
                            S.op("dve", lambda e, f=f, tr=tr: e.tensor_scalar(out=tr.t[:], in0=f.t[:], scalar1=0.0, scalar2=None, op0=ALU.max),
                                 reads=[f], writes=[tr])
                            S.op("act", lambda e, fc=fc, tr=tr: e.activation(out=hid.t[:, fc, :], in_=tr.t[:], func=AF.Square), reads=[tr], writes=[hid])
                        for g in range(8):
                            sl = ring.load(w_dn_tm[fb * 8 + g])
                            for s in range(4):
                                for kk in range(8):
                                    S.op("pe", lambda e, s=s, kk=kk, sl=sl: e.matmul(tm[s].t[:], lhsT=hid.t[:, kk, s * 128:(s + 1) * 128],
                                                                                    rhs=sl.t[:, kk * 512:(kk + 1) * 512], start=(kk == 0), stop=(kk == 7)),
                                         reads=[hid, sl], writes=[tm[s]])
                                add_into(s, g, tm[s], tm[s].t[:])
                    S.dma("sp", gbc.t[:], g_post_d, writes=[gbc], sem_buf=gbc)
                    for s in range(4):
                        norm_rows(hacc[s], hacc[s].t[:], 4096, yb, ss, rstd)
                        transpose_out(yb, 32, tps, lambda c, s=s: actT.t[:, c, s * 128:(s + 1) * 128], g_ple, lambda c: [aTp[c // 8]])
                        r0 = tok0 + s * 128
                        S.dma("sp", pf.t[:], pp[r0:r0 + 128, :], writes=[pf], sem_buf=pf)
                        for kk in range(2):
                            S.op("pe", lambda e, kk=kk: e.transpose(tps[0].t[:, kk * 128:(kk + 1) * 128], pf.t[:, kk * 128:(kk + 1) * 128], ident.t[:]),
                                 reads=[pf, ident], writes=[tps[0]])
                        S.op("act", lambda e, s=s: e.copy(out=pT.t[:, :, s * 128:(s + 1) * 128],
                                                          in_=tps[0].t[:, 0:256].rearrange("p (a b) -> p a b", b=128)), reads=[tps[0]], writes=[pT])

                    def emm(s, g, f):
                        for kk in range(2):
                            S.op("pe", lambda e, kk=kk: e.matmul(f.t[:], lhsT=pT.t[:, kk, s * 128:(s + 1) * 128],
                                                                 rhs=wpe.t[:, g * 1024 + kk * 512: g * 1024 + (kk + 1) * 512], start=(kk == 0), stop=(kk == 1)),
                                 reads=[pT, wpe], writes=[f])
                    for s in range(4):
                        S.op("dve", lambda e, s=s: e.memset(sse[s].t[:], 0.0), writes=[sse[s]])
                        for g in range(8):
                            f = fm[g % 2]
                            emm(s, g, f)
                            S.op("act", lambda e, s=s, g=g, f=f: e.activation(out=tmpg.t[:], in_=f.t[:], func=AF.Square, accum_out=sse[s].t[:, g:g + 1]),
                                 reads=[f, sse[s]], writes=[tmpg, sse[s]])
                        S.op("dve", lambda e, s=s: e.tensor_reduce(out=rse[s].t[:], in_=sse[s].t[:], axis=AX.X, op=ALU.add), reads=[sse[s]], writes=[rse[s]])
                        rstd_from_ss(rse[s], rse[s], 4096)

                    for g in range(8):
                        def evg(s, g=g):
                            f = fm[s % 2]
                            emm(s, g, f)
                            S.op("act", lambda e: e.activation(out=tmpg.t[:], in_=tm[s].t[:], func=AF.Sigmoid), reads=[tm[s]], writes=[tmpg])
                            S.op("dve", lambda e: e.scalar_tensor_tensor(out=tmpe.t[:], in0=f.t[:], scalar=rse[s].t[:], in1=g_post.t[:, g * 512:(g + 1) * 512],
                                                                         op0=ALU.mult, op1=ALU.mult), reads=[f, rse[s], g_post], writes=[tmpe])
                            S.op("dve", lambda e: e.tensor_tensor(out=tmpe.t[:], in0=tmpe.t[:], in1=tmpg.t[:], op=ALU.mult), reads=[tmpe, tmpg], writes=[tmpe])
                            add_into(s, g, tmpe, tmpe.t[:])
                        gemm_tm_c([w_pg_tm[g * 4 + q] for q in range(4)], evg)
                    S.dma("sp", gbc.t[:], g_fin_d, writes=[gbc], sem_buf=gbc)
                    for s in range(4):
                        S.op("dve", lambda e: e.memset(ss.t[:], 0.0), writes=[ss])
                        S.op("act", lambda e, s=s: e.activation(out=yb.t[:], in_=hacc[s].t[:], func=AF.Square, accum_out=ss.t[:]),
                             reads=[hacc[s], ss], writes=[yb, ss])
                        rstd_from_ss(rstd, ss, 4096)
                        S.op("dve", lambda e, s=s: e.scalar_tensor_tensor(out=hacc[s].t[:], in0=hacc[s].t[:], scalar=rstd.t[:], in1=g_fin.t[:],
                                                                          op0=ALU.mult, op1=ALU.mult), reads=[hacc[s], rstd, g_fin], writes=[hacc[s]])
                        r0 = tok0 + s * 128
                        finals.append(S.dma("sp", out_d[r0:r0 + 128, :], hacc[s].t[:], reads=[hacc[s]], sem_buf=hacc[s], kind="r"))
                print("phase C", S.flush(final_waits=finals))
    return nc


def _fm_tiles(w):
    K, N = w.shape
    kc = K // 128
    t = w.reshape(kc, 128, N // 128, 128).transpose(2, 1, 0, 3)
    return np.ascontiguousarray(t).reshape(N // 128, 128, kc * 128)


def _tm_tiles(w, width=512):
    K, N = w.shape
    nq = K // 1024
    t = w.reshape(nq, 8, 128, N // width, width).transpose(3, 0, 2, 1, 4)
    return np.ascontiguousarray(t).reshape((N // width) * nq, 128, 8 * width)


_CACHE = {}


def _prep_shared(inp):
    f32 = np.float32
    w_in = np.asarray(inp["w_in"], f32)[0]
    QL, KVL = 768, 512
    o_kr = QL + KVL
    o_hq = o_kr + 64
    o_hf = o_hq + 2048
    o_hi = o_hf + 2048
    o_hg = o_hi + 2048
    kr = w_in[:, o_kr:o_kr + 64]
    krs = np.concatenate([kr[:, 32:64], kr[:, 0:32]], axis=1)
    fm_cols = [np.concatenate([kr, krs], axis=1)]
    for h in range(16):
        fm_cols.append(w_in[:, o_hq + h * 128: o_hq + (h + 1) * 128])
        fm_cols.append(w_in[:, o_hf + h * 128: o_hf + (h + 1) * 128])
    w_in_fm = _fm_tiles(np.concatenate(fm_cols, axis=1))
    def pad_tiles(t):
        o = np.zeros((t.shape[0], 128, 4096), f32)
        o[:, :, :t.shape[2]] = t
        return o
    tm_list = [_tm_tiles(w_in[:, 0:512]), pad_tiles(_tm_tiles(w_in[:, 512:768], 256)),
               pad_tiles(_tm_tiles(w_in[:, 768:1024], 256)), pad_tiles(_tm_tiles(w_in[:, 1024:1280], 256))]
    w_in_tm = np.concatenate(tm_list, axis=0)
    w_in_tmh = np.concatenate([_tm_tiles(np.concatenate([w_in[:, o_hi + h * 128: o_hi + (h + 1) * 128],
                                                         w_in[:, o_hg + h * 128: o_hg + (h + 1) * 128]], axis=1), 256)
                               for h in range(16)], axis=0)
    w_in_tmp = np.concatenate([_tm_tiles(w_in[:, o_hi + h * 128: o_hi + (h + 1) * 128], 128) for h in range(16)], axis=0)
    w_uq = np.asarray(inp["w_uq"], f32)[0]
    w_ukv = np.asarray(inp["w_ukv"], f32)[0]
    w_attn = np.zeros((16, 128, 4096), f32)
    for h in range(16):
        qn = w_uq[:, h * 192: h * 192 + 128]
        qr = w_uq[:, h * 192 + 128: h * 192 + 192]
        qrs = np.concatenate([qr[:, 32:64], qr[:, 0:32]], axis=1)
        w_attn[h, :, 0:768] = _fm_tiles(qn)[0]
        w_attn[h, :, 768:1536] = _fm_tiles(np.concatenate([qr, qrs], axis=1))[0]
        uk = w_ukv[:, h * 256: h * 256 + 128]
        uv = w_ukv[:, h * 256 + 128: h * 256 + 256]
        w_attn[h, :, 1536:2048] = _fm_tiles(uk)[0]
        w_attn[h, :, 2048:2560] = _fm_tiles(uv)[0]
    w_pe = np.asarray(inp["w_ple"], f32)[0]
    t = w_pe.reshape(2, 128, 8, 512).transpose(1, 2, 0, 3)
    w_pe_t = np.ascontiguousarray(t).reshape(128, 2, 4096).transpose(1, 0, 2)
    def col(v, n):
        return np.ascontiguousarray(np.asarray(v, f32).reshape(n, 128).T)
    lbr = np.asarray(inp["hg_lower_bound"], f32)
    k = np.arange(128)
    maskbd = ((k[:, None] // 64 == k[None, :] // 64) & (k[:, None] <= k[None, :])).astype(ml_dtypes.bfloat16)
    t512 = np.arange(512)
    cme = np.broadcast_to(((t512 // 64) % 2 == 0)[None, :], (128, 512)).astype(ml_dtypes.bfloat16)
    cmo = np.broadcast_to(((t512 // 64) % 2 == 1)[None, :], (128, 512)).astype(ml_dtypes.bfloat16)
    rme = np.stack([(k < 64), (k >= 64)], axis=1).astype(f32)
    rmask = np.broadcast_to((t512 % 64 != 0)[None, :], (128, 512)).astype(f32)
    cm = np.zeros((128, 4, 512), ml_dtypes.bfloat16)
    for j in range(4):
        cm[:, j, :] = ((j * 128 + k)[:, None] <= t512[None, :])
    inv = (10000.0 ** (-np.arange(0, 64, 2, dtype=f32) / 64)).astype(f32)
    invf = np.stack([np.concatenate([inv, inv]), np.concatenate([-np.ones(32, f32), np.ones(32, f32)])], axis=1).astype(f32)
    sh = dict(
        w_in_fm=w_in_fm, w_in_tm=w_in_tm, w_in_tmh=w_in_tmh, w_in_tmp=w_in_tmp, w_attn=w_attn,
        w_o_tm=_tm_tiles(np.asarray(inp["w_o"], f32)[0]),
        w_up_fm=_fm_tiles(np.asarray(inp["w_up"], f32)[0]),
        w_dn_tm=None, w_pg_tm=_tm_tiles(np.asarray(inp["w_ple_gate"], f32)[0]),
        w_pe_t=np.ascontiguousarray(w_pe_t),
        g_mix=col(inp["norm_mix"][0], 32), g_mlp=col(inp["norm_mlp"][0], 32), g_ple=col(inp["norm_ple"][0], 32),
        g_qa=col(inp["q_a_norm"][0], 6), g_kva=col(inp["kv_a_norm"][0], 4), g_hg=col(inp["hg_out_norm"][0], 16),
        lbraw=np.ascontiguousarray(np.concatenate([lbr[0].reshape(16, 128).T, lbr[1].reshape(16, 128).T], axis=1)),
        g_post=np.ascontiguousarray(np.broadcast_to(np.asarray(inp["ple_post_norm"], f32)[0][None, :], (128, 4096))),
        g_fin=np.ascontiguousarray(np.broadcast_to(np.asarray(inp["final_norm"], f32)[None, :], (128, 4096))),
        ident=np.eye(128).astype(np.float32), ones=np.ones((128, 128), ml_dtypes.bfloat16),
        maskbd=maskbd, cme=np.ascontiguousarray(cme), cmo=np.ascontiguousarray(cmo), rme=rme, rmask=np.ascontiguousarray(rmask),
        cmask=np.ascontiguousarray(cm.reshape(128, 2048)), invf=invf,
    )
    wd = np.asarray(inp["w_down"], f32)[0]
    t = wd.reshape(16, 8, 128, 8, 512).transpose(0, 3, 2, 1, 4)
    sh["w_dn_tm"] = np.ascontiguousarray(t).reshape(128, 128, 4096)
    return sh


def _per_core(inp, c):
    b, half = c // 2, c % 2
    x = np.asarray(inp["x"], np.float32)
    pos = np.asarray(inp["positions"], np.int32)
    own = slice(half * 2048, half * 2048 + 2048)
    xs = np.zeros((4096, 4096), np.float32)
    ps = np.zeros((4096,), np.int32)
    if half == 1:
        xs[0:2048] = x[b, 0:2048]
        ps[0:2048] = pos[b, 0:2048]
    xs[2048:] = x[b, own]
    ps[2048:] = pos[b, own]
    return dict(
        xs=xs, posr=np.ascontiguousarray(np.broadcast_to(ps[None, :], (64, 4096))),
        pp=np.ascontiguousarray(np.asarray(inp["p"], np.float32)[0, b, own]),
        pbias=np.full((128, 1), 0.0 if half == 1 else -30000.0, np.float32),
    )


def kernel(**inputs):
    sh = _prep_shared(inputs)
    nc = build_program()
    in_maps = []
    for c in range(8):
        m = dict(sh)
        m.update(_per_core(inputs, c))
        in_maps.append(m)
    res = run_bass_kernel_spmd(nc, in_maps, core_ids=list(range(8)))
    out = np.zeros((4, 4096, 4096), np.float32)
    for c in range(8):
        b, half = c // 2, c % 2
        out[b, half * 2048:(half + 1) * 2048] = res.results[c]["out"]
    return out
```

```python
import math
from contextlib import ExitStack
import numpy as np
import ml_dtypes
import concourse.bass as bass
import concourse.mybir as mybir
from concourse.bass_utils import run_bass_kernel_spmd

F32 = mybir.dt.float32
BF16 = mybir.dt.bfloat16
I32 = mybir.dt.int32
AF = mybir.ActivationFunctionType
ALU = mybir.AluOpType
AX = mybir.AxisListType
ENGS = ("pe", "act", "dve", "pool", "sp")
EPS = 1e-6
PI = math.pi


class Buf:
    __slots__ = ("name", "t", "w", "rs", "sem_w", "cnt_w", "sem_r", "cnt_r", "excl")

    def __init__(self, name, t=None):
        self.excl = False
        self.name = name
        self.t = t
        self.w = None
        self.rs = []
        self.sem_w = None
        self.cnt_w = 0
        self.sem_r = None
        self.cnt_r = 0


class Op:
    __slots__ = ("eng", "fn", "deps", "sig", "sigval", "is_dma", "dsem", "dval")

    def __init__(self, eng, fn):
        self.eng = eng
        self.fn = fn
        self.deps = []
        self.sig = False
        self.sigval = 0
        self.is_dma = False
        self.dsem = None
        self.dval = 0


class Sched:
    def __init__(self, nc, stack):
        self.nc = nc
        self.stack = stack
        self.top = stack
        self.streams = {e: [] for e in ENGS}
        self.esem = {e: stack.enter_context(nc.semaphore("es_" + e)) for e in ENGS}
        self.nsem = 0
        self.base = {}
        self.bar = {}
        self.nbuf = 0

    def new_sem(self, name):
        self.nsem += 1
        return self.top.enter_context(self.nc.semaphore("ds%d" % self.nsem))

    def sbuf(self, name, shape, dt, top=False):
        self.nbuf += 1
        st = self.top if top else self.stack
        t = st.enter_context(self.nc.sbuf_tensor("%s_%d" % (name, self.nbuf), list(shape), dt))
        return Buf(name, t)

    def psum(self, name, shape, dt):
        self.nbuf += 1
        t = self.stack.enter_context(self.nc.psum_tensor("%s_%d" % (name, self.nbuf), list(shape), dt))
        b = Buf(name, t)
        b.excl = True
        return b

    def _deps(self, op, reads, writes):
        deps = []
        for r in reads:
            if r.w is not None:
                deps.append(r.w)
            if r.excl:
                deps.extend(p for p in r.rs if p.eng != op.eng)
        for w in writes:
            if w.w is not None:
                deps.append(w.w)
            deps.extend(w.rs)
        op.deps = [d for d in deps if d is not op]
        for r in reads:
            r.rs.append(op)
        for w in writes:
            w.w = op
            w.rs = []

    def op(self, eng, fn, reads=(), writes=()):
        o = Op(eng, fn)
        self._deps(o, reads, writes)
        self.streams[eng].append(o)
        return o

    def dma(self, q, out_ap, in_ap, reads=(), writes=(), sem_buf=None, kind="w"):
        def fn(e):
            return e.dma_start(out=out_ap, in_=in_ap)
        o = Op(q, fn)
        o.is_dma = True
        if kind == "w":
            if sem_buf.sem_w is None:
                sem_buf.sem_w = self.new_sem(sem_buf.name)
            sem_buf.cnt_w += 16
            o.dsem, o.dval = sem_buf.sem_w, sem_buf.cnt_w
        else:
            if sem_buf.sem_r is None:
                sem_buf.sem_r = self.new_sem(sem_buf.name)
            sem_buf.cnt_r += 16
            o.dsem, o.dval = sem_buf.sem_r, sem_buf.cnt_r
        self._deps(o, reads, writes)
        self.streams[q].append(o)
        return o

    def flush(self, final_waits=()):
        r = self.finalize(final_waits)
        pend = []
        for e in ENGS:
            lastc = None
            for o in self.streams[e]:
                if o.is_dma:
                    pend.append(o)
                else:
                    lastc = o
            if lastc is not None:
                pend.append(lastc)
        self.base = {e: sum(1 for o in self.streams[e] if o.sig and not o.is_dma) + self.base.get(e, 0)
                     for e in ENGS}
        self.streams = {e: [] for e in ENGS}
        old = self.bar
        self.bar = {e: list(pend) + list(old.get(e, [])) for e in ENGS}
        return r

    def finalize(self, final_waits=()):
        for e in ENGS:
            if self.bar.get(e) and self.streams[e]:
                o0 = self.streams[e][0]
                o0.deps = list(o0.deps) + [d for d in self.bar[e] if d is not o0]
                self.bar[e] = []
        for e in ENGS:
            for o in self.streams[e]:
                for d in o.deps:
                    if not d.is_dma:
                        if d.eng == "pe" and o.eng == "pe" and not o.is_dma:
                            continue
                        d.sig = True
        for o in final_waits:
            if not o.is_dma:
                o.sig = True
        for e in ENGS:
            for o in reversed(self.streams[e]):
                if not o.is_dma:
                    o.sig = True
                    break
        for e in ENGS:
            c = self.base.get(e, 0)
            for o in self.streams[e]:
                if o.sig and not o.is_dma:
                    c += 1
                    o.sigval = c
        streams = self.streams
        esem = self.esem
        stats = {e: [0, 0] for e in ENGS}

        proto = {e: [] for e in ENGS}
        self.proto = proto

        def emit(e, eng):
            waited = {}
            for o in streams[e]:
                mywaits = []
                need = {}
                for d in o.deps:
                    if d.is_dma:
                        key, val = d.dsem, d.dval
                    else:
                        if d.eng == "pe" and e == "pe" and not o.is_dma:
                            continue
                        key, val = esem[d.eng], d.sigval
                    if val > need.get(key, 0):
                        need[key] = val
                for key, val in need.items():
                    if val > waited.get(key, 0):
                        eng.wait_ge(key, val)
                        waited[key] = val
                        stats[e][1] += 1
                        mywaits.append((id(key), val))
                proto[e].append((mywaits, (id(o.dsem), 16) if o.is_dma else ((id(esem[e]), 1) if o.sig else None)))
                ins = o.fn(eng)
                stats[e][0] += 1
                if o.is_dma:
                    ins.then_inc(o.dsem, 16)
                elif o.sig:
                    ins.then_inc(esem[e], 1)
            if e == "sp":
                for o in final_waits:
                    if o.is_dma:
                        eng.wait_ge(o.dsem, o.dval)
                    else:
                        eng.wait_ge(esem[o.eng], o.sigval)

        with self.nc.Block() as block:
            @block.tensor
            def _(eng):
                emit("pe", eng)

            @block.scalar
            def _(eng):
                emit("act", eng)

            @block.vector
            def _(eng):
                emit("dve", eng)

            @block.gpsimd
            def _(eng):
                emit("pool", eng)

            @block.sync
            def _(eng):
                emit("sp", eng)
        if not hasattr(self, "semval"):
            self.semval = {}
        semval = self.semval
        ptr = {e: 0 for e in ENGS}
        progress = True
        while progress:
            progress = False
            for e in ENGS:
                while ptr[e] < len(proto[e]):
                    waits, inc = proto[e][ptr[e]]
                    if all(semval.get(k, 0) >= v for k, v in waits):
                        if inc is not None:
                            semval[inc[0]] = semval.get(inc[0], 0) + inc[1]
                        ptr[e] += 1
                        progress = True
                    else:
                        break
        stuck = {e: (ptr[e], len(proto[e])) for e in ENGS if ptr[e] < len(proto[e])}
        if stuck:
            print("DEADLOCK in emitted protocol:", stuck)
            for e in stuck:
                waits, inc = proto[e][ptr[e]]
                print("  ", e, "waiting", [(k, v, semval.get(k, 0)) for k, v in waits])
        return stats


class Ring:
    def __init__(self, S, n, name):
        self.S = S
        self.slots = [S.sbuf("%s%d" % (name, i), [128, 4096], BF16) for i in range(n)]
        self.i = 0

    def load(self, src, width=4096):
        sl = self.slots[self.i % len(self.slots)]
        self.i += 1
        b = min(width, 2048)
        self.S.dma("pool", sl.t[:, 0:width].rearrange("p (a b) -> p a b", b=b),
                   src.rearrange("p (a b) -> p a b", b=b), writes=[sl], sem_buf=sl)
        return sl


LAST_DRAM = []
DBG_COPY = False
SMALL = ()
import os
KN_SUB = int(os.environ.get('KN_SUB', '4'))
KN_EVAC = os.environ.get('KN_EVAC', 'mix')
KN_TPSEP = int(os.environ.get('KN_TPSEP', '0'))
KN_NCH = int(os.environ.get('KN_NCH', '32'))
KN_H = int(os.environ.get('KN_H', '9'))
KN_LEAD = int(os.environ.get('KN_LEAD', '6'))
KN_RING = int(os.environ.get('KN_RING', '3'))
KN_ESTEP = int(os.environ.get('KN_ESTEP', '1'))
KN_RINGC = int(os.environ.get('KN_RINGC', '4'))
KN_CQ = int(os.environ.get('KN_CQ', '2048'))
NTILE_PRE = 4
NTILE_OWN = 4
QSCALE = 192 ** -0.5


def build_program(phases="ABC", alim=0, tiles=None):
    nc = bass.Bass("TRN2", target_bir_lowering=False)

    LAST_DRAM.clear()

    def dram(name, shape, dt, kind="ExternalInput"):
        if SMALL and name in SMALL:
            shape = [1] + list(shape[1:])
        if kind == "ExternalInput":
            LAST_DRAM.append((name, tuple(shape), "bf16" if dt == BF16 else ("i32" if dt == I32 else "f32")))
        return nc.dram_tensor(name, list(shape), dt, kind=kind).ap()

    xs = dram("xs", [4096, 4096], F32)
    posr = dram("posr", [64, 4096], I32)
    pp = dram("pp", [2048, 256], F32)
    pbias_d = dram("pbias", [128, 1], F32)
    w_in_fm = dram("w_in_fm", [33, 128, 4096], F32)
    w_in_tm = dram("w_in_tm", [16, 128, 4096], F32)
    w_in_tmh = dram("w_in_tmh", [64, 128, 2048], F32)
    w_in_tmp = dram("w_in_tmp", [64, 128, 1024], F32)
    w_attn = dram("w_attn", [16, 128, 4096], F32)
    w_o_tm = dram("w_o_tm", [32, 128, 4096], F32)
    w_up_fm = dram("w_up_fm", [128, 128, 4096], F32)
    w_dn_tm = dram("w_dn_tm", [128, 128, 4096], F32)
    w_pg_tm = dram("w_pg_tm", [32, 128, 4096], F32)
    w_pe_t = dram("w_pe_t", [2, 128, 4096], F32)
    g_mix_d = dram("g_mix", [128, 32], F32)
    g_mlp_d = dram("g_mlp", [128, 32], F32)
    g_ple_d = dram("g_ple", [128, 32], F32)
    g_qa_d = dram("g_qa", [128, 6], F32)
    g_kva_d = dram("g_kva", [128, 4], F32)
    g_hg_d = dram("g_hg", [128, 16], F32)
    lbraw_d = dram("lbraw", [128, 32], F32)
    g_post_d = dram("g_post", [128, 4096], F32)
    g_fin_d = dram("g_fin", [128, 4096], F32)
    ident_d = dram("ident", [128, 128], F32)
    ones_d = dram("ones", [128, 128], BF16)
    maskbd_d = dram("maskbd", [128, 128], BF16)
    cme_d = dram("cme", [128, 512], BF16)
    cmo_d = dram("cmo", [128, 512], BF16)
    rme_d = dram("rme", [128, 2], F32)
    rmask_d = dram("rmask", [128, 512], F32)
    cmask_d = dram("cmask", [128, 2048], BF16)
    invf_d = dram("invf", [64, 2], F32)
    out_d = dram("out", [2048, 4096], F32, kind="ExternalOutput")
    mix_d = dram("mixT", [4096, 2048], BF16, kind="Internal")

    with ExitStack() as top:
        S = Sched(nc, top)
        MIX = Buf("MIX")

        def const(name, shape, dt, src, top_=True):
            b = S.sbuf(name, shape, dt, top=top_)
            S.dma("sp", b.t[:], src, writes=[b], sem_buf=b)
            return b

        ident = const("ident", [128, 128], F32, ident_d)
        invf = const("invf", [64, 2], F32, invf_d)
        mid = ExitStack()
        S.stack = mid
        ckvnT = S.sbuf("ckvnT", [128, 4, 4096], BF16)
        krT = S.sbuf("krT", [64, 4096], BF16)
        cqnT = S.sbuf("cqnT", [128, 6, KN_CQ], BF16)

        def rstd_from_ss(rstd, ss, dim):
            S.op("dve", lambda e: e.tensor_scalar(out=rstd.t[:], in0=ss.t[:], scalar1=1.0 / dim, scalar2=EPS,
                                                  op0=ALU.mult, op1=ALU.add), reads=[ss], writes=[rstd])
            S.op("act", lambda e: e.activation(out=rstd.t[:], in_=rstd.t[:], func=AF.Sqrt), reads=[rstd], writes=[rstd])
            S.op("dve", lambda e: e.reciprocal(out=rstd.t[:], in_=rstd.t[:]), reads=[rstd], writes=[rstd])

        cnt = [0]

        def evac_scaled(out_ap, in_ap, scale_ap, reads, writes):
            cnt[0] += 1
            if KN_EVAC == "act":
                S.op("act", lambda e: e.copy(out=out_ap, in_=in_ap), reads=reads, writes=writes)
            elif DBG_COPY:
                S.op("dve", lambda e: e.tensor_copy(out=out_ap, in_=in_ap), reads=reads, writes=writes)
            elif KN_EVAC == "mix" and cnt[0] % 2 == 0:
                S.op("act", lambda e: e.activation(out=out_ap, in_=in_ap, func=AF.Identity, scale=scale_ap),
                     reads=reads, writes=writes)
            else:
                S.op("dve", lambda e: e.tensor_scalar(out=out_ap, in0=in_ap, scalar1=scale_ap, scalar2=None,
                                                      op0=ALU.mult), reads=reads, writes=writes)

        def copy_any(out_ap, in_ap, reads, writes):
            cnt[0] += 1
            if cnt[0] % 2:
                S.op("act", lambda e: e.copy(out=out_ap, in_=in_ap), reads=reads, writes=writes)
            else:
                S.op("dve", lambda e: e.tensor_copy(out=out_ap, in_=in_ap), reads=reads, writes=writes)

        def norm_rows(src, src_ap, width, yb, ss, rstd):
            S.op("dve", lambda e: e.memset(ss.t[:], 0.0), writes=[ss])
            S.op("act", lambda e: e.activation(out=yb.t[:, 0:width], in_=src_ap, func=AF.Square, accum_out=ss.t[:]),
                 reads=[src, ss], writes=[yb, ss])
            rstd_from_ss(rstd, ss, width)
            S.op("dve", lambda e: e.tensor_scalar(out=yb.t[:, 0:width], in0=src_ap, scalar1=rstd.t[:], scalar2=None,
                                                  op0=ALU.mult), reads=[src, rstd], writes=[yb])

        def transpose_out(yb, nchunk, tps, dst_fn, gain, dst_bufs, noevac=False):
            for c0 in range(0, nchunk, 4):
                tp = tps[(c0 // 4) % 2]
                n = min(4, nchunk - c0)
                for j in range(n):
                    c = c0 + j
                    S.op("pe", lambda e, c=c, j=j, tp=tp: e.transpose(tp.t[:, j * 128:(j + 1) * 128],
                                                                      yb.t[:, c * 128:(c + 1) * 128], ident.t[:]),
                         reads=[yb, ident], writes=[tp])
                for j in range(n):
                    if noevac:
                        break
                    c = c0 + j
                    evac_scaled(dst_fn(c), tp.t[:, j * 128:(j + 1) * 128], gain.t[:, c:c + 1],
                                reads=[tp, gain], writes=dst_bufs(c))

        def rope_gen(src, posi_b, posi_ap, ang_b, ang_ap, tcs_b, tcs_ap, tsn_b, tsn_ap, cosb, cos_ap, sinb, sin_ap):
            S.dma("sp", posi_ap, src, writes=[posi_b], sem_buf=posi_b)
            S.op("dve", lambda e: e.tensor_copy(out=ang_ap, in_=posi_ap), reads=[posi_b], writes=[ang_b])
            S.op("dve", lambda e: e.tensor_scalar(out=ang_ap, in0=ang_ap, scalar1=invf.t[:, 0:1], scalar2=None,
                                                  op0=ALU.mult), reads=[ang_b, invf], writes=[ang_b])
            for which in (0, 1):
                shift = 0.5 * PI if which == 0 else 0.0
                S.op("dve", lambda e, shift=shift: e.tensor_scalar(out=tcs_ap, in0=ang_ap, scalar1=shift, scalar2=None, op0=ALU.add),
                     reads=[ang_b], writes=[tcs_b])
                S.op("dve", lambda e: e.tensor_scalar(out=tsn_ap, in0=tcs_ap, scalar1=1.0 / (2 * PI), scalar2=None, op0=ALU.mult),
                     reads=[tcs_b], writes=[tsn_b])
                S.op("dve", lambda e: e.tensor_copy(out=posi_ap, in_=tsn_ap), reads=[tsn_b], writes=[posi_b])
                S.op("dve", lambda e: e.tensor_copy(out=tsn_ap, in_=posi_ap), reads=[posi_b], writes=[tsn_b])
                S.op("dve", lambda e: e.scalar_tensor_tensor(out=tcs_ap, in0=tsn_ap, scalar=-2 * PI, in1=tcs_ap, op0=ALU.mult, op1=ALU.add),
                     reads=[tsn_b, tcs_b], writes=[tcs_b])
                S.op("dve", lambda e: e.tensor_scalar(out=tsn_ap, in0=tcs_ap, scalar1=PI, scalar2=-2 * PI, op0=ALU.is_gt, op1=ALU.mult),
                     reads=[tcs_b], writes=[tsn_b])
                S.op("dve", lambda e: e.tensor_tensor(out=tcs_ap, in0=tcs_ap, in1=tsn_ap, op=ALU.add), reads=[tcs_b, tsn_b], writes=[tcs_b])
                S.op("dve", lambda e: e.tensor_scalar(out=tcs_ap, in0=tcs_ap, scalar1=-PI, scalar2=PI, op0=ALU.max, op1=ALU.min),
                     reads=[tcs_b], writes=[tcs_b])
                if which == 0:
                    S.op("act", lambda e: e.activation(out=cos_ap, in_=tcs_ap, func=AF.Sin), reads=[tcs_b], writes=[cosb])
                else:
                    S.op("act", lambda e: e.activation(out=sin_ap, in_=tcs_ap, func=AF.Sin, scale=invf.t[:, 1:2]),
                         reads=[tcs_b, invf], writes=[sinb])

        if "A" in phases:
            with ExitStack() as pa:
                S.stack = pa
                g_mix = const("g_mix", [128, 32], F32, g_mix_d, False)
                g_qa = const("g_qa", [128, 6], F32, g_qa_d, False)
                g_kva = const("g_kva", [128, 4], F32, g_kva_d, False)
                g_hg = const("g_hg", [128, 16], F32, g_hg_d, False)
                lbraw = const("lbraw", [128, 32], F32, lbraw_d, False)
                maskbd = const("maskbd", [128, 128], BF16, maskbd_d, False)
                cme = const("cme", [128, 512], BF16, cme_d, False)
                cmo = const("cmo", [128, 512], BF16, cmo_d, False)
                rme = const("rme", [128, 2], F32, rme_d, False)
                rmask = const("rmask", [128, 512], F32, rmask_d, False)
                ring = Ring(S, KN_RING, "wra")
                xsb = S.sbuf("xsb", [128, 4096], F32)
                yb = S.sbuf("yb", [128, 4096], F32)
                uT = S.sbuf("uT", [128, 32, 512], BF16)
                uTp = [Buf("uTp%d" % i) for i in range(4)]
                class _CC:
                    def __init__(self, ap):
                        self.ap = ap
                    def __getitem__(self, k):
                        return self.ap[k]
                cc = [xsb, xsb, xsb, yb]
                ccv = [_CC(xsb.t[:, 0:1280]), _CC(xsb.t[:, 1280:2560]), _CC(xsb.t[:, 2560:3840]), _CC(yb.t[:, 0:1280])]
                ss = S.sbuf("ss", [128, 1], F32)
                rstd = S.sbuf("rstd", [128, 1], F32)
                tm = [S.psum("tm%d" % i, [128, 512], F32) for i in range(4)]
                fm = [S.psum("fm%d" % i, [128, 512], F32) for i in range(2)]
                tpt = S.psum("tp", [128, 512], F32)
                tpt2 = S.psum("tp2", [128, 512], F32)
                tps = [Buf("tpa", None), Buf("tpb", None)]
                tps[0].excl = tps[1].excl = True
                class _V:
                    def __init__(self, t, off):
                        self.t_, self.off = t, off
                    def __getitem__(self, k):
                        rows, cols = k
                        return self.t_[rows, self.off + cols.start:self.off + cols.stop]
                tps[0].t = _V(tpt.t, 0)
                tps[1].t = _V(tpt2.t, 0)
                smA = smS = tps[0]
                smO = smT1 = smT2 = tps[1]
                apA = tpt.t[:, 0:128]
                apS = tpt.t[:, 128:256]
                apO = tpt2.t[:, 0:128]
                apT1 = tpt2.t[:, 128:256]
                apT2 = tpt2.t[:, 256:384]
                lb = S.sbuf("lb", [128, 16], F32)
                oml = S.sbuf("oml", [128, 16], F32)
                S.op("dve", lambda e: e.tensor_tensor(out=lb.t[:], in0=lbraw.t[:, 0:16], in1=lbraw.t[:, 16:32],
                                                      op=ALU.subtract), reads=[lbraw], writes=[lb])
                S.op("act", lambda e: e.activation(out=lb.t[:], in_=lb.t[:], func=AF.Sigmoid), reads=[lb], writes=[lb])
                S.op("dve", lambda e: e.tensor_scalar(out=oml.t[:], in0=lb.t[:], scalar1=-1.0, scalar2=1.0,
                                                      op0=ALU.mult, op1=ALU.add), reads=[lb], writes=[oml])
                Sst = [S.sbuf("S%d" % h, [128, 128], F32) for h in range(16)]
                Seb2 = [S.sbuf("Seb%d" % h, [128, 128], BF16) for h in range(2)]
                Sob = S.sbuf("Sob", [128, 128], BF16)
                for h in range(16):
                    S.op("dve", lambda e, h=h: e.memset(Sst[h].t[:], 0.0), writes=[Sst[h]])
                def f32t(n):
                    return S.sbuf(n, [128, 512], F32)
                def b16t(n):
                    return S.sbuf(n, [128, 512], BF16)
                escr = S.sbuf("escr", [128, 3072], F32)
                t_sg, t_lf, t_bb, t_qs, t_ex, t_kk = [Buf(n_, escr.t[:, i_ * 512:(i_ + 1) * 512]) for i_, n_ in enumerate(("sg", "lf", "bb", "qs", "ex", "kk"))]
                t_Am = b16t("Am")
                t_qe2 = [b16t("qe%d" % i) for i in range(2)]
                t_qee2 = [b16t("qee%d" % i) for i in range(2)]
                t_qeo2 = [b16t("qeo%d" % i) for i in range(2)]
                t_ke2 = [b16t("ke%d" % i) for i in range(2)]
                t_ke32 = [f32t("ke3%d" % i) for i in range(2)]
                elast2 = [S.sbuf("elast%d" % i, [128, 8], F32) for i in range(2)]
                Vv = [S.sbuf("Vv%d" % i, [128, 4, 128], BF16) for i in range(2)]
                Ve = [S.sbuf("Ve%d" % i, [128, 4, 128], BF16) for i in range(2)]
                Vo = [S.sbuf("Vo%d" % i, [128, 4, 128], BF16) for i in range(2)]
                gs = [S.sbuf("gs%d" % i, [128, 4, 128], F32) for i in range(2)]
                onesf = S.sbuf("onesf", [128, 1], F32)
                S.op("dve", lambda e: e.memset(onesf.t[:], 1.0), writes=[onesf])
                ke3T4 = S.sbuf("ke3T4", [128, 512], BF16)
                ke3T = S.sbuf("ke3T", [128, 128], BF16)
                onb = S.sbuf("onb", [128, 128], F32)
                oTs = [S.sbuf("oT%d" % i, [128, 512], BF16) for i in range(2)]
                sso = S.sbuf("sso", [128, 1], F32)
                epst = S.sbuf("epst", [128, 1], F32)
                S.op("dve", lambda e: e.memset(epst.t[:], EPS), writes=[epst])
                rso = S.sbuf("rso", [128, 1], F32)
                junk = S.sbuf("junk", [128, 128], F32)
                class _T:
                    def __init__(self, ap):
                        self.ap = ap
                    def __getitem__(self, k):
                        return self.ap[k]
                def _alias(b, dt=None):
                    nb_ = Buf(b.name)
                    nb_.__class__ = Buf
                    return b
                posi_b, ang_b, tcs_b, tsn_b, cosk, sink = t_kk, t_sg, t_lf, t_bb, t_qs, t_ex
                posi_ap = t_kk.t[0:64, :].bitcast(I32)
                ang_ap, tcs_ap, tsn_ap = t_sg.t[0:64, :], t_lf.t[0:64, :], t_bb.t[0:64, :]
                cosk_ap, sink_ap = t_qs.t[0:64, :], t_ex.t[0:64, :]
                ybc = Buf("ybc", escr.t[:, 0:1280])

                def rope_tables(tokbase, cos_ap, sin_ap, cosb, sinb):
                    rope_gen(posr[:, tokbase:tokbase + 512], posi_b, posi_ap, ang_b, ang_ap, tcs_b, tcs_ap, tsn_b, tsn_ap,
                             cosb, cos_ap, sinb, sin_ap)

                def gemm_tm(tiles, width, evac):
                    for q in range(4):
                        sl = ring.load(tiles[q], 8 * width)
                        for s in range(4):
                            for kk in range(8):
                                kc = q * 8 + kk
                                S.op("pe", lambda e, s=s, kc=kc, kk=kk, sl=sl, q=q: e.matmul(
                                    tm[s].t[:, 0:width], lhsT=uT.t[:, kc, s * 128:(s + 1) * 128],
                                    rhs=sl.t[:, kk * width:(kk + 1) * width],
                                    start=(kc == 0), stop=(kc == 31)), reads=[uTp[q], sl], writes=[tm[s]])
                    for s in range(4):
                        evac(s)

                def gemm_fm(tile, outs):
                    sl = ring.load(tile)
                    for (pb, pap, c0, ncol) in outs:
                        for kc in range(32):
                            S.op("pe", lambda e, kc=kc, pap=pap, c0=c0, ncol=ncol, sl=sl: e.matmul(
                                pap, lhsT=sl.t[:, kc * 128 + c0: kc * 128 + c0 + ncol], rhs=uT.t[:, kc, :],
                                start=(kc == 0), stop=(kc == 31)), reads=[uTp[kc // 8], sl], writes=[pb])

                for T in (tiles if tiles is not None else range(NTILE_PRE + NTILE_OWN)):
                    own = T >= NTILE_PRE
                    tok0 = T * 512
                    otok0 = (T - NTILE_PRE) * 512
                    for s in range(KN_SUB):
                        r0 = tok0 + s * 128
                        S.dma("sp", xsb.t[:], xs[r0:r0 + 128, :], writes=[xsb], sem_buf=xsb)
                        norm_rows(xsb, xsb.t[:], 4096, yb, ss, rstd)
                        if alim == 11:
                            continue
                        transpose_out(yb, KN_NCH, tps, lambda c, s=s: uT.t[:, c, s * 128:(s + 1) * 128], g_mix,
                                      (lambda c: []) if alim == 13 else (lambda c: [uTp[c // 8]]), noevac=(alim == 12))
                    if alim in (1, 11, 12, 13):
                        continue
                    groups = [(2, 1024, 256)] if not own else [(0, 0, 512), (1, 512, 256), (2, 768 + 256, 256)]
                    glist = ([0, 1] if own else []) + [2, 3]
                    for g in glist:
                        coff = {0: 0, 1: 512, 2: 768, 3: 1024}[g]
                        width = 512 if g == 0 else 256
                        def ev(s, coff=coff, width=width):
                            copy_any(ccv[s][:, coff:coff + width], tm[s].t[:, 0:width], [tm[s]], [cc[s]])
                        gemm_tm([w_in_tm[g * 4 + q][:, 0:8 * width] for q in range(4)], width, ev)
                    for s in range(4):
                        if own:
                            norm_rows(cc[s], ccv[s][:, 0:768], 768, ybc, ss, rstd)
                            transpose_out(ybc, 6, tps,
                                          lambda c, s=s: cqnT.t[:, c, otok0 + s * 128: otok0 + (s + 1) * 128], g_qa,
                                          lambda c: [cqnT])
                        S.op("dve", lambda e: e.memset(ss.t[:], 0.0), writes=[ss])
                        S.op("act", lambda e, s=s: e.activation(out=ybc.t[:, 0:512], in_=ccv[s][:, 768:1280], func=AF.Square,
                                                                accum_out=ss.t[:]), reads=[cc[s], ss], writes=[ybc, ss])
                        rstd_from_ss(rstd, ss, 512)
                        S.op("dve", lambda e, s=s: e.tensor_scalar(out=ybc.t[:, 0:512], in0=ccv[s][:, 768:1280], scalar1=rstd.t[:],
                                                                   scalar2=None, op0=ALU.mult), reads=[cc[s], rstd], writes=[ybc])
                        transpose_out(ybc, 4, tps,
                                      lambda c, s=s: ckvnT.t[:, c, tok0 + s * 128: tok0 + (s + 1) * 128], g_kva,
                                      lambda c: [ckvnT])
                    if alim == 2:
                        continue
                    gemm_fm(w_in_fm[0], [(fm[0], fm[0].t[0:64, :], 0, 64), (fm[1], fm[1].t[0:64, :], 64, 64)])
                    rope_tables(tok0, cosk_ap, sink_ap, cosk, sink)
                    S.op("dve", lambda e: e.tensor_tensor(out=tcs_ap, in0=fm[0].t[0:64, :], in1=cosk_ap, op=ALU.mult),
                         reads=[fm[0], cosk], writes=[tcs_b])
                    S.op("dve", lambda e: e.tensor_tensor(out=tsn_ap, in0=fm[1].t[0:64, :], in1=sink_ap, op=ALU.mult),
                         reads=[fm[1], sink], writes=[tsn_b])
                    S.op("dve", lambda e, tok0=tok0: e.tensor_tensor(out=krT.t[:, tok0:tok0 + 512], in0=tcs_ap, in1=tsn_ap,
                                                                     op=ALU.add), reads=[tcs_b, tsn_b], writes=[krT])
                    if alim == 3:
                        continue
                    nh = 16 if alim == 0 else 2

                    def G_gen(h, own=own):
                        vs = h % 2
                        fms = ([(1 + 2 * h, fm[0])] if own else []) + [(2 + 2 * h, fm[1])]
                        for (ti, fb_) in fms:
                            sl = ring.load(w_in_fm[ti])
                            for kc in range(32):
                                S.op("pe", lambda e, kc=kc, fb_=fb_, sl=sl: e.matmul(
                                    fb_.t[:], lhsT=sl.t[:, kc * 128:(kc + 1) * 128], rhs=uT.t[:, kc, :],
                                    start=(kc == 0), stop=(kc == 31)), reads=[uTp[kc // 8], sl], writes=[fb_])
                                if kc % 8 == 7:
                                    yield "fm"
                        if own:
                            tiles = [w_in_tmh[h * 4 + q] for q in range(4)]
                            width = 256
                        else:
                            tiles = [w_in_tmp[h * 4 + q] for q in range(4)]
                            width = 128
                        n = 0
                        for q in range(4):
                            sl = ring.load(tiles[q], 8 * width)
                            for s in range(4):
                                for kk in range(8):
                                    kc = q * 8 + kk
                                    S.op("pe", lambda e, s=s, kc=kc, kk=kk, sl=sl, width=width: e.matmul(
                                        tm[s].t[:, 0:width], lhsT=uT.t[:, kc, s * 128:(s + 1) * 128],
                                        rhs=sl.t[:, kk * width:(kk + 1) * width],
                                        start=(kc == 0), stop=(kc == 31)), reads=[uTp[q], sl], writes=[tm[s]])
                                    n += 1
                                    if n % 8 == 0 and n < 128:
                                        yield "tm"
                        for s in range(4):
                            vsrc = tm[s].t[:, 0:128]
                            if own:
                                S.op("act", lambda e, s=s, vs=vs, vsrc=vsrc: e.copy(out=Vv[vs].t[:, s, :], in_=vsrc),
                                     reads=[tm[s]], writes=[Vv[vs]])
                                S.op("act", lambda e, s=s, vs=vs: e.activation(out=gs[vs].t[:, s, :], in_=tm[s].t[:, 128:256], func=AF.Silu),
                                     reads=[tm[s]], writes=[gs[vs]])
                            else:
                                copy_any(Vv[vs].t[:, s, :], vsrc, [tm[s]], [Vv[vs]])
                                continue
                            S.op("dve", lambda e, s=s, vs=vs, vsrc=vsrc: e.tensor_scalar(out=Ve[vs].t[:, s, :], in0=vsrc, scalar1=rme.t[:, 0:1],
                                                                                         scalar2=None, op0=ALU.mult), reads=[tm[s], rme], writes=[Ve[vs]])
                            S.op("dve", lambda e, s=s, vs=vs, vsrc=vsrc: e.tensor_scalar(out=Vo[vs].t[:, s, :], in0=vsrc, scalar1=rme.t[:, 1:2],
                                                                                         scalar2=None, op0=ALU.mult), reads=[tm[s], rme], writes=[Vo[vs]])
                        yield "tm"

                    def E_gen(h, own=own):
                        par = h % 2
                        t_qe, t_qee, t_qeo, t_ke, t_ke3, elast = t_qe2[par], t_qee2[par], t_qeo2[par], t_ke2[par], t_ke32[par], elast2[par]
                        S.op("act", lambda e: e.activation(out=t_sg.t[:], in_=fm[1].t[:], func=AF.Sigmoid), reads=[fm[1]], writes=[t_sg])
                        if own:
                            yield
                            S.op("act", lambda e: e.activation(out=t_qs.t[:], in_=fm[0].t[:], func=AF.Silu), reads=[fm[0]], writes=[t_qs])
                        yield
                        S.op("dve", lambda e, h=h: e.tensor_scalar(out=t_sg.t[:], in0=t_sg.t[:], scalar1=oml.t[:, h:h + 1], scalar2=lb.t[:, h:h + 1],
                                                                   op0=ALU.mult, op1=ALU.add), reads=[t_sg, oml, lb], writes=[t_sg])
                        yield
                        S.op("act", lambda e: e.activation(out=t_lf.t[:], in_=t_sg.t[:], func=AF.Ln), reads=[t_sg], writes=[t_lf])
                        yield
                        S.op("dve", lambda e: e.tensor_scalar(out=t_kk.t[:], in0=t_sg.t[:], scalar1=-1.0, scalar2=1.0,
                                                              op0=ALU.mult, op1=ALU.add), reads=[t_sg], writes=[t_kk])
                        if not own:
                            yield
                            S.op("dve", lambda e: e.tensor_tensor_scan(out=t_bb.t[:], data0=onesf.t[:, 0:1].to_broadcast([128, 512]), data1=t_lf.t[:], initial=0.0,
                                                                       op0=ALU.mult, op1=ALU.add), reads=[onesf, t_lf], writes=[t_bb])
                            yield
                            S.op("act", lambda e: e.activation(out=elast.t[:, 0:1], in_=t_bb.t[:, 511:512], func=AF.Exp), reads=[t_bb], writes=[elast])
                            yield
                            S.op("dve", lambda e: e.tensor_scalar(out=t_ex.t[:], in0=t_bb.t[:], scalar1=t_bb.t[:, 511:512], scalar2=-1.0,
                                                                  op0=ALU.subtract, op1=ALU.mult), reads=[t_bb], writes=[t_ex])
                            yield
                            S.op("act", lambda e: e.activation(out=t_ex.t[:], in_=t_ex.t[:], func=AF.Exp), reads=[t_ex], writes=[t_ex])
                            yield
                            S.op("dve", lambda e: e.tensor_tensor(out=t_ke3.t[:], in0=t_kk.t[:], in1=t_ex.t[:], op=ALU.mult),
                                 reads=[t_kk, t_ex], writes=[t_ke3])
                            yield
                            return
                        yield
                        S.op("dve", lambda e: e.tensor_tensor_scan(out=t_bb.t[:], data0=rmask.t[:], data1=t_lf.t[:], initial=0.0,
                                                                   op0=ALU.mult, op1=ALU.add), reads=[rmask, t_lf], writes=[t_bb])
                        bbv = t_bb.t[:].rearrange("p (c t) -> p c t", t=64)
                        yield
                        S.op("act", lambda e, bbv=bbv: e.activation(out=elast.t[:], in_=bbv[:, :, 63], func=AF.Exp), reads=[t_bb], writes=[elast])
                        yield
                        S.op("dve", lambda e, bbv=bbv: e.tensor_tensor(out=t_ex.t[:].rearrange("p (c t) -> p c t", t=64),
                                                                       in0=bbv[:, :, 63:64].to_broadcast([128, 8, 64]), in1=bbv, op=ALU.subtract),
                             reads=[t_bb], writes=[t_ex])
                        yield
                        S.op("act", lambda e: e.activation(out=t_ex.t[:], in_=t_ex.t[:], func=AF.Exp), reads=[t_ex], writes=[t_ex])
                        yield
                        S.op("dve", lambda e: e.tensor_tensor(out=t_ke3.t[:], in0=t_kk.t[:], in1=t_ex.t[:], op=ALU.mult),
                             reads=[t_kk, t_ex], writes=[t_ke3])
                        if own:
                            yield
                            S.op("act", lambda e: e.activation(out=t_ex.t[:], in_=t_bb.t[:], func=AF.Exp), reads=[t_bb, t_ke3], writes=[t_ex])
                            yield
                            S.op("dve", lambda e: e.tensor_tensor(out=t_qe.t[:], in0=t_qs.t[:], in1=t_ex.t[:], op=ALU.mult),
                                 reads=[t_qs, t_ex], writes=[t_qe])
                            yield
                            S.op("act", lambda e: e.activation(out=t_ex.t[:], in_=t_bb.t[:], func=AF.Exp, scale=-1.0), reads=[t_bb], writes=[t_ex])
                            yield
                            S.op("dve", lambda e: e.tensor_tensor(out=t_qee.t[:], in0=t_qe.t[:], in1=cme.t[:], op=ALU.mult),
                                 reads=[t_qe, cme], writes=[t_qee])
                            yield
                            S.op("dve", lambda e: e.tensor_tensor(out=t_qeo.t[:], in0=t_qe.t[:], in1=cmo.t[:], op=ALU.mult),
                                 reads=[t_qe, cmo], writes=[t_qeo])
                            yield
                            S.op("dve", lambda e: e.tensor_tensor(out=t_ke.t[:], in0=t_kk.t[:], in1=t_ex.t[:], op=ALU.mult),
                                 reads=[t_kk, t_ex], writes=[t_ke])

                    def P_gen(h, own=own, otok0=otok0):
                        vs = h % 2
                        oT = oTs[h % 2]
                        par = h % 2
                        t_qe, t_qee, t_qeo, t_ke, t_ke3, elast = t_qe2[par], t_qee2[par], t_qeo2[par], t_ke2[par], t_ke32[par], elast2[par]
                        Sebh = Seb2[par]
                        if own:
                            S.op("act", lambda e, h=h: e.copy(out=Sebh.t[:], in_=Sst[h].t[:]), reads=[Sst[h]], writes=[Sebh])
                        if not own:
                            for pr in range(4):
                                S.op("pe", lambda e, pr=pr: e.transpose(tpt2.t[:, pr * 128:(pr + 1) * 128], t_ke3.t[:, pr * 128:(pr + 1) * 128], ident.t[:]),
                                     reads=[t_ke3, ident], writes=[smT1])
                            S.op("act", lambda e: e.copy(out=ke3T4.t[:], in_=tpt2.t[:]), reads=[smT1], writes=[ke3T4])
                            yield
                            for pr in range(4):
                                S.op("pe", lambda e, pr=pr, vs=vs: e.matmul(apS, lhsT=ke3T4.t[:, pr * 128:(pr + 1) * 128], rhs=Vv[vs].t[:, pr, :],
                                                                            start=(pr == 0), stop=(pr == 3)), reads=[ke3T4, Vv[vs]], writes=[smS])
                            S.op("dve", lambda e, h=h: e.scalar_tensor_tensor(out=Sst[h].t[:], in0=Sst[h].t[:], scalar=elast.t[:, 0:1],
                                                                              in1=apS, op0=ALU.mult, op1=ALU.add),
                                 reads=[Sst[h], elast, smS], writes=[Sst[h]])
                            yield
                            return
                        for pr in range(4):
                            cs = slice(pr * 128, pr * 128 + 128)
                            if own:
                                S.op("pe", lambda e, cs=cs: e.matmul(apA, lhsT=t_ke.t[:, cs], rhs=t_qe.t[:, cs], start=True, stop=True),
                                     reads=[t_ke, t_qe], writes=[smA])
                                S.op("dve", lambda e, cs=cs: e.tensor_tensor(out=t_Am.t[:, cs], in0=apA, in1=maskbd.t[:], op=ALU.mult),
                                     reads=[smA, maskbd], writes=[t_Am])
                            S.op("pe", lambda e, cs=cs: e.transpose(apT1, t_ke3.t[:, cs], ident.t[:]), reads=[t_ke3, ident], writes=[smT1])
                            S.op("act", lambda e: e.copy(out=ke3T.t[:], in_=apT1), reads=[smT1], writes=[ke3T])
                            yield
                            S.op("pe", lambda e, pr=pr, vs=vs: e.matmul(apS, lhsT=ke3T.t[:], rhs=Ve[vs].t[:, pr, :], start=True, stop=True),
                                 reads=[ke3T, Ve[vs]], writes=[smS])
                            S.op("dve", lambda e, h=h, pr=pr: e.scalar_tensor_tensor(out=Sst[h].t[:], in0=Sst[h].t[:], scalar=elast.t[:, 2 * pr:2 * pr + 1],
                                                                                     in1=apS, op0=ALU.mult, op1=ALU.add),
                                 reads=[Sst[h], elast, smS], writes=[Sst[h]])
                            if own:
                                S.op("act", lambda e, h=h: e.copy(out=Sob.t[:], in_=Sst[h].t[:]), reads=[Sst[h]], writes=[Sob])
                            yield
                            if own:
                                S.op("pe", lambda e, cs=cs, pr=pr, vs=vs: e.matmul(apO, lhsT=t_Am.t[:, cs], rhs=Vv[vs].t[:, pr, :], start=True, stop=False),
                                     reads=[t_Am, Vv[vs]], writes=[smO])
                                S.op("pe", lambda e, cs=cs, h=h: e.matmul(apO, lhsT=t_qee.t[:, cs], rhs=Sebh.t[:], start=False, stop=False),
                                     reads=[t_qee, Sebh], writes=[smO])
                                S.op("pe", lambda e, cs=cs: e.matmul(apO, lhsT=t_qeo.t[:, cs], rhs=Sob.t[:], start=False, stop=True),
                                     reads=[t_qeo, Sob], writes=[smO])
                            S.op("pe", lambda e, pr=pr, vs=vs: e.matmul(apS, lhsT=ke3T.t[:], rhs=Vo[vs].t[:, pr, :], start=True, stop=True),
                                 reads=[ke3T, Vo[vs]], writes=[smS])
                            S.op("dve", lambda e, h=h, pr=pr: e.scalar_tensor_tensor(out=Sst[h].t[:], in0=Sst[h].t[:], scalar=elast.t[:, 2 * pr + 1:2 * pr + 2],
                                                                                     in1=apS, op0=ALU.mult, op1=ALU.add),
                                 reads=[Sst[h], elast, smS], writes=[Sst[h]])
                            if own:
                                S.op("act", lambda e, h=h: e.copy(out=Sebh.t[:], in_=Sst[h].t[:]), reads=[Sst[h]], writes=[Sebh])
                                S.op("dve", lambda e: e.memset(sso.t[:], 0.0), writes=[sso])
                                S.op("act", lambda e: e.activation(out=junk.t[:], in_=apO, func=AF.Square, accum_out=sso.t[:]),
                                     reads=[smO, sso], writes=[junk, sso])
                                S.op("act", lambda e: e.activation(out=rso.t[:], in_=sso.t[:], func=AF.Sqrt, bias=epst.t[:], scale=1.0 / 128),
                                     reads=[sso, epst], writes=[rso])
                                S.op("dve", lambda e: e.reciprocal(out=rso.t[:], in_=rso.t[:]), reads=[rso], writes=[rso])
                                S.op("dve", lambda e, pr=pr, vs=vs: e.scalar_tensor_tensor(out=onb.t[:], in0=apO, scalar=rso.t[:], in1=gs[vs].t[:, pr, :],
                                                                                           op0=ALU.mult, op1=ALU.mult), reads=[smO, rso, gs[vs]], writes=[onb])
                            yield
                            if own:
                                yield
                                yield
                                S.op("pe", lambda e: e.transpose(apT2, onb.t[:], ident.t[:]), reads=[onb, ident], writes=[smT2])
                                S.op("dve", lambda e, cs=cs, h=h, oT=oT: e.tensor_scalar(out=oT.t[:, cs], in0=apT2, scalar1=g_hg.t[:, h:h + 1], scalar2=None, op0=ALU.mult),
                                     reads=[smT2, g_hg], writes=[oT])
                                yield
                        if own:
                            S.dma("sp", mix_d[2048 + h * 128: 2048 + (h + 1) * 128, otok0:otok0 + 512], oT.t[:],
                                  reads=[oT], writes=[MIX], sem_buf=oT, kind="r")

                    def drain(g):
                        for _ in g:
                            pass

                    def interleave3(g, p, e_, nfm):
                        gi = iter(g) if g is not None else None
                        pi = iter(p) if p is not None else None
                        ei = iter(e_) if e_ is not None else None
                        gcount = 0
                        while gi is not None or pi is not None or ei is not None:
                            if gi is not None:
                                if next(gi, None) is None:
                                    gi = None
                                gcount += 1
                            if pi is not None and next(pi, "end") == "end":
                                pi = None
                            if ei is not None and (gcount > nfm or gi is None):
                                for _ in range(KN_ESTEP):
                                    if next(ei, "end") == "end":
                                        ei = None
                                        break

                    drain(G_gen(0))
                    drain(E_gen(0))
                    nfm = 8 if own else 4
                    for i in range(nh):
                        if i + 1 < nh:
                            interleave3(G_gen(i + 1), P_gen(i), E_gen(i + 1), nfm)
                        else:
                            drain(P_gen(i))
                print("phase A", S.flush())

        if "B" in phases:
            with ExitStack() as pb_:
                S.stack = pb_
                ones = const("ones", [128, 128], BF16, ones_d, False)
                cmask = const("cmask", [128, 2048], BF16, cmask_d, False)
                pbias = const("pbias", [128, 1], F32, pbias_d, False)
                ring = Ring(S, 2, "wrb")
                cosq = S.sbuf("cosq", [64, 2048], F32)
                sinq = S.sbuf("sinq", [64, 2048], F32)
                posi = S.sbuf("posi", [64, 512], I32)
                ang = S.sbuf("ang", [64, 512], F32)
                tcs = S.sbuf("tcs", [64, 512], F32)
                tsn = S.sbuf("tsn", [64, 512], F32)
                for qt in range(4):
                    tb = 2048 + qt * 512
                    cs = slice(qt * 512, qt * 512 + 512)
                    rope_gen(posr[:, tb:tb + 512], posi, posi.t[:], ang, ang.t[:], tcs, tcs.t[:], tsn, tsn.t[:],
                             cosq, cosq.t[:, cs], sinq, sinq.t[:, cs])
                KT = S.sbuf("KT", [128, 4096], BF16)
                Vh = S.sbuf("Vh", [128, 32, 128], BF16)
                QN = S.sbuf("QN", [128, 2048], BF16)
                QR = S.sbuf("QR", [64, 2048], BF16)
                sq = S.sbuf("sq", [128, 512], BF16)
                PTs = [S.sbuf("PT%d" % i, [128, 512], BF16) for i in range(3)]
                rden = S.sbuf("rden", [128, 512], F32)
                OTs = [S.sbuf("OT%d" % i, [128, 512], BF16) for i in range(2)]
                red = S.sbuf("red", [128, 1], F32)
                krmax = S.sbuf("krmax", [128, 1], F32)
                knmax = S.sbuf("knmax", [128, 1], F32)
                qnmax = S.sbuf("qnmax", [128, 1], F32)
                qrmax = S.sbuf("qrmax", [128, 1], F32)
                negB = S.sbuf("negB", [128, 1], F32)
                negBp = S.sbuf("negBp", [128, 1], F32)
                stp = [S.psum("st%d" % i, [128, 512], F32) for i in range(2)]
                otp = S.psum("otp", [128, 512], F32)
                dnp = S.psum("dnp", [128, 512], F32)
                fm = [S.psum("fmb%d" % i, [128, 512], F32) for i in range(2)]
                vps = S.psum("vps", [128, 512], F32)
                nb = S.psum("nb", [128, 512], F32)

                sq2 = S.sbuf("sq2", [128, 512], BF16)
                sqs = [sq, sq2]
                pend = []
                sqn = [0]

                def sqmax_flush(keep=0):
                    while len(pend) > keep:
                        sqb, nrows, acc = pend.pop(0)
                        S.op("pe", lambda e, sqb=sqb, nrows=nrows: e.matmul(nb.t[:], lhsT=ones.t[0:nrows, :], rhs=sqb.t[0:nrows, :], start=True, stop=True),
                             reads=[ones, sqb], writes=[nb])
                        S.op("dve", lambda e: e.tensor_reduce(out=red.t[:], in_=nb.t[:], axis=AX.X, op=ALU.max), reads=[nb], writes=[red])
                        S.op("dve", lambda e, acc=acc: e.tensor_tensor(out=acc.t[:], in0=acc.t[:], in1=red.t[:], op=ALU.max), reads=[acc, red], writes=[acc])

                def sqmax(src_buf, src_ap, nrows, acc):
                    sqmax_flush(0)
                    sqb = sqs[sqn[0] % 2]
                    sqn[0] += 1
                    S.op("dve", lambda e: e.tensor_tensor(out=sqb.t[0:nrows, :], in0=src_ap, in1=src_ap, op=ALU.mult),
                         reads=[src_buf], writes=[sqb])
                    pend.append((sqb, nrows, acc))

                S.op("dve", lambda e: e.memset(krmax.t[:], 0.0), writes=[krmax])
                for kt in range(8):
                    sqmax(krT, krT.t[:, kt * 512:(kt + 1) * 512], 64, krmax)
                sqmax_flush(0)

                for h in range(16):
                    sl = ring.load(w_attn[h][:, 0:2560], 2560) if False else ring.load(w_attn[h])
                    S.op("dve", lambda e: e.memset(knmax.t[:], 0.0), writes=[knmax])
                    S.op("dve", lambda e: e.memset(qnmax.t[:], 0.0), writes=[qnmax])
                    S.op("dve", lambda e: e.memset(qrmax.t[:], 0.0), writes=[qrmax])
                    for kt in range(8):
                        f = fm[kt % 2]
                        ks = slice(kt * 512, kt * 512 + 512)
                        for kc in range(4):
                            S.op("pe", lambda e, kc=kc, ks=ks, f=f, sl=sl: e.matmul(f.t[:], lhsT=sl.t[:, 1536 + kc * 128: 1536 + (kc + 1) * 128],
                                                                                   rhs=ckvnT.t[:, kc, ks], start=(kc == 0), stop=(kc == 3)),
                                 reads=[sl, ckvnT], writes=[f])
                        copy_any(KT.t[:, ks], f.t[:], [f], [KT])
                        sqmax(KT, KT.t[:, ks], 128, knmax)
                    for sg_ in range(8):
                        for j in range(4):
                            st_ = sg_ * 4 + j
                            for kc in range(4):
                                S.op("pe", lambda e, kc=kc, st_=st_, j=j, sl=sl: e.matmul(vps.t[:, j * 128:(j + 1) * 128],
                                                                                         lhsT=ckvnT.t[:, kc, st_ * 128:(st_ + 1) * 128],
                                                                                         rhs=sl.t[:, 2048 + kc * 128: 2048 + (kc + 1) * 128],
                                                                                         start=(kc == 0), stop=(kc == 3)),
                                     reads=[sl, ckvnT], writes=[vps])
                        copy_any(Vh.t[:, sg_ * 4:(sg_ + 1) * 4, :], vps.t[:].rearrange("p (a b) -> p a b", b=128), [vps], [Vh])
                    for qt in range(4):
                        cs = slice(qt * 512, qt * 512 + 512)
                        f = fm[0]
                        for kc in range(6):
                            S.op("pe", lambda e, kc=kc, cs=cs, sl=sl: e.matmul(fm[0].t[:], lhsT=sl.t[:, kc * 128:(kc + 1) * 128], rhs=cqnT.t[:, kc, cs],
                                                                              start=(kc == 0), stop=(kc == 5)), reads=[sl, cqnT], writes=[fm[0]])
                        copy_any(QN.t[:, cs], fm[0].t[:], [fm[0]], [QN])
                        sqmax(QN, QN.t[:, cs], 128, qnmax)
                        for kc in range(6):
                            S.op("pe", lambda e, kc=kc, cs=cs, sl=sl: e.matmul(fm[1].t[0:64, :], lhsT=sl.t[:, 768 + kc * 128: 768 + kc * 128 + 64],
                                                                              rhs=cqnT.t[:, kc, cs], start=(kc == 0), stop=(kc == 5)),
                                 reads=[sl, cqnT], writes=[fm[1]])
                        S.op("dve", lambda e, cs=cs: e.tensor_tensor(out=tcs.t[:], in0=fm[1].t[0:64, :], in1=cosq.t[:, cs], op=ALU.mult),
                             reads=[fm[1], cosq], writes=[tcs])
                        for kc in range(6):
                            S.op("pe", lambda e, kc=kc, cs=cs, sl=sl: e.matmul(fm[1].t[0:64, :], lhsT=sl.t[:, 768 + kc * 128 + 64: 768 + kc * 128 + 128],
                                                                              rhs=cqnT.t[:, kc, cs], start=(kc == 0), stop=(kc == 5)),
                                 reads=[sl, cqnT], writes=[fm[1]])
                        S.op("dve", lambda e, cs=cs: e.tensor_tensor(out=tsn.t[:], in0=fm[1].t[0:64, :], in1=sinq.t[:, cs], op=ALU.mult),
                             reads=[fm[1], sinq], writes=[tsn])
                        S.op("dve", lambda e, cs=cs: e.tensor_tensor(out=QR.t[:, cs], in0=tcs.t[:], in1=tsn.t[:], op=ALU.add),
                             reads=[tcs, tsn], writes=[QR])
                        sqmax(QR, QR.t[:, cs], 64, qrmax)
                    sqmax_flush(0)
                    S.op("dve", lambda e: e.tensor_tensor(out=qnmax.t[:], in0=qnmax.t[:], in1=qrmax.t[:], op=ALU.add), reads=[qnmax, qrmax], writes=[qnmax])
                    S.op("dve", lambda e: e.tensor_tensor(out=knmax.t[:], in0=knmax.t[:], in1=krmax.t[:], op=ALU.add), reads=[knmax, krmax], writes=[knmax])
                    S.op("dve", lambda e: e.tensor_tensor(out=negB.t[:], in0=qnmax.t[:], in1=knmax.t[:], op=ALU.mult), reads=[qnmax, knmax], writes=[negB])
                    S.op("act", lambda e: e.activation(out=negB.t[:], in_=negB.t[:], func=AF.Sqrt), reads=[negB], writes=[negB])
                    S.op("dve", lambda e: e.tensor_scalar(out=negB.t[:], in0=negB.t[:], scalar1=-QSCALE, scalar2=None, op0=ALU.mult),
                         reads=[negB], writes=[negB])
                    S.op("dve", lambda e: e.tensor_tensor(out=negBp.t[:], in0=negB.t[:], in1=pbias.t[:], op=ALU.add), reads=[negB, pbias], writes=[negBp])
                    for qt in range(4):
                        cs = slice(qt * 512, qt * 512 + 512)
                        nkb = 16 + 4 * qt + 4
                        OT = OTs[qt % 2]

                        def qk(kb, cs=cs, qt=qt):
                            st_ = stp[kb % 2]
                            ks = slice(kb * 128, kb * 128 + 128)
                            S.op("pe", lambda e: e.matmul(st_.t[:], lhsT=KT.t[:, ks], rhs=QN.t[:, cs], start=True, stop=False),
                                 reads=[KT, QN], writes=[st_])
                            S.op("pe", lambda e: e.matmul(st_.t[:], lhsT=krT.t[:, ks], rhs=QR.t[:, cs], start=False, stop=True),
                                 reads=[krT, QR], writes=[st_])
                            PT = PTs[kb % 3]
                            bias = negBp if kb < 16 else negB
                            S.op("act", lambda e: e.activation(out=PT.t[:], in_=st_.t[:], func=AF.Exp, bias=bias.t[:], scale=QSCALE),
                                 reads=[st_, bias], writes=[PT])
                            j = kb - (16 + 4 * qt)
                            if j >= 0:
                                S.op("dve", lambda e: e.tensor_tensor(out=PT.t[:], in0=PT.t[:], in1=cmask.t[:, j * 512:(j + 1) * 512], op=ALU.mult),
                                     reads=[PT, cmask], writes=[PT])

                        def pv(kb, nkb=nkb):
                            PT = PTs[kb % 3]
                            S.op("pe", lambda e: e.matmul(otp.t[:], lhsT=Vh.t[:, kb, :], rhs=PT.t[:], start=(kb == 0), stop=(kb == nkb - 1)),
                                 reads=[Vh, PT], writes=[otp])
                            S.op("pe", lambda e: e.matmul(dnp.t[:], lhsT=ones.t[:], rhs=PT.t[:], start=(kb == 0), stop=(kb == nkb - 1)),
                                 reads=[ones, PT], writes=[dnp])

                        qk(0)
                        for kb in range(nkb):
                            if kb + 1 < nkb:
                                qk(kb + 1)
                            pv(kb)
                        S.op("dve", lambda e: e.reciprocal(out=rden.t[:], in_=dnp.t[:]), reads=[dnp], writes=[rden])
                        S.op("dve", lambda e, OT=OT: e.tensor_tensor(out=OT.t[:], in0=otp.t[:], in1=rden.t[:], op=ALU.mult),
                             reads=[otp, rden], writes=[OT])
                        S.dma("sp", mix_d[h * 128:(h + 1) * 128, qt * 512:(qt + 1) * 512], OT.t[:], reads=[OT], writes=[MIX],
                              sem_buf=OT, kind="r")
                print("phase B", S.flush())

        mid.close()
        finals = []
        if "C" in phases:
            with ExitStack() as pc:
                S.stack = pc
                g_mlp = const("g_mlp", [128, 32], F32, g_mlp_d, False)
                g_ple = const("g_ple", [128, 32], F32, g_ple_d, False)
                gbc = S.sbuf("gbc", [128, 4096], F32)
                g_post = g_fin = gbc
                ring = Ring(S, KN_RINGC, "wrc")
                hacc = [S.sbuf("hacc%d" % s, [128, 4096], F32) for s in range(4)]
                actT = S.sbuf("actT", [128, 32, 512], BF16)
                aTp = [Buf("aTp%d" % i) for i in range(4)]
                hids = [S.sbuf("hid%d" % i, [128, 8, 512], BF16) for i in range(2)]
                yb = S.sbuf("ybC", [128, 4096], F32)
                wpe = S.sbuf("wpe", [128, 8192], BF16)
                for t in range(2):
                    S.dma("pool", wpe.t[:, t * 4096:(t + 1) * 4096].rearrange("p (a b) -> p a b", b=2048),
                          w_pe_t[t].rearrange("p (a b) -> p a b", b=2048), writes=[wpe], sem_buf=wpe)
                tmpr = [S.sbuf("tmpr%d" % i, [128, 512], F32) for i in range(2)]
                tmpg = S.sbuf("tmpg", [128, 512], F32)
                tmpe = S.sbuf("tmpe", [128, 512], F32)
                pf = S.sbuf("pf", [128, 256], F32)
                pbf = S.sbuf("pbf", [128, 256], BF16)
                pT = S.sbuf("pT", [128, 2, 512], BF16)
                ss = S.sbuf("ssC", [128, 1], F32)
                rstd = S.sbuf("rstdC", [128, 1], F32)
                sse = [S.sbuf("sse%d" % s, [128, 8], F32) for s in range(4)]
                rse = [S.sbuf("rse%d" % s, [128, 1], F32) for s in range(4)]
                tm = [S.psum("tmc%d" % i, [128, 512], F32) for i in range(4)]
                fm = [S.psum("fmc%d" % i, [128, 512], F32) for i in range(2)]
                tpt = S.psum("tpc", [128, 512], F32)
                tptb = S.psum("tpcb", [128, 512], F32)

                class _V2:
                    def __init__(self, t, off):
                        self.t_, self.off = t, off
                    def __getitem__(self, k):
                        rows, cols = k
                        return self.t_[rows, self.off + cols.start:self.off + cols.stop]
                tps = [Buf("tpa"), Buf("tpb")]
                tps[0].excl = tps[1].excl = True
                tps[0].t = _V2(tpt.t, 0)
                tps[1].t = _V2(tptb.t, 0)
                ones_c = const("ones_c", [128, 128], BF16, ones_d, False)

                def gemm_tm_c(tiles, evac):
                    for q in range(4):
                        sl = ring.load(tiles[q])
                        for s in range(4):
                            for kk in range(8):
                                kc = q * 8 + kk
                                S.op("pe", lambda e, s=s, kc=kc, kk=kk, sl=sl: e.matmul(
                                    tm[s].t[:], lhsT=actT.t[:, kc, s * 128:(s + 1) * 128], rhs=sl.t[:, kk * 512:(kk + 1) * 512],
                                    start=(kc == 0), stop=(kc == 31)), reads=[aTp[q], sl], writes=[tm[s]])
                    for s in range(4):
                        evac(s)

                def add_into(s, g, src_buf, src_ap):
                    S.op("dve", lambda e: e.tensor_tensor(out=hacc[s].t[:, g * 512:(g + 1) * 512], in0=hacc[s].t[:, g * 512:(g + 1) * 512],
                                                          in1=src_ap, op=ALU.add), reads=[hacc[s], src_buf], writes=[hacc[s]])

                for it in range(NTILE_OWN):
                    tok0 = it * 512
                    for s in range(4):
                        r0 = 2048 + tok0 + s * 128
                        S.dma("sp", hacc[s].t[:], xs[r0:r0 + 128, :], writes=[hacc[s]], sem_buf=hacc[s])
                    mixv = mix_d.rearrange("(c p) t -> p c t", p=128)

                    def load_mix(tk):
                        for j in range(4):
                            S.dma("sp", actT.t[:, 8 * j:8 * j + 8, :], mixv[:, 8 * j:8 * j + 8, tk:tk + 512], reads=[MIX], writes=[aTp[j]],
                                  sem_buf=aTp[j])
                    if it == 0:
                        load_mix(0)
                    for g in range(8):
                        gemm_tm_c([w_o_tm[g * 4 + q] for q in range(4)], lambda s, g=g: add_into(s, g, tm[s], tm[s].t[:]))
                    for s in range(4):
                        norm_rows(hacc[s], hacc[s].t[:], 4096, yb, ss, rstd)
                        transpose_out(yb, 32, tps, lambda c, s=s: actT.t[:, c, s * 128:(s + 1) * 128], g_mlp, lambda c: [aTp[c // 8]])
                    def w_up_block(fb):
                        hid = hids[fb % 2]
                        for fc in range(8):
                            sl = ring.load(w_up_fm[fb * 8 + fc])
                            f = fm[fc % 2]
                            for kc in range(32):
                                S.op("pe", lambda e, kc=kc, f=f, sl=sl: e.matmul(f.t[:], lhsT=sl.t[:, kc * 128:(kc + 1) * 128], rhs=actT.t[:, kc, :],
                                                                                start=(kc == 0), stop=(kc == 31)), reads=[aTp[kc // 8], sl], writes=[f])
                            tr = tmpr[fc % 2]
                            S.op("dve", lambda e, f=f, tr=tr: e.tensor_scalar(out=tr.t[:], in0=f.t[:], scalar1=0.0, scalar2=None, op0=ALU.max),
                                 reads=[f], writes=[tr])
                            S.op("act", lambda e, fc=fc, tr=tr, hid=hid: e.activation(out=hid.t[:, fc, :], in_=tr.t[:], func=AF.Square), reads=[tr], writes=[hid])

                    def w_down_block(fb):
                        hid = hids[fb % 2]
                        for g in range(8):
                            sl = ring.load(w_dn_tm[fb * 8 + g])
                            for s in range(4):
                                for kk in range(8):
                                    S.op("pe", lambda e, s=s, kk=kk, sl=sl, hid=hid: e.matmul(tm[s].t[:], lhsT=hid.t[:, kk, s * 128:(s + 1) * 128],
                                                                                             rhs=sl.t[:, kk * 512:(kk + 1) * 512], start=(kk == 0), stop=(kk == 7)),
                                         reads=[hid, sl], writes=[tm[s]])
                                add_into(s, g, tm[s], tm[s].t[:])

                    w_up_block(0)
                    for fb in range(16):
                        if fb + 1 < 16:
                            w_up_block(fb + 1)
                        w_down_block(fb)
                    S.dma("sp", gbc.t[:], g_post_d, writes=[gbc], sem_buf=gbc)
                    for s in range(4):
                        norm_rows(hacc[s], hacc[s].t[:], 4096, yb, ss, rstd)
                        transpose_out(yb, 32, tps, lambda c, s=s: actT.t[:, c, s * 128:(s + 1) * 128], g_ple, lambda c: [aTp[c // 8]])
                        r0 = tok0 + s * 128
                        S.dma("sp", pf.t[:], pp[r0:r0 + 128, :], writes=[pf], sem_buf=pf)
                        for kk in range(2):
                            S.op("pe", lambda e, kk=kk: e.transpose(tps[0].t[:, kk * 128:(kk + 1) * 128], pf.t[:, kk * 128:(kk + 1) * 128], ident.t[:]),
                                 reads=[pf, ident], writes=[tps[0]])
                        S.op("act", lambda e, s=s: e.copy(out=pT.t[:, :, s * 128:(s + 1) * 128],
                                                          in_=tps[0].t[:, 0:256].rearrange("p (a b) -> p a b", b=128)), reads=[tps[0]], writes=[pT])

                    def emm(s, g, f):
                        for kk in range(2):
                            S.op("pe", lambda e, kk=kk: e.matmul(f.t[:], lhsT=pT.t[:, kk, s * 128:(s + 1) * 128],
                                                                 rhs=wpe.t[:, g * 1024 + kk * 512: g * 1024 + (kk + 1) * 512], start=(kk == 0), stop=(kk == 1)),
                                 reads=[pT, wpe], writes=[f])
                    for s in range(4):
                        S.op("dve", lambda e, s=s: e.memset(sse[s].t[:], 0.0), writes=[sse[s]])
                        for g in range(8):
                            f = fm[g % 2]
                            emm(s, g, f)
                            S.op("act", lambda e, s=s, g=g, f=f: e.activation(out=tmpg.t[:], in_=f.t[:], func=AF.Square, accum_out=sse[s].t[:, g:g + 1]),
                                 reads=[f, sse[s]], writes=[tmpg, sse[s]])
                        S.op("dve", lambda e, s=s: e.tensor_reduce(out=rse[s].t[:], in_=sse[s].t[:], axis=AX.X, op=ALU.add), reads=[sse[s]], writes=[rse[s]])
                        rstd_from_ss(rse[s], rse[s], 4096)

                    for g in range(8):
                        def evg(s, g=g):
                            f = fm[s % 2]
                            emm(s, g, f)
                            S.op("act", lambda e: e.activation(out=tmpg.t[:], in_=tm[s].t[:], func=AF.Sigmoid), reads=[tm[s]], writes=[tmpg])
                            S.op("dve", lambda e: e.scalar_tensor_tensor(out=tmpe.t[:], in0=f.t[:], scalar=rse[s].t[:], in1=g_post.t[:, g * 512:(g + 1) * 512],
                                                                         op0=ALU.mult, op1=ALU.mult), reads=[f, rse[s], g_post], writes=[tmpe])
                            S.op("dve", lambda e: e.tensor_tensor(out=tmpe.t[:], in0=tmpe.t[:], in1=tmpg.t[:], op=ALU.mult), reads=[tmpe, tmpg], writes=[tmpe])
                            add_into(s, g, tmpe, tmpe.t[:])
                        gemm_tm_c([w_pg_tm[g * 4 + q] for q in range(4)], evg)
                    if it + 1 < NTILE_OWN:
                        load_mix(tok0 + 512)
                    S.dma("sp", gbc.t[:], g_fin_d, writes=[gbc], sem_buf=gbc)
                    for s in range(4):
                        S.op("dve", lambda e: e.memset(ss.t[:], 0.0), writes=[ss])
                        S.op("act", lambda e, s=s: e.activation(out=yb.t[:], in_=hacc[s].t[:], func=AF.Square, accum_out=ss.t[:]),
                             reads=[hacc[s], ss], writes=[yb, ss])
                        rstd_from_ss(rstd, ss, 4096)
                        S.op("dve", lambda e, s=s: e.scalar_tensor_tensor(out=hacc[s].t[:], in0=hacc[s].t[:], scalar=rstd.t[:], in1=g_fin.t[:],
                                                                          op0=ALU.mult, op1=ALU.mult), reads=[hacc[s], rstd, g_fin], writes=[hacc[s]])
                        r0 = tok0 + s * 128
                        finals.append(S.dma("sp", out_d[r0:r0 + 128, :], hacc[s].t[:], reads=[hacc[s]], sem_buf=hacc[s], kind="r"))
                print("phase C", S.flush(final_waits=finals))
    return nc


def _fm_tiles(w):
    K, N = w.shape
    kc = K // 128
    t = w.reshape(kc, 128, N // 128, 128).transpose(2, 1, 0, 3)
    return np.ascontiguousarray(t).reshape(N // 128, 128, kc * 128)


def _tm_tiles(w, width=512):
    K, N = w.shape
    nq = K // 1024
    t = w.reshape(nq, 8, 128, N // width, width).transpose(3, 0, 2, 1, 4)
    return np.ascontiguousarray(t).reshape((N // width) * nq, 128, 8 * width)


_CACHE = {}


def _prep_shared(inp):
    f32 = np.float32
    w_in = np.asarray(inp["w_in"], f32)[0]
    QL, KVL = 768, 512
    o_kr = QL + KVL
    o_hq = o_kr + 64
    o_hf = o_hq + 2048
    o_hi = o_hf + 2048
    o_hg = o_hi + 2048
    kr = w_in[:, o_kr:o_kr + 64]
    krs = np.concatenate([kr[:, 32:64], kr[:, 0:32]], axis=1)
    fm_cols = [np.concatenate([kr, krs], axis=1)]
    for h in range(16):
        fm_cols.append(w_in[:, o_hq + h * 128: o_hq + (h + 1) * 128])
        fm_cols.append(w_in[:, o_hf + h * 128: o_hf + (h + 1) * 128])
    w_in_fm = _fm_tiles(np.concatenate(fm_cols, axis=1))
    def pad_tiles(t):
        o = np.zeros((t.shape[0], 128, 4096), f32)
        o[:, :, :t.shape[2]] = t
        return o
    tm_list = [_tm_tiles(w_in[:, 0:512]), pad_tiles(_tm_tiles(w_in[:, 512:768], 256)),
               pad_tiles(_tm_tiles(w_in[:, 768:1024], 256)), pad_tiles(_tm_tiles(w_in[:, 1024:1280], 256))]
    w_in_tm = np.concatenate(tm_list, axis=0)
    w_in_tmh = np.concatenate([_tm_tiles(np.concatenate([w_in[:, o_hi + h * 128: o_hi + (h + 1) * 128],
                                                         w_in[:, o_hg + h * 128: o_hg + (h + 1) * 128]], axis=1), 256)
                               for h in range(16)], axis=0)
    w_in_tmp = np.concatenate([_tm_tiles(w_in[:, o_hi + h * 128: o_hi + (h + 1) * 128], 128) for h in range(16)], axis=0)
    w_uq = np.asarray(inp["w_uq"], f32)[0]
    w_ukv = np.asarray(inp["w_ukv"], f32)[0]
    w_attn = np.zeros((16, 128, 4096), f32)
    for h in range(16):
        qn = w_uq[:, h * 192: h * 192 + 128]
        qr = w_uq[:, h * 192 + 128: h * 192 + 192]
        qrs = np.concatenate([qr[:, 32:64], qr[:, 0:32]], axis=1)
        w_attn[h, :, 0:768] = _fm_tiles(qn)[0]
        w_attn[h, :, 768:1536] = _fm_tiles(np.concatenate([qr, qrs], axis=1))[0]
        uk = w_ukv[:, h * 256: h * 256 + 128]
        uv = w_ukv[:, h * 256 + 128: h * 256 + 256]
        w_attn[h, :, 1536:2048] = _fm_tiles(uk)[0]
        w_attn[h, :, 2048:2560] = _fm_tiles(uv)[0]
    w_pe = np.asarray(inp["w_ple"], f32)[0]
    t = w_pe.reshape(2, 128, 8, 512).transpose(1, 2, 0, 3)
    w_pe_t = np.ascontiguousarray(t).reshape(128, 2, 4096).transpose(1, 0, 2)
    def col(v, n):
        return np.ascontiguousarray(np.asarray(v, f32).reshape(n, 128).T)
    lbr = np.asarray(inp["hg_lower_bound"], f32)
    k = np.arange(128)
    maskbd = ((k[:, None] // 64 == k[None, :] // 64) & (k[:, None] <= k[None, :])).astype(ml_dtypes.bfloat16)
    t512 = np.arange(512)
    cme = np.broadcast_to(((t512 // 64) % 2 == 0)[None, :], (128, 512)).astype(ml_dtypes.bfloat16)
    cmo = np.broadcast_to(((t512 // 64) % 2 == 1)[None, :], (128, 512)).astype(ml_dtypes.bfloat16)
    rme = np.stack([(k < 64), (k >= 64)], axis=1).astype(f32)
    rmask = np.broadcast_to((t512 % 64 != 0)[None, :], (128, 512)).astype(f32)
    cm = np.zeros((128, 4, 512), ml_dtypes.bfloat16)
    for j in range(4):
        cm[:, j, :] = ((j * 128 + k)[:, None] <= t512[None, :])
    inv = (10000.0 ** (-np.arange(0, 64, 2, dtype=f32) / 64)).astype(f32)
    invf = np.stack([np.concatenate([inv, inv]), np.concatenate([-np.ones(32, f32), np.ones(32, f32)])], axis=1).astype(f32)
    sh = dict(
        w_in_fm=w_in_fm, w_in_tm=w_in_tm, w_in_tmh=w_in_tmh, w_in_tmp=w_in_tmp, w_attn=w_attn,
        w_o_tm=_tm_tiles(np.asarray(inp["w_o"], f32)[0]),
        w_up_fm=_fm_tiles(np.asarray(inp["w_up"], f32)[0]),
        w_dn_tm=None, w_pg_tm=_tm_tiles(np.asarray(inp["w_ple_gate"], f32)[0]),
        w_pe_t=np.ascontiguousarray(w_pe_t),
        g_mix=col(inp["norm_mix"][0], 32), g_mlp=col(inp["norm_mlp"][0], 32), g_ple=col(inp["norm_ple"][0], 32),
        g_qa=col(inp["q_a_norm"][0], 6), g_kva=col(inp["kv_a_norm"][0], 4), g_hg=col(inp["hg_out_norm"][0], 16),
        lbraw=np.ascontiguousarray(np.concatenate([lbr[0].reshape(16, 128).T, lbr[1].reshape(16, 128).T], axis=1)),
        g_post=np.ascontiguousarray(np.broadcast_to(np.asarray(inp["ple_post_norm"], f32)[0][None, :], (128, 4096))),
        g_fin=np.ascontiguousarray(np.broadcast_to(np.asarray(inp["final_norm"], f32)[None, :], (128, 4096))),
        ident=np.eye(128).astype(np.float32), ones=np.ones((128, 128), ml_dtypes.bfloat16),
        maskbd=maskbd, cme=np.ascontiguousarray(cme), cmo=np.ascontiguousarray(cmo), rme=rme, rmask=np.ascontiguousarray(rmask),
        cmask=np.ascontiguousarray(cm.reshape(128, 2048)), invf=invf,
    )
    wd = np.asarray(inp["w_down"], f32)[0]
    t = wd.reshape(16, 8, 128, 8, 512).transpose(0, 3, 2, 1, 4)
    sh["w_dn_tm"] = np.ascontiguousarray(t).reshape(128, 128, 4096)
    return sh


def _per_core(inp, c):
    b, half = c // 2, c % 2
    x = np.asarray(inp["x"], np.float32)
    pos = np.asarray(inp["positions"], np.int32)
    own = slice(half * 2048, half * 2048 + 2048)
    xs = np.zeros((4096, 4096), np.float32)
    ps = np.zeros((4096,), np.int32)
    if half == 1:
        xs[0:2048] = x[b, 0:2048]
        ps[0:2048] = pos[b, 0:2048]
    xs[2048:] = x[b, own]
    ps[2048:] = pos[b, own]
    return dict(
        xs=xs, posr=np.ascontiguousarray(np.broadcast_to(ps[None, :], (64, 4096))),
        pp=np.ascontiguousarray(np.asarray(inp["p"], np.float32)[0, b, own]),
        pbias=np.full((128, 1), 0.0 if half == 1 else -30000.0, np.float32),
    )


def kernel(**inputs):
    sh = _prep_shared(inputs)
    nc = build_program()
    in_maps = []
    for c in range(8):
        m = dict(sh)
        m.update(_per_core(inputs, c))
        in_maps.append(m)
    res = run_bass_kernel_spmd(nc, in_maps, core_ids=list(range(8)))
    out = np.zeros((4, 4096, 4096), np.float32)
    for c in range(8):
        b, half = c // 2, c % 2
        out[b, half * 2048:(half + 1) * 2048] = res.results[c]["out"]
    return out
```

```python
import math
from contextlib import ExitStack
import numpy as np
import ml_dtypes
import concourse.bass as bass
import concourse.mybir as mybir
from concourse.bass_utils import run_bass_kernel_spmd

F32 = mybir.dt.float32
BF16 = mybir.dt.bfloat16
I32 = mybir.dt.int32
AF = mybir.ActivationFunctionType
ALU = mybir.AluOpType
AX = mybir.AxisListType
ENGS = ("pe", "act", "dve", "pool", "sp")
EPS = 1e-6
PI = math.pi


class Buf:
    __slots__ = ("name", "t", "w", "rs", "sem_w", "cnt_w", "sem_r", "cnt_r", "excl")

    def __init__(self, name, t=None):
        self.excl = False
        self.name = name
        self.t = t
        self.w = None
        self.rs = []
        self.sem_w = None
        self.cnt_w = 0
        self.sem_r = None
        self.cnt_r = 0


class Op:
    __slots__ = ("eng", "fn", "deps", "sig", "sigval", "is_dma", "dsem", "dval")

    def __init__(self, eng, fn):
        self.eng = eng
        self.fn = fn
        self.deps = []
        self.sig = False
        self.sigval = 0
        self.is_dma = False
        self.dsem = None
        self.dval = 0


class Sched:
    def __init__(self, nc, stack):
        self.nc = nc
        self.stack = stack
        self.top = stack
        self.streams = {e: [] for e in ENGS}
        self.esem = {e: stack.enter_context(nc.semaphore("es_" + e)) for e in ENGS}
        self.nsem = 0
        self.base = {}
        self.bar = {}
        self.nbuf = 0

    def new_sem(self, name):
        self.nsem += 1
        return self.top.enter_context(self.nc.semaphore("ds%d" % self.nsem))

    def sbuf(self, name, shape, dt, top=False):
        self.nbuf += 1
        st = self.top if top else self.stack
        t = st.enter_context(self.nc.sbuf_tensor("%s_%d" % (name, self.nbuf), list(shape), dt))
        return Buf(name, t)

    def psum(self, name, shape, dt):
        self.nbuf += 1
        t = self.stack.enter_context(self.nc.psum_tensor("%s_%d" % (name, self.nbuf), list(shape), dt))
        b = Buf(name, t)
        b.excl = True
        return b

    def _deps(self, op, reads, writes):
        deps = []
        for r in reads:
            if r.w is not None:
                deps.append(r.w)
            if r.excl:
                deps.extend(p for p in r.rs if p.eng != op.eng)
        for w in writes:
            if w.w is not None:
                deps.append(w.w)
            deps.extend(w.rs)
        op.deps = [d for d in deps if d is not op]
        for r in reads:
            r.rs.append(op)
        for w in writes:
            w.w = op
            w.rs = []

    def op(self, eng, fn, reads=(), writes=()):
        o = Op(eng, fn)
        self._deps(o, reads, writes)
        self.streams[eng].append(o)
        return o

    def dma(self, q, out_ap, in_ap, reads=(), writes=(), sem_buf=None, kind="w"):
        def fn(e):
            return e.dma_start(out=out_ap, in_=in_ap)
        o = Op(q, fn)
        o.is_dma = True
        if kind == "w":
            if sem_buf.sem_w is None:
                sem_buf.sem_w = self.new_sem(sem_buf.name)
            sem_buf.cnt_w += 16
            o.dsem, o.dval = sem_buf.sem_w, sem_buf.cnt_w
        else:
            if sem_buf.sem_r is None:
                sem_buf.sem_r = self.new_sem(sem_buf.name)
            sem_buf.cnt_r += 16
            o.dsem, o.dval = sem_buf.sem_r, sem_buf.cnt_r
        self._deps(o, reads, writes)
        self.streams[q].append(o)
        return o

    def flush(self, final_waits=()):
        r = self.finalize(final_waits)
        pend = []
        for e in ENGS:
            lastc = None
            for o in self.streams[e]:
                if o.is_dma:
                    pend.append(o)
                else:
                    lastc = o
            if lastc is not None:
                pend.append(lastc)
        self.base = {e: sum(1 for o in self.streams[e] if o.sig and not o.is_dma) + self.base.get(e, 0)
                     for e in ENGS}
        self.streams = {e: [] for e in ENGS}
        old = self.bar
        self.bar = {e: list(pend) + list(old.get(e, [])) for e in ENGS}
        return r

    def finalize(self, final_waits=()):
        for e in ENGS:
            if self.bar.get(e) and self.streams[e]:
                o0 = self.streams[e][0]
                o0.deps = list(o0.deps) + [d for d in self.bar[e] if d is not o0]
                self.bar[e] = []
        for e in ENGS:
            for o in self.streams[e]:
                for d in o.deps:
                    if not d.is_dma:
                        if d.eng == "pe" and o.eng == "pe" and not o.is_dma:
                            continue
                        d.sig = True
        for o in final_waits:
            if not o.is_dma:
                o.sig = True
        for e in ENGS:
            for o in reversed(self.streams[e]):
                if not o.is_dma:
                    o.sig = True
                    break
        for e in ENGS:
            c = self.base.get(e, 0)
            for o in self.streams[e]:
                if o.sig and not o.is_dma:
                    c += 1
                    o.sigval = c
        streams = self.streams
        esem = self.esem
        stats = {e: [0, 0] for e in ENGS}

        proto = {e: [] for e in ENGS}
        self.proto = proto

        def emit(e, eng):
            waited = {}
            for o in streams[e]:
                mywaits = []
                need = {}
                for d in o.deps:
                    if d.is_dma:
                        key, val = d.dsem, d.dval
                    else:
                        if d.eng == "pe" and e == "pe" and not o.is_dma:
                            continue
                        key, val = esem[d.eng], d.sigval
                    if val > need.get(key, 0):
                        need[key] = val
                for key, val in need.items():
                    if val > waited.get(key, 0):
                        eng.wait_ge(key, val)
                        waited[key] = val
                        stats[e][1] += 1
                        mywaits.append((id(key), val))
                proto[e].append((mywaits, (id(o.dsem), 16) if o.is_dma else ((id(esem[e]), 1) if o.sig else None)))
                ins = o.fn(eng)
                stats[e][0] += 1
                if o.is_dma:
                    ins.then_inc(o.dsem, 16)
                elif o.sig:
                    ins.then_inc(esem[e], 1)
            if e == "sp":
                for o in final_waits:
                    if o.is_dma:
                        eng.wait_ge(o.dsem, o.dval)
                    else:
                        eng.wait_ge(esem[o.eng], o.sigval)

        with self.nc.Block() as block:
            @block.tensor
            def _(eng):
                emit("pe", eng)

            @block.scalar
            def _(eng):
                emit("act", eng)

            @block.vector
            def _(eng):
                emit("dve", eng)

            @block.gpsimd
            def _(eng):
                emit("pool", eng)

            @block.sync
            def _(eng):
                emit("sp", eng)
        if not hasattr(self, "semval"):
            self.semval = {}
        semval = self.semval
        ptr = {e: 0 for e in ENGS}
        progress = True
        while progress:
            progress = False
            for e in ENGS:
                while ptr[e] < len(proto[e]):
                    waits, inc = proto[e][ptr[e]]
                    if all(semval.get(k, 0) >= v for k, v in waits):
                        if inc is not None:
                            semval[inc[0]] = semval.get(inc[0], 0) + inc[1]
                        ptr[e] += 1
                        progress = True
                    else:
                        break
        stuck = {e: (ptr[e], len(proto[e])) for e in ENGS if ptr[e] < len(proto[e])}
        if stuck:
            print("DEADLOCK in emitted protocol:", stuck)
            for e in stuck:
                waits, inc = proto[e][ptr[e]]
                print("  ", e, "waiting", [(k, v, semval.get(k, 0)) for k, v in waits])
        return stats


class Ring:
    def __init__(self, S, n, name):
        self.S = S
        self.slots = [S.sbuf("%s%d" % (name, i), [128, 4096], BF16) for i in range(n)]
        self.i = 0

    def load(self, src, width=4096):
        sl = self.slots[self.i % len(self.slots)]
        self.i += 1
        b = min(width, 2048)
        self.S.dma("pool", sl.t[:, 0:width].rearrange("p (a b) -> p a b", b=b),
                   src.rearrange("p (a b) -> p a b", b=b), writes=[sl], sem_buf=sl)
        return sl


LAST_DRAM = []
DBG_COPY = False
SMALL = ()
import os
KN_SUB = int(os.environ.get('KN_SUB', '4'))
KN_EVAC = os.environ.get('KN_EVAC', 'mix')
KN_TPSEP = int(os.environ.get('KN_TPSEP', '0'))
KN_NCH = int(os.environ.get('KN_NCH', '32'))
KN_H = int(os.environ.get('KN_H', '9'))
KN_LEAD = int(os.environ.get('KN_LEAD', '6'))
KN_RING = int(os.environ.get('KN_RING', '3'))
KN_ESTEP = int(os.environ.get('KN_ESTEP', '1'))
KN_RINGC = int(os.environ.get('KN_RINGC', '4'))
KN_CQ = int(os.environ.get('KN_CQ', '2048'))
NTILE_PRE = 4
NTILE_OWN = 4
QSCALE = 192 ** -0.5


def build_program(phases="ABC", alim=0, tiles=None):
    nc = bass.Bass("TRN2", target_bir_lowering=False)

    LAST_DRAM.clear()

    def dram(name, shape, dt, kind="ExternalInput"):
        if SMALL and name in SMALL:
            shape = [1] + list(shape[1:])
        if kind == "ExternalInput":
            LAST_DRAM.append((name, tuple(shape), "bf16" if dt == BF16 else ("i32" if dt == I32 else "f32")))
        return nc.dram_tensor(name, list(shape), dt, kind=kind).ap()

    xs = dram("xs", [4096, 4096], F32)
    posr = dram("posr", [64, 4096], I32)
    pp = dram("pp", [2048, 256], F32)
    pbias_d = dram("pbias", [128, 1], F32)
    w_in_fm = dram("w_in_fm", [33, 128, 4096], F32)
    w_in_tm = dram("w_in_tm", [16, 128, 4096], F32)
    w_in_tmh = dram("w_in_tmh", [64, 128, 2048], F32)
    w_in_tmp = dram("w_in_tmp", [64, 128, 1024], F32)
    w_attn = dram("w_attn", [16, 128, 4096], F32)
    w_o_tm = dram("w_o_tm", [32, 128, 4096], F32)
    w_up_fm = dram("w_up_fm", [128, 128, 4096], F32)
    w_dn_tm = dram("w_dn_tm", [128, 128, 4096], F32)
    w_pg_tm = dram("w_pg_tm", [32, 128, 4096], F32)
    w_pe_t = dram("w_pe_t", [2, 128, 4096], F32)
    g_mix_d = dram("g_mix", [128, 32], F32)
    g_mlp_d = dram("g_mlp", [128, 32], F32)
    g_ple_d = dram("g_ple", [128, 32], F32)
    g_qa_d = dram("g_qa", [128, 6], F32)
    g_kva_d = dram("g_kva", [128, 4], F32)
    g_hg_d = dram("g_hg", [128, 16], F32)
    lbraw_d = dram("lbraw", [128, 32], F32)
    g_post_d = dram("g_post", [128, 4096], F32)
    g_fin_d = dram("g_fin", [128, 4096], F32)
    ident_d = dram("ident", [128, 128], BF16)
    ones_d = dram("ones", [128, 128], BF16)
    maskbd_d = dram("maskbd", [128, 128], BF16)
    cme_d = dram("cme", [128, 512], BF16)
    cmo_d = dram("cmo", [128, 512], BF16)
    rme_d = dram("rme", [128, 2], F32)
    rmask_d = dram("rmask", [128, 512], F32)
    cmask_d = dram("cmask", [128, 2048], BF16)
    invf_d = dram("invf", [64, 2], F32)
    out_d = dram("out", [2048, 4096], F32, kind="ExternalOutput")
    mix_d = dram("mixT", [4096, 2048], BF16, kind="Internal")

    with ExitStack() as top:
        S = Sched(nc, top)
        MIX = Buf("MIX")

        def const(name, shape, dt, src, top_=True):
            b = S.sbuf(name, shape, dt, top=top_)
            S.dma("sp", b.t[:], src, writes=[b], sem_buf=b)
            return b

        ident = const("ident", [128, 128], BF16, ident_d)
        invf = const("invf", [64, 2], F32, invf_d)
        mid = ExitStack()
        S.stack = mid
        ckvnT = S.sbuf("ckvnT", [128, 4, 4096], BF16)
        krT = S.sbuf("krT", [64, 4096], BF16)
        cqnT = S.sbuf("cqnT", [128, 6, KN_CQ], BF16)

        def rstd_from_ss(rstd, ss, dim):
            S.op("dve", lambda e: e.tensor_scalar(out=rstd.t[:], in0=ss.t[:], scalar1=1.0 / dim, scalar2=EPS,
                                                  op0=ALU.mult, op1=ALU.add), reads=[ss], writes=[rstd])
            S.op("act", lambda e: e.activation(out=rstd.t[:], in_=rstd.t[:], func=AF.Sqrt), reads=[rstd], writes=[rstd])
            S.op("dve", lambda e: e.reciprocal(out=rstd.t[:], in_=rstd.t[:]), reads=[rstd], writes=[rstd])

        cnt = [0]

        def evac_scaled(out_ap, in_ap, scale_ap, reads, writes):
            cnt[0] += 1
            if KN_EVAC == "act":
                S.op("act", lambda e: e.copy(out=out_ap, in_=in_ap), reads=reads, writes=writes)
            elif DBG_COPY:
                S.op("dve", lambda e: e.tensor_copy(out=out_ap, in_=in_ap), reads=reads, writes=writes)
            elif KN_EVAC == "mix" and cnt[0] % 2 == 0:
                S.op("act", lambda e: e.activation(out=out_ap, in_=in_ap, func=AF.Identity, scale=scale_ap),
                     reads=reads, writes=writes)
            else:
                S.op("dve", lambda e: e.tensor_scalar(out=out_ap, in0=in_ap, scalar1=scale_ap, scalar2=None,
                                                      op0=ALU.mult), reads=reads, writes=writes)

        def copy_any(out_ap, in_ap, reads, writes):
            cnt[0] += 1
            if cnt[0] % 2:
                S.op("act", lambda e: e.copy(out=out_ap, in_=in_ap), reads=reads, writes=writes)
            else:
                S.op("dve", lambda e: e.tensor_copy(out=out_ap, in_=in_ap), reads=reads, writes=writes)

        def norm_rows(src, src_ap, width, yb, ss, rstd):
            S.op("dve", lambda e: e.memset(ss.t[:], 0.0), writes=[ss])
            S.op("act", lambda e: e.activation(out=yb.t[:, 0:width], in_=src_ap, func=AF.Square, accum_out=ss.t[:]),
                 reads=[src, ss], writes=[yb, ss])
            rstd_from_ss(rstd, ss, width)
            S.op("dve", lambda e: e.tensor_scalar(out=yb.t[:, 0:width], in0=src_ap, scalar1=rstd.t[:], scalar2=None,
                                                  op0=ALU.mult), reads=[src, rstd], writes=[yb])

        def transpose_out(yb, nchunk, tps, dst_fn, gain, dst_bufs, noevac=False):
            for c0 in range(0, nchunk, 8):
                tp = tps[(c0 // 8) % 2]
                n = min(8, nchunk - c0)
                for j in range(n):
                    c = c0 + j
                    S.op("pe", lambda e, c=c, j=j, tp=tp: e.transpose(tp.t[:, j * 128:(j + 1) * 128],
                                                                      yb.t[:, c * 128:(c + 1) * 128], ident.t[:]),
                         reads=[yb, ident], writes=[tp])
                for j in range(n):
                    if noevac:
                        break
                    c = c0 + j
                    evac_scaled(dst_fn(c), tp.t[:, j * 128:(j + 1) * 128], gain.t[:, c:c + 1],
                                reads=[tp, gain], writes=dst_bufs(c))

        def rope_gen(src, posi_b, posi_ap, ang_b, ang_ap, tcs_b, tcs_ap, tsn_b, tsn_ap, cosb, cos_ap, sinb, sin_ap):
            S.dma("sp", posi_ap, src, writes=[posi_b], sem_buf=posi_b)
            S.op("dve", lambda e: e.tensor_copy(out=ang_ap, in_=posi_ap), reads=[posi_b], writes=[ang_b])
            S.op("dve", lambda e: e.tensor_scalar(out=ang_ap, in0=ang_ap, scalar1=invf.t[:, 0:1], scalar2=None,
                                                  op0=ALU.mult), reads=[ang_b, invf], writes=[ang_b])
            for which in (0, 1):
                shift = 0.5 * PI if which == 0 else 0.0
                S.op("dve", lambda e, shift=shift: e.tensor_scalar(out=tcs_ap, in0=ang_ap, scalar1=shift, scalar2=None, op0=ALU.add),
                     reads=[ang_b], writes=[tcs_b])
                S.op("dve", lambda e: e.tensor_scalar(out=tsn_ap, in0=tcs_ap, scalar1=1.0 / (2 * PI), scalar2=None, op0=ALU.mult),
                     reads=[tcs_b], writes=[tsn_b])
                S.op("dve", lambda e: e.tensor_copy(out=posi_ap, in_=tsn_ap), reads=[tsn_b], writes=[posi_b])
                S.op("dve", lambda e: e.tensor_copy(out=tsn_ap, in_=posi_ap), reads=[posi_b], writes=[tsn_b])
                S.op("dve", lambda e: e.scalar_tensor_tensor(out=tcs_ap, in0=tsn_ap, scalar=-2 * PI, in1=tcs_ap, op0=ALU.mult, op1=ALU.add),
                     reads=[tsn_b, tcs_b], writes=[tcs_b])
                S.op("dve", lambda e: e.tensor_scalar(out=tsn_ap, in0=tcs_ap, scalar1=PI, scalar2=-2 * PI, op0=ALU.is_gt, op1=ALU.mult),
                     reads=[tcs_b], writes=[tsn_b])
                S.op("dve", lambda e: e.tensor_tensor(out=tcs_ap, in0=tcs_ap, in1=tsn_ap, op=ALU.add), reads=[tcs_b, tsn_b], writes=[tcs_b])
                S.op("dve", lambda e: e.tensor_scalar(out=tcs_ap, in0=tcs_ap, scalar1=-PI, scalar2=PI, op0=ALU.max, op1=ALU.min),
                     reads=[tcs_b], writes=[tcs_b])
                if which == 0:
                    S.op("act", lambda e: e.activation(out=cos_ap, in_=tcs_ap, func=AF.Sin), reads=[tcs_b], writes=[cosb])
                else:
                    S.op("act", lambda e: e.activation(out=sin_ap, in_=tcs_ap, func=AF.Sin, scale=invf.t[:, 1:2]),
                         reads=[tcs_b, invf], writes=[sinb])

        if "A" in phases:
            with ExitStack() as pa:
                S.stack = pa
                g_mix = const("g_mix", [128, 32], F32, g_mix_d, False)
                g_qa = const("g_qa", [128, 6], F32, g_qa_d, False)
                g_kva = const("g_kva", [128, 4], F32, g_kva_d, False)
                g_hg = const("g_hg", [128, 16], F32, g_hg_d, False)
                lbraw = const("lbraw", [128, 32], F32, lbraw_d, False)
                maskbd = const("maskbd", [128, 128], BF16, maskbd_d, False)
                cme = const("cme", [128, 512], BF16, cme_d, False)
                cmo = const("cmo", [128, 512], BF16, cmo_d, False)
                rme = const("rme", [128, 2], F32, rme_d, False)
                rmask = const("rmask", [128, 512], F32, rmask_d, False)
                ring = Ring(S, KN_RING, "wra")
                xsb = S.sbuf("xsb", [128, 4096], F32)
                ybs = [S.sbuf("yb%d" % i, [128, 4096], BF16) for i in range(2)]
                yb = ybs[0]
                sss = [S.sbuf("ssA%d" % i, [128, 1], F32) for i in range(2)]
                rstds = [S.sbuf("rstdA%d" % i, [128, 1], F32) for i in range(2)]
                uT = S.sbuf("uT", [128, 32, 512], BF16)
                uTp = [Buf("uTp%d" % i) for i in range(4)]
                class _CC:
                    def __init__(self, ap):
                        self.ap = ap
                    def __getitem__(self, k):
                        return self.ap[k]
                cc = [xsb, xsb, xsb, yb]
                ccv = [_CC(xsb.t[:, 0:1280]), _CC(xsb.t[:, 1280:2560]), _CC(xsb.t[:, 2560:3840]), _CC(yb.t[:, 0:2560].bitcast(F32))]
                ss = S.sbuf("ss", [128, 1], F32)
                rstd = S.sbuf("rstd", [128, 1], F32)
                tm = [S.psum("tm%d" % i, [128, 512], F32) for i in range(4)]
                fm = [S.psum("fm%d" % i, [128, 512], F32) for i in range(2)]
                tpt = S.psum("tp", [128, 1024], BF16)
                tpt2 = S.psum("tp2", [128, 1024], BF16)
                tps = [Buf("tpa", None), Buf("tpb", None)]
                tps[0].excl = tps[1].excl = True
                class _V:
                    def __init__(self, t, off):
                        self.t_, self.off = t, off
                    def __getitem__(self, k):
                        rows, cols = k
                        return self.t_[rows, self.off + cols.start:self.off + cols.stop]
                tps[0].t = _V(tpt.t, 0)
                tps[1].t = _V(tpt2.t, 0)
                smA = smS = tps[0]
                smO = smT1 = smT2 = tps[1]
                apA = tpt.t[:, 0:256].bitcast(F32)
                apS = tpt.t[:, 256:512].bitcast(F32)
                apO = tpt2.t[:, 0:256].bitcast(F32)
                apT1 = tpt2.t[:, 256:384]
                apT2 = tpt2.t[:, 384:512]
                lb = S.sbuf("lb", [128, 16], F32)
                oml = S.sbuf("oml", [128, 16], F32)
                S.op("dve", lambda e: e.tensor_tensor(out=lb.t[:], in0=lbraw.t[:, 0:16], in1=lbraw.t[:, 16:32],
                                                      op=ALU.subtract), reads=[lbraw], writes=[lb])
                S.op("act", lambda e: e.activation(out=lb.t[:], in_=lb.t[:], func=AF.Sigmoid), reads=[lb], writes=[lb])
                S.op("dve", lambda e: e.tensor_scalar(out=oml.t[:], in0=lb.t[:], scalar1=-1.0, scalar2=1.0,
                                                      op0=ALU.mult, op1=ALU.add), reads=[lb], writes=[oml])
                Sst = [S.sbuf("S%d" % h, [128, 128], F32) for h in range(16)]
                Seb2 = [S.sbuf("Seb%d" % h, [128, 128], BF16) for h in range(2)]
                Sob = S.sbuf("Sob", [128, 128], BF16)
                for h in range(16):
                    S.op("dve", lambda e, h=h: e.memset(Sst[h].t[:], 0.0), writes=[Sst[h]])
                def f32t(n):
                    return S.sbuf(n, [128, 512], F32)
                def b16t(n):
                    return S.sbuf(n, [128, 512], BF16)
                escr = S.sbuf("escr", [128, 3072], F32)
                t_sg, t_lf, t_bb, t_qs, t_ex, t_kk = [Buf(n_, escr.t[:, i_ * 512:(i_ + 1) * 512]) for i_, n_ in enumerate(("sg", "lf", "bb", "qs", "ex", "kk"))]
                t_Am = b16t("Am")
                t_qe2 = [b16t("qe%d" % i) for i in range(2)]
                t_qee2 = [b16t("qee%d" % i) for i in range(2)]
                t_qeo2 = [b16t("qeo%d" % i) for i in range(2)]
                t_ke2 = [b16t("ke%d" % i) for i in range(2)]
                t_ke32 = [b16t("ke3%d" % i) for i in range(2)]
                elast2 = [S.sbuf("elast%d" % i, [128, 8], F32) for i in range(2)]
                Vv = [S.sbuf("Vv%d" % i, [128, 4, 128], BF16) for i in range(2)]
                Ve = [S.sbuf("Ve%d" % i, [128, 4, 128], BF16) for i in range(2)]
                Vo = [S.sbuf("Vo%d" % i, [128, 4, 128], BF16) for i in range(2)]
                gs = [S.sbuf("gs%d" % i, [128, 4, 128], F32) for i in range(2)]
                onesf = S.sbuf("onesf", [128, 1], F32)
                S.op("dve", lambda e: e.memset(onesf.t[:], 1.0), writes=[onesf])
                ke3T4 = S.sbuf("ke3T4", [128, 512], BF16)
                ke3T = S.sbuf("ke3T", [128, 128], BF16)
                onb = S.sbuf("onb", [128, 128], BF16)
                oTs = [S.sbuf("oT%d" % i, [128, 512], BF16) for i in range(2)]
                sso = S.sbuf("sso", [128, 1], F32)
                epst = S.sbuf("epst", [128, 1], F32)
                S.op("dve", lambda e: e.memset(epst.t[:], EPS), writes=[epst])
                rso = S.sbuf("rso", [128, 1], F32)
                junk = S.sbuf("junk", [128, 128], F32)
                class _T:
                    def __init__(self, ap):
                        self.ap = ap
                    def __getitem__(self, k):
                        return self.ap[k]
                def _alias(b, dt=None):
                    nb_ = Buf(b.name)
                    nb_.__class__ = Buf
                    return b
                posi_b, ang_b, tcs_b, tsn_b, cosk, sink = t_kk, t_sg, t_lf, t_bb, t_qs, t_ex
                posi_ap = t_kk.t[0:64, :].bitcast(I32)
                ang_ap, tcs_ap, tsn_ap = t_sg.t[0:64, :], t_lf.t[0:64, :], t_bb.t[0:64, :]
                cosk_ap, sink_ap = t_qs.t[0:64, :], t_ex.t[0:64, :]
                ybc = Buf("ybc", escr.t[:, 0:640].bitcast(BF16))

                def rope_tables(tokbase, cos_ap, sin_ap, cosb, sinb):
                    rope_gen(posr[:, tokbase:tokbase + 512], posi_b, posi_ap, ang_b, ang_ap, tcs_b, tcs_ap, tsn_b, tsn_ap,
                             cosb, cos_ap, sinb, sin_ap)

                def gemm_tm(tiles, width, evac):
                    for q in range(4):
                        sl = ring.load(tiles[q], 8 * width)
                        for s in range(4):
                            for kk in range(8):
                                kc = q * 8 + kk
                                S.op("pe", lambda e, s=s, kc=kc, kk=kk, sl=sl, q=q: e.matmul(
                                    tm[s].t[:, 0:width], lhsT=uT.t[:, kc, s * 128:(s + 1) * 128],
                                    rhs=sl.t[:, kk * width:(kk + 1) * width],
                                    start=(kc == 0), stop=(kc == 31)), reads=[uTp[q], sl], writes=[tm[s]])
                    for s in range(4):
                        evac(s)

                def gemm_fm(tile, outs):
                    sl = ring.load(tile)
                    for (pb, pap, c0, ncol) in outs:
                        for kc in range(32):
                            S.op("pe", lambda e, kc=kc, pap=pap, c0=c0, ncol=ncol, sl=sl: e.matmul(
                                pap, lhsT=sl.t[:, kc * 128 + c0: kc * 128 + c0 + ncol], rhs=uT.t[:, kc, :],
                                start=(kc == 0), stop=(kc == 31)), reads=[uTp[kc // 8], sl], writes=[pb])

                for T in (tiles if tiles is not None else range(NTILE_PRE + NTILE_OWN)):
                    own = T >= NTILE_PRE
                    tok0 = T * 512
                    otok0 = (T - NTILE_PRE) * 512
                    for s in range(KN_SUB):
                        r0 = tok0 + s * 128
                        S.dma("sp", xsb.t[:], xs[r0:r0 + 128, :], writes=[xsb], sem_buf=xsb)
                        yb = ybs[s % 2]
                        norm_rows(xsb, xsb.t[:], 4096, yb, sss[s % 2], rstds[s % 2])
                        if alim == 11:
                            continue
                        transpose_out(yb, KN_NCH, tps, lambda c, s=s: uT.t[:, c, s * 128:(s + 1) * 128], g_mix,
                                      (lambda c: []) if alim == 13 else (lambda c: [uTp[c // 8]]), noevac=(alim == 12))
                    if alim in (1, 11, 12, 13):
                        continue
                    groups = [(2, 1024, 256)] if not own else [(0, 0, 512), (1, 512, 256), (2, 768 + 256, 256)]
                    glist = ([0, 1] if own else []) + [2, 3]
                    for g in glist:
                        coff = {0: 0, 1: 512, 2: 768, 3: 1024}[g]
                        width = 512 if g == 0 else 256
                        def ev(s, coff=coff, width=width):
                            copy_any(ccv[s][:, coff:coff + width], tm[s].t[:, 0:width], [tm[s]], [cc[s]])
                        gemm_tm([w_in_tm[g * 4 + q][:, 0:8 * width] for q in range(4)], width, ev)
                    for s in range(4):
                        if own:
                            norm_rows(cc[s], ccv[s][:, 0:768], 768, ybc, ss, rstd)
                            transpose_out(ybc, 6, tps,
                                          lambda c, s=s: cqnT.t[:, c, otok0 + s * 128: otok0 + (s + 1) * 128], g_qa,
                                          lambda c: [cqnT])
                        S.op("dve", lambda e: e.memset(ss.t[:], 0.0), writes=[ss])
                        S.op("act", lambda e, s=s: e.activation(out=ybc.t[:, 0:512], in_=ccv[s][:, 768:1280], func=AF.Square,
                                                                accum_out=ss.t[:]), reads=[cc[s], ss], writes=[ybc, ss])
                        rstd_from_ss(rstd, ss, 512)
                        S.op("dve", lambda e, s=s: e.tensor_scalar(out=ybc.t[:, 0:512], in0=ccv[s][:, 768:1280], scalar1=rstd.t[:],
                                                                   scalar2=None, op0=ALU.mult), reads=[cc[s], rstd], writes=[ybc])
                        transpose_out(ybc, 4, tps,
                                      lambda c, s=s: ckvnT.t[:, c, tok0 + s * 128: tok0 + (s + 1) * 128], g_kva,
                                      lambda c: [ckvnT])
                    if alim == 2:
                        continue
                    gemm_fm(w_in_fm[0], [(fm[0], fm[0].t[0:64, :], 0, 64), (fm[1], fm[1].t[0:64, :], 64, 64)])
                    rope_tables(tok0, cosk_ap, sink_ap, cosk, sink)
                    S.op("dve", lambda e: e.tensor_tensor(out=tcs_ap, in0=fm[0].t[0:64, :], in1=cosk_ap, op=ALU.mult),
                         reads=[fm[0], cosk], writes=[tcs_b])
                    S.op("dve", lambda e: e.tensor_tensor(out=tsn_ap, in0=fm[1].t[0:64, :], in1=sink_ap, op=ALU.mult),
                         reads=[fm[1], sink], writes=[tsn_b])
                    S.op("dve", lambda e, tok0=tok0: e.tensor_tensor(out=krT.t[:, tok0:tok0 + 512], in0=tcs_ap, in1=tsn_ap,
                                                                     op=ALU.add), reads=[tcs_b, tsn_b], writes=[krT])
                    if alim == 3:
                        continue
                    nh = 16 if alim == 0 else 2

                    def G_gen(h, own=own):
                        vs = h % 2
                        fms = ([(1 + 2 * h, fm[0])] if own else []) + [(2 + 2 * h, fm[1])]
                        for (ti, fb_) in fms:
                            sl = ring.load(w_in_fm[ti])
                            for kc in range(32):
                                S.op("pe", lambda e, kc=kc, fb_=fb_, sl=sl: e.matmul(
                                    fb_.t[:], lhsT=sl.t[:, kc * 128:(kc + 1) * 128], rhs=uT.t[:, kc, :],
                                    start=(kc == 0), stop=(kc == 31)), reads=[uTp[kc // 8], sl], writes=[fb_])
                                if kc % 8 == 7:
                                    yield "fm"
                        if own:
                            tiles = [w_in_tmh[h * 4 + q] for q in range(4)]
                            width = 256
                        else:
                            tiles = [w_in_tmp[h * 4 + q] for q in range(4)]
                            width = 128
                        n = 0
                        for q in range(4):
                            sl = ring.load(tiles[q], 8 * width)
                            for s in range(4):
                                for kk in range(8):
                                    kc = q * 8 + kk
                                    S.op("pe", lambda e, s=s, kc=kc, kk=kk, sl=sl, width=width: e.matmul(
                                        tm[s].t[:, 0:width], lhsT=uT.t[:, kc, s * 128:(s + 1) * 128],
                                        rhs=sl.t[:, kk * width:(kk + 1) * width],
                                        start=(kc == 0), stop=(kc == 31)), reads=[uTp[q], sl], writes=[tm[s]])
                                    n += 1
                                    if n % 8 == 0 and n < 128:
                                        yield "tm"
                        for s in range(4):
                            vsrc = tm[s].t[:, 0:128]
                            if own:
                                S.op("act", lambda e, s=s, vs=vs, vsrc=vsrc: e.copy(out=Vv[vs].t[:, s, :], in_=vsrc),
                                     reads=[tm[s]], writes=[Vv[vs]])
                                S.op("act", lambda e, s=s, vs=vs: e.activation(out=gs[vs].t[:, s, :], in_=tm[s].t[:, 128:256], func=AF.Silu),
                                     reads=[tm[s]], writes=[gs[vs]])
                            else:
                                copy_any(Vv[vs].t[:, s, :], vsrc, [tm[s]], [Vv[vs]])
                                continue
                            S.op("dve", lambda e, s=s, vs=vs, vsrc=vsrc: e.tensor_scalar(out=Ve[vs].t[:, s, :], in0=vsrc, scalar1=rme.t[:, 0:1],
                                                                                         scalar2=None, op0=ALU.mult), reads=[tm[s], rme], writes=[Ve[vs]])
                            S.op("dve", lambda e, s=s, vs=vs, vsrc=vsrc: e.tensor_scalar(out=Vo[vs].t[:, s, :], in0=vsrc, scalar1=rme.t[:, 1:2],
                                                                                         scalar2=None, op0=ALU.mult), reads=[tm[s], rme], writes=[Vo[vs]])
                        yield "tm"

                    def E_gen(h, own=own):
                        par = h % 2
                        t_qe, t_qee, t_qeo, t_ke, t_ke3, elast = t_qe2[par], t_qee2[par], t_qeo2[par], t_ke2[par], t_ke32[par], elast2[par]
                        S.op("act", lambda e: e.activation(out=t_sg.t[:], in_=fm[1].t[:], func=AF.Sigmoid), reads=[fm[1]], writes=[t_sg])
                        if own:
                            yield
                            S.op("act", lambda e: e.activation(out=t_qs.t[:], in_=fm[0].t[:], func=AF.Silu), reads=[fm[0]], writes=[t_qs])
                        yield
                        S.op("dve", lambda e, h=h: e.tensor_scalar(out=t_sg.t[:], in0=t_sg.t[:], scalar1=oml.t[:, h:h + 1], scalar2=lb.t[:, h:h + 1],
                                                                   op0=ALU.mult, op1=ALU.add), reads=[t_sg, oml, lb], writes=[t_sg])
                        yield
                        S.op("act", lambda e: e.activation(out=t_lf.t[:], in_=t_sg.t[:], func=AF.Ln), reads=[t_sg], writes=[t_lf])
                        yield
                        S.op("dve", lambda e: e.tensor_scalar(out=t_kk.t[:], in0=t_sg.t[:], scalar1=-1.0, scalar2=1.0,
                                                              op0=ALU.mult, op1=ALU.add), reads=[t_sg], writes=[t_kk])
                        if not own:
                            yield
                            S.op("dve", lambda e: e.tensor_tensor_scan(out=t_bb.t[:], data0=onesf.t[:, 0:1].to_broadcast([128, 512]), data1=t_lf.t[:], initial=0.0,
                                                                       op0=ALU.mult, op1=ALU.add), reads=[onesf, t_lf], writes=[t_bb])
                            yield
                            S.op("act", lambda e: e.activation(out=elast.t[:, 0:1], in_=t_bb.t[:, 511:512], func=AF.Exp), reads=[t_bb], writes=[elast])
                            yield
                            S.op("dve", lambda e: e.tensor_scalar(out=t_ex.t[:], in0=t_bb.t[:], scalar1=t_bb.t[:, 511:512], scalar2=-1.0,
                                                                  op0=ALU.subtract, op1=ALU.mult), reads=[t_bb], writes=[t_ex])
                            yield
                            S.op("act", lambda e: e.activation(out=t_ex.t[:], in_=t_ex.t[:], func=AF.Exp), reads=[t_ex], writes=[t_ex])
                            yield
                            S.op("dve", lambda e: e.tensor_tensor(out=t_ke3.t[:], in0=t_kk.t[:], in1=t_ex.t[:], op=ALU.mult),
                                 reads=[t_kk, t_ex], writes=[t_ke3])
                            yield
                            return
                        yield
                        S.op("dve", lambda e: e.tensor_tensor_scan(out=t_bb.t[:], data0=rmask.t[:], data1=t_lf.t[:], initial=0.0,
                                                                   op0=ALU.mult, op1=ALU.add), reads=[rmask, t_lf], writes=[t_bb])
                        bbv = t_bb.t[:].rearrange("p (c t) -> p c t", t=64)
                        yield
                        S.op("act", lambda e, bbv=bbv: e.activation(out=elast.t[:], in_=bbv[:, :, 63], func=AF.Exp), reads=[t_bb], writes=[elast])
                        yield
                        S.op("dve", lambda e, bbv=bbv: e.tensor_tensor(out=t_ex.t[:].rearrange("p (c t) -> p c t", t=64),
                                                                       in0=bbv[:, :, 63:64].to_broadcast([128, 8, 64]), in1=bbv, op=ALU.subtract),
                             reads=[t_bb], writes=[t_ex])
                        yield
                        S.op("act", lambda e: e.activation(out=t_ex.t[:], in_=t_ex.t[:], func=AF.Exp), reads=[t_ex], writes=[t_ex])
                        yield
                        S.op("dve", lambda e: e.tensor_tensor(out=t_ke3.t[:], in0=t_kk.t[:], in1=t_ex.t[:], op=ALU.mult),
                             reads=[t_kk, t_ex], writes=[t_ke3])
                        if own:
                            yield
                            S.op("act", lambda e: e.activation(out=t_ex.t[:], in_=t_bb.t[:], func=AF.Exp), reads=[t_bb, t_ke3], writes=[t_ex])
                            yield
                            S.op("dve", lambda e: e.tensor_tensor(out=t_qe.t[:], in0=t_qs.t[:], in1=t_ex.t[:], op=ALU.mult),
                                 reads=[t_qs, t_ex], writes=[t_qe])
                            yield
                            S.op("act", lambda e: e.activation(out=t_ex.t[:], in_=t_bb.t[:], func=AF.Exp, scale=-1.0), reads=[t_bb], writes=[t_ex])
                            yield
                            S.op("dve", lambda e: e.tensor_tensor(out=t_qee.t[:], in0=t_qe.t[:], in1=cme.t[:], op=ALU.mult),
                                 reads=[t_qe, cme], writes=[t_qee])
                            yield
                            S.op("dve", lambda e: e.tensor_tensor(out=t_qeo.t[:], in0=t_qe.t[:], in1=cmo.t[:], op=ALU.mult),
                                 reads=[t_qe, cmo], writes=[t_qeo])
                            yield
                            S.op("dve", lambda e: e.tensor_tensor(out=t_ke.t[:], in0=t_kk.t[:], in1=t_ex.t[:], op=ALU.mult),
                                 reads=[t_kk, t_ex], writes=[t_ke])

                    def P_gen(h, own=own, otok0=otok0):
                        vs = h % 2
                        oT = oTs[h % 2]
                        par = h % 2
                        t_qe, t_qee, t_qeo, t_ke, t_ke3, elast = t_qe2[par], t_qee2[par], t_qeo2[par], t_ke2[par], t_ke32[par], elast2[par]
                        Sebh = Seb2[par]
                        if own:
                            S.op("act", lambda e, h=h: e.copy(out=Sebh.t[:], in_=Sst[h].t[:]), reads=[Sst[h]], writes=[Sebh])
                        if not own:
                            for pr in range(4):
                                S.op("pe", lambda e, pr=pr: e.transpose(tpt2.t[:, pr * 128:(pr + 1) * 128], t_ke3.t[:, pr * 128:(pr + 1) * 128], ident.t[:]),
                                     reads=[t_ke3, ident], writes=[smT1])
                            S.op("act", lambda e: e.copy(out=ke3T4.t[:], in_=tpt2.t[:, 0:512]), reads=[smT1], writes=[ke3T4])
                            yield
                            for pr in range(4):
                                S.op("pe", lambda e, pr=pr, vs=vs: e.matmul(apS, lhsT=ke3T4.t[:, pr * 128:(pr + 1) * 128], rhs=Vv[vs].t[:, pr, :],
                                                                            start=(pr == 0), stop=(pr == 3)), reads=[ke3T4, Vv[vs]], writes=[smS])
                            S.op("dve", lambda e, h=h: e.scalar_tensor_tensor(out=Sst[h].t[:], in0=Sst[h].t[:], scalar=elast.t[:, 0:1],
                                                                              in1=apS, op0=ALU.mult, op1=ALU.add),
                                 reads=[Sst[h], elast, smS], writes=[Sst[h]])
                            yield
                            return
                        for pr in range(4):
                            cs = slice(pr * 128, pr * 128 + 128)
                            if own:
                                S.op("pe", lambda e, cs=cs: e.matmul(apA, lhsT=t_ke.t[:, cs], rhs=t_qe.t[:, cs], start=True, stop=True),
                                     reads=[t_ke, t_qe], writes=[smA])
                                S.op("dve", lambda e, cs=cs: e.tensor_tensor(out=t_Am.t[:, cs], in0=apA, in1=maskbd.t[:], op=ALU.mult),
                                     reads=[smA, maskbd], writes=[t_Am])
                            S.op("pe", lambda e, cs=cs: e.transpose(apT1, t_ke3.t[:, cs], ident.t[:]), reads=[t_ke3, ident], writes=[smT1])
                            S.op("act", lambda e: e.copy(out=ke3T.t[:], in_=apT1), reads=[smT1], writes=[ke3T])
                            yield
                            S.op("pe", lambda e, pr=pr, vs=vs: e.matmul(apS, lhsT=ke3T.t[:], rhs=Ve[vs].t[:, pr, :], start=True, stop=True),
                                 reads=[ke3T, Ve[vs]], writes=[smS])
                            S.op("dve", lambda e, h=h, pr=pr: e.scalar_tensor_tensor(out=Sst[h].t[:], in0=Sst[h].t[:], scalar=elast.t[:, 2 * pr:2 * pr + 1],
                                                                                     in1=apS, op0=ALU.mult, op1=ALU.add),
                                 reads=[Sst[h], elast, smS], writes=[Sst[h]])
                            if own:
                                S.op("act", lambda e, h=h: e.copy(out=Sob.t[:], in_=Sst[h].t[:]), reads=[Sst[h]], writes=[Sob])
                            yield
                            if own:
                                S.op("pe", lambda e, cs=cs, pr=pr, vs=vs: e.matmul(apO, lhsT=t_Am.t[:, cs], rhs=Vv[vs].t[:, pr, :], start=True, stop=False),
                                     reads=[t_Am, Vv[vs]], writes=[smO])
                                S.op("pe", lambda e, cs=cs, h=h: e.matmul(apO, lhsT=t_qee.t[:, cs], rhs=Sebh.t[:], start=False, stop=False),
                                     reads=[t_qee, Sebh], writes=[smO])
                                S.op("pe", lambda e, cs=cs: e.matmul(apO, lhsT=t_qeo.t[:, cs], rhs=Sob.t[:], start=False, stop=True),
                                     reads=[t_qeo, Sob], writes=[smO])
                            S.op("pe", lambda e, pr=pr, vs=vs: e.matmul(apS, lhsT=ke3T.t[:], rhs=Vo[vs].t[:, pr, :], start=True, stop=True),
                                 reads=[ke3T, Vo[vs]], writes=[smS])
                            S.op("dve", lambda e, h=h, pr=pr: e.scalar_tensor_tensor(out=Sst[h].t[:], in0=Sst[h].t[:], scalar=elast.t[:, 2 * pr + 1:2 * pr + 2],
                                                                                     in1=apS, op0=ALU.mult, op1=ALU.add),
                                 reads=[Sst[h], elast, smS], writes=[Sst[h]])
                            if own:
                                S.op("act", lambda e, h=h: e.copy(out=Sebh.t[:], in_=Sst[h].t[:]), reads=[Sst[h]], writes=[Sebh])
                                S.op("dve", lambda e: e.memset(sso.t[:], 0.0), writes=[sso])
                                S.op("act", lambda e: e.activation(out=junk.t[:], in_=apO, func=AF.Square, accum_out=sso.t[:]),
                                     reads=[smO, sso], writes=[junk, sso])
                                S.op("act", lambda e: e.activation(out=rso.t[:], in_=sso.t[:], func=AF.Sqrt, bias=epst.t[:], scale=1.0 / 128),
                                     reads=[sso, epst], writes=[rso])
                                S.op("dve", lambda e: e.reciprocal(out=rso.t[:], in_=rso.t[:]), reads=[rso], writes=[rso])
                                S.op("dve", lambda e, pr=pr, vs=vs: e.scalar_tensor_tensor(out=onb.t[:], in0=apO, scalar=rso.t[:], in1=gs[vs].t[:, pr, :],
                                                                                           op0=ALU.mult, op1=ALU.mult), reads=[smO, rso, gs[vs]], writes=[onb])
                            yield
                            if own:
                                yield
                                yield
                                S.op("pe", lambda e: e.transpose(apT2, onb.t[:], ident.t[:]), reads=[onb, ident], writes=[smT2])
                                S.op("dve", lambda e, cs=cs, h=h, oT=oT: e.tensor_scalar(out=oT.t[:, cs], in0=apT2, scalar1=g_hg.t[:, h:h + 1], scalar2=None, op0=ALU.mult),
                                     reads=[smT2, g_hg], writes=[oT])
                                yield
                        if own:
                            S.dma("sp", mix_d[2048 + h * 128: 2048 + (h + 1) * 128, otok0:otok0 + 512], oT.t[:],
                                  reads=[oT], writes=[MIX], sem_buf=oT, kind="r")

                    def drain(g):
                        for _ in g:
                            pass

                    def interleave3(g, p, e_, nfm):
                        gi = iter(g) if g is not None else None
                        pi = iter(p) if p is not None else None
                        ei = iter(e_) if e_ is not None else None
                        gcount = 0
                        while gi is not None or pi is not None or ei is not None:
                            if gi is not None:
                                if next(gi, None) is None:
                                    gi = None
                                gcount += 1
                            if pi is not None and next(pi, "end") == "end":
                                pi = None
                            if ei is not None and (gcount > nfm or gi is None):
                                for _ in range(KN_ESTEP):
                                    if next(ei, "end") == "end":
                                        ei = None
                                        break

                    drain(G_gen(0))
                    drain(E_gen(0))
                    nfm = 8 if own else 4
                    for i in range(nh):
                        if i + 1 < nh:
                            interleave3(G_gen(i + 1), P_gen(i), E_gen(i + 1), nfm)
                        else:
                            drain(P_gen(i))
                print("phase A", S.flush())

        if "B" in phases:
            with ExitStack() as pb_:
                S.stack = pb_
                ones = const("ones", [128, 128], BF16, ones_d, False)
                cmask = const("cmask", [128, 2048], BF16, cmask_d, False)
                pbias = const("pbias", [128, 1], F32, pbias_d, False)
                ring = Ring(S, 2, "wrb")
                cosq = S.sbuf("cosq", [64, 2048], F32)
                sinq = S.sbuf("sinq", [64, 2048], F32)
                posi = S.sbuf("posi", [64, 512], I32)
                ang = S.sbuf("ang", [64, 512], F32)
                tcs = S.sbuf("tcs", [64, 512], F32)
                tsn = S.sbuf("tsn", [64, 512], F32)
                for qt in range(4):
                    tb = 2048 + qt * 512
                    cs = slice(qt * 512, qt * 512 + 512)
                    rope_gen(posr[:, tb:tb + 512], posi, posi.t[:], ang, ang.t[:], tcs, tcs.t[:], tsn, tsn.t[:],
                             cosq, cosq.t[:, cs], sinq, sinq.t[:, cs])
                KT = S.sbuf("KT", [128, 4096], BF16)
                Vh = S.sbuf("Vh", [128, 32, 128], BF16)
                QN = S.sbuf("QN", [128, 2048], BF16)
                QR = S.sbuf("QR", [64, 2048], BF16)
                sq = S.sbuf("sq", [128, 512], BF16)
                PTs = [S.sbuf("PT%d" % i, [128, 512], BF16) for i in range(4)]
                rden = S.sbuf("rden", [128, 512], F32)
                OTs = [S.sbuf("OT%d" % i, [128, 512], BF16) for i in range(2)]
                red = S.sbuf("red", [128, 1], F32)
                krmax = S.sbuf("krmax", [128, 1], F32)
                knmax = S.sbuf("knmax", [128, 1], F32)
                qnmax = S.sbuf("qnmax", [128, 1], F32)
                qrmax = S.sbuf("qrmax", [128, 1], F32)
                negB = S.sbuf("negB", [128, 1], F32)
                negBp = S.sbuf("negBp", [128, 1], F32)
                stp = [S.psum("st%d" % i, [128, 512], F32) for i in range(3)]
                otp = S.psum("otp", [128, 512], F32)
                dnp = S.psum("dnp", [128, 512], F32)
                fm = [S.psum("fmb%d" % i, [128, 512], F32) for i in range(2)]
                vps = S.psum("vps", [128, 512], F32)
                nb = vps

                sq2 = S.sbuf("sq2", [128, 512], BF16)
                sqs = [sq, sq2]
                pend = []
                sqn = [0]

                def sqmax_flush(keep=0):
                    while len(pend) > keep:
                        sqb, nrows, acc = pend.pop(0)
                        S.op("pe", lambda e, sqb=sqb, nrows=nrows: e.matmul(nb.t[:], lhsT=ones.t[0:nrows, :], rhs=sqb.t[0:nrows, :], start=True, stop=True),
                             reads=[ones, sqb], writes=[nb])
                        S.op("dve", lambda e: e.tensor_reduce(out=red.t[:], in_=nb.t[:], axis=AX.X, op=ALU.max), reads=[nb], writes=[red])
                        S.op("dve", lambda e, acc=acc: e.tensor_tensor(out=acc.t[:], in0=acc.t[:], in1=red.t[:], op=ALU.max), reads=[acc, red], writes=[acc])

                def sqmax(src_buf, src_ap, nrows, acc):
                    sqmax_flush(0)
                    sqb = sqs[sqn[0] % 2]
                    sqn[0] += 1
                    S.op("dve", lambda e: e.tensor_tensor(out=sqb.t[0:nrows, :], in0=src_ap, in1=src_ap, op=ALU.mult),
                         reads=[src_buf], writes=[sqb])
                    pend.append((sqb, nrows, acc))

                S.op("dve", lambda e: e.memset(krmax.t[:], 0.0), writes=[krmax])
                for kt in range(8):
                    sqmax(krT, krT.t[:, kt * 512:(kt + 1) * 512], 64, krmax)
                sqmax_flush(0)

                for h in range(16):
                    sl = ring.load(w_attn[h][:, 0:2560], 2560) if False else ring.load(w_attn[h])
                    S.op("dve", lambda e: e.memset(knmax.t[:], 0.0), writes=[knmax])
                    S.op("dve", lambda e: e.memset(qnmax.t[:], 0.0), writes=[qnmax])
                    S.op("dve", lambda e: e.memset(qrmax.t[:], 0.0), writes=[qrmax])
                    for kt in range(8):
                        f = fm[kt % 2]
                        ks = slice(kt * 512, kt * 512 + 512)
                        for kc in range(4):
                            S.op("pe", lambda e, kc=kc, ks=ks, f=f, sl=sl: e.matmul(f.t[:], lhsT=sl.t[:, 1536 + kc * 128: 1536 + (kc + 1) * 128],
                                                                                   rhs=ckvnT.t[:, kc, ks], start=(kc == 0), stop=(kc == 3)),
                                 reads=[sl, ckvnT], writes=[f])
                        copy_any(KT.t[:, ks], f.t[:], [f], [KT])
                        sqmax(KT, KT.t[:, ks], 128, knmax)
                    for sg_ in range(8):
                        vb = (vps, fm[0], fm[1])[sg_ % 3]
                        for j in range(4):
                            st_ = sg_ * 4 + j
                            for kc in range(4):
                                S.op("pe", lambda e, kc=kc, st_=st_, j=j, sl=sl, vb=vb: e.matmul(vb.t[:, j * 128:(j + 1) * 128],
                                                                                         lhsT=ckvnT.t[:, kc, st_ * 128:(st_ + 1) * 128],
                                                                                         rhs=sl.t[:, 2048 + kc * 128: 2048 + (kc + 1) * 128],
                                                                                         start=(kc == 0), stop=(kc == 3)),
                                     reads=[sl, ckvnT], writes=[vb])
                        copy_any(Vh.t[:, sg_ * 4:(sg_ + 1) * 4, :], vb.t[:].rearrange("p (a b) -> p a b", b=128), [vb], [Vh])
                    for qt in range(4):
                        cs = slice(qt * 512, qt * 512 + 512)
                        f = fm[0]
                        for kc in range(6):
                            S.op("pe", lambda e, kc=kc, cs=cs, sl=sl: e.matmul(fm[0].t[:], lhsT=sl.t[:, kc * 128:(kc + 1) * 128], rhs=cqnT.t[:, kc, cs],
                                                                              start=(kc == 0), stop=(kc == 5)), reads=[sl, cqnT], writes=[fm[0]])
                        copy_any(QN.t[:, cs], fm[0].t[:], [fm[0]], [QN])
                        sqmax(QN, QN.t[:, cs], 128, qnmax)
                        for kc in range(6):
                            S.op("pe", lambda e, kc=kc, cs=cs, sl=sl: e.matmul(fm[1].t[0:64, :], lhsT=sl.t[:, 768 + kc * 128: 768 + kc * 128 + 64],
                                                                              rhs=cqnT.t[:, kc, cs], start=(kc == 0), stop=(kc == 5)),
                                 reads=[sl, cqnT], writes=[fm[1]])
                        S.op("dve", lambda e, cs=cs: e.tensor_tensor(out=tcs.t[:], in0=fm[1].t[0:64, :], in1=cosq.t[:, cs], op=ALU.mult),
                             reads=[fm[1], cosq], writes=[tcs])
                        for kc in range(6):
                            S.op("pe", lambda e, kc=kc, cs=cs, sl=sl: e.matmul(fm[1].t[0:64, :], lhsT=sl.t[:, 768 + kc * 128 + 64: 768 + kc * 128 + 128],
                                                                              rhs=cqnT.t[:, kc, cs], start=(kc == 0), stop=(kc == 5)),
                                 reads=[sl, cqnT], writes=[fm[1]])
                        S.op("dve", lambda e, cs=cs: e.tensor_tensor(out=tsn.t[:], in0=fm[1].t[0:64, :], in1=sinq.t[:, cs], op=ALU.mult),
                             reads=[fm[1], sinq], writes=[tsn])
                        S.op("dve", lambda e, cs=cs: e.tensor_tensor(out=QR.t[:, cs], in0=tcs.t[:], in1=tsn.t[:], op=ALU.add),
                             reads=[tcs, tsn], writes=[QR])
                        sqmax(QR, QR.t[:, cs], 64, qrmax)
                    sqmax_flush(0)
                    S.op("dve", lambda e: e.tensor_tensor(out=qnmax.t[:], in0=qnmax.t[:], in1=qrmax.t[:], op=ALU.add), reads=[qnmax, qrmax], writes=[qnmax])
                    S.op("dve", lambda e: e.tensor_tensor(out=knmax.t[:], in0=knmax.t[:], in1=krmax.t[:], op=ALU.add), reads=[knmax, krmax], writes=[knmax])
                    S.op("dve", lambda e: e.tensor_tensor(out=negB.t[:], in0=qnmax.t[:], in1=knmax.t[:], op=ALU.mult), reads=[qnmax, knmax], writes=[negB])
                    S.op("act", lambda e: e.activation(out=negB.t[:], in_=negB.t[:], func=AF.Sqrt), reads=[negB], writes=[negB])
                    S.op("dve", lambda e: e.tensor_scalar(out=negB.t[:], in0=negB.t[:], scalar1=-QSCALE, scalar2=None, op0=ALU.mult),
                         reads=[negB], writes=[negB])
                    S.op("dve", lambda e: e.tensor_tensor(out=negBp.t[:], in0=negB.t[:], in1=pbias.t[:], op=ALU.add), reads=[negB, pbias], writes=[negBp])
                    for qt in range(4):
                        cs = slice(qt * 512, qt * 512 + 512)
                        nkb = 16 + 4 * qt + 4
                        OT = OTs[qt % 2]

                        def qk(kb, cs=cs, qt=qt):
                            st_ = stp[kb % 3]
                            ks = slice(kb * 128, kb * 128 + 128)
                            S.op("pe", lambda e: e.matmul(st_.t[:], lhsT=KT.t[:, ks], rhs=QN.t[:, cs], start=True, stop=False),
                                 reads=[KT, QN], writes=[st_])
                            S.op("pe", lambda e: e.matmul(st_.t[:], lhsT=krT.t[:, ks], rhs=QR.t[:, cs], start=False, stop=True),
                                 reads=[krT, QR], writes=[st_])
                            PT = PTs[kb % 4]
                            bias = negBp if kb < 16 else negB
                            S.op("act", lambda e: e.activation(out=PT.t[:], in_=st_.t[:], func=AF.Exp, bias=bias.t[:], scale=QSCALE),
                                 reads=[st_, bias], writes=[PT])
                            j = kb - (16 + 4 * qt)
                            if j >= 0:
                                S.op("dve", lambda e: e.tensor_tensor(out=PT.t[:], in0=PT.t[:], in1=cmask.t[:, j * 512:(j + 1) * 512], op=ALU.mult),
                                     reads=[PT, cmask], writes=[PT])

                        def pv(kb, nkb=nkb):
                            PT = PTs[kb % 4]
                            S.op("pe", lambda e: e.matmul(otp.t[:], lhsT=Vh.t[:, kb, :], rhs=PT.t[:], start=(kb == 0), stop=(kb == nkb - 1)),
                                 reads=[Vh, PT], writes=[otp])
                            S.op("pe", lambda e: e.matmul(dnp.t[:], lhsT=ones.t[:], rhs=PT.t[:], start=(kb == 0), stop=(kb == nkb - 1)),
                                 reads=[ones, PT], writes=[dnp])

                        qk(0)
                        qk(1)
                        for kb in range(nkb):
                            if kb + 2 < nkb:
                                qk(kb + 2)
                            pv(kb)
                        S.op("dve", lambda e: e.reciprocal(out=rden.t[:], in_=dnp.t[:]), reads=[dnp], writes=[rden])
                        S.op("dve", lambda e, OT=OT: e.tensor_tensor(out=OT.t[:], in0=otp.t[:], in1=rden.t[:], op=ALU.mult),
                             reads=[otp, rden], writes=[OT])
                        S.dma("sp", mix_d[h * 128:(h + 1) * 128, qt * 512:(qt + 1) * 512], OT.t[:], reads=[OT], writes=[MIX],
                              sem_buf=OT, kind="r")
                print("phase B", S.flush())

        mid.close()
        finals = []
        if "C" in phases:
            with ExitStack() as pc:
                S.stack = pc
                g_mlp = const("g_mlp", [128, 32], F32, g_mlp_d, False)
                g_ple = const("g_ple", [128, 32], F32, g_ple_d, False)
                gbc = S.sbuf("gbc", [128, 4096], F32)
                g_post = g_fin = gbc
                ring = Ring(S, KN_RINGC, "wrc")
                hacc = [S.sbuf("hacc%d" % s, [128, 4096], F32) for s in range(4)]
                actT = S.sbuf("actT", [128, 32, 512], BF16)
                aTp = [Buf("aTp%d" % i) for i in range(4)]
                hids = [S.sbuf("hid%d" % i, [128, 8, 512], BF16) for i in range(2)]
                ybs = [S.sbuf("ybC%d" % i, [128, 4096], BF16) for i in range(2)]
                yb = ybs[0]
                wpe = S.sbuf("wpe", [128, 8192], BF16)
                for t in range(2):
                    S.dma("pool", wpe.t[:, t * 4096:(t + 1) * 4096].rearrange("p (a b) -> p a b", b=2048),
                          w_pe_t[t].rearrange("p (a b) -> p a b", b=2048), writes=[wpe], sem_buf=wpe)
                tmpr = [S.sbuf("tmpr%d" % i, [128, 512], F32) for i in range(2)]
                tmpg = S.sbuf("tmpg", [128, 512], F32)
                tmpe = S.sbuf("tmpe", [128, 512], F32)
                pf = S.sbuf("pf", [128, 256], F32)
                pbf = S.sbuf("pbf", [128, 256], BF16)
                pT = S.sbuf("pT", [128, 2, 512], BF16)
                sss = [S.sbuf("ssC%d" % i, [128, 1], F32) for i in range(2)]
                rstds = [S.sbuf("rstdC%d" % i, [128, 1], F32) for i in range(2)]
                ss, rstd = sss[0], rstds[0]
                sse = [S.sbuf("sse%d" % s, [128, 8], F32) for s in range(4)]
                rse = [S.sbuf("rse%d" % s, [128, 1], F32) for s in range(4)]
                tm = [S.psum("tmc%d" % i, [128, 512], F32) for i in range(4)]
                fm = [S.psum("fmc%d" % i, [128, 512], F32) for i in range(2)]
                tpt = S.psum("tpc", [128, 1024], BF16)
                tptb = S.psum("tpcb", [128, 1024], BF16)

                class _V2:
                    def __init__(self, t, off):
                        self.t_, self.off = t, off
                    def __getitem__(self, k):
                        rows, cols = k
                        return self.t_[rows, self.off + cols.start:self.off + cols.stop]
                tps = [Buf("tpa"), Buf("tpb")]
                tps[0].excl = tps[1].excl = True
                tps[0].t = _V2(tpt.t, 0)
                tps[1].t = _V2(tptb.t, 0)
                ones_c = const("ones_c", [128, 128], BF16, ones_d, False)

                def gemm_tm_c(tiles, evac):
                    for q in range(4):
                        sl = ring.load(tiles[q])
                        for s in range(4):
                            for kk in range(8):
                                kc = q * 8 + kk
                                S.op("pe", lambda e, s=s, kc=kc, kk=kk, sl=sl: e.matmul(
                                    tm[s].t[:], lhsT=actT.t[:, kc, s * 128:(s + 1) * 128], rhs=sl.t[:, kk * 512:(kk + 1) * 512],
                                    start=(kc == 0), stop=(kc == 31)), reads=[aTp[q], sl], writes=[tm[s]])
                    for s in range(4):
                        evac(s)

                def add_into(s, g, src_buf, src_ap):
                    S.op("dve", lambda e: e.tensor_tensor(out=hacc[s].t[:, g * 512:(g + 1) * 512], in0=hacc[s].t[:, g * 512:(g + 1) * 512],
                                                          in1=src_ap, op=ALU.add), reads=[hacc[s], src_buf], writes=[hacc[s]])

                for it in range(NTILE_OWN):
                    tok0 = it * 512
                    for s in range(4):
                        r0 = 2048 + tok0 + s * 128
                        S.dma("sp", hacc[s].t[:], xs[r0:r0 + 128, :], writes=[hacc[s]], sem_buf=hacc[s])
                    mixv = mix_d.rearrange("(c p) t -> p c t", p=128)

                    def load_mix(tk):
                        for j in range(4):
                            S.dma("sp", actT.t[:, 8 * j:8 * j + 8, :], mixv[:, 8 * j:8 * j + 8, tk:tk + 512], reads=[MIX], writes=[aTp[j]],
                                  sem_buf=aTp[j])
                    if it == 0:
                        load_mix(0)
                    for g in range(8):
                        gemm_tm_c([w_o_tm[g * 4 + q] for q in range(4)], lambda s, g=g: add_into(s, g, tm[s], tm[s].t[:]))
                    for s in range(4):
                        yb = ybs[s % 2]
                        norm_rows(hacc[s], hacc[s].t[:], 4096, yb, sss[s % 2], rstds[s % 2])
                        transpose_out(yb, 32, tps, lambda c, s=s: actT.t[:, c, s * 128:(s + 1) * 128], g_mlp, lambda c: [aTp[c // 8]])
                    def w_up_block(fb):
                        hid = hids[fb % 2]
                        for fc in range(8):
                            sl = ring.load(w_up_fm[fb * 8 + fc])
                            f = fm[fc % 2]
                            for kc in range(32):
                                S.op("pe", lambda e, kc=kc, f=f, sl=sl: e.matmul(f.t[:], lhsT=sl.t[:, kc * 128:(kc + 1) * 128], rhs=actT.t[:, kc, :],
                                                                                start=(kc == 0), stop=(kc == 31)), reads=[aTp[kc // 8], sl], writes=[f])
                            tr = tmpr[fc % 2]
                            S.op("dve", lambda e, f=f, tr=tr: e.tensor_scalar(out=tr.t[:], in0=f.t[:], scalar1=0.0, scalar2=None, op0=ALU.max),
                                 reads=[f], writes=[tr])
                            S.op("act", lambda e, fc=fc, tr=tr, hid=hid: e.activation(out=hid.t[:, fc, :], in_=tr.t[:], func=AF.Square), reads=[tr], writes=[hid])

                    def w_down_block(fb):
                        hid = hids[fb % 2]
                        for g in range(8):
                            sl = ring.load(w_dn_tm[fb * 8 + g])
                            for s in range(4):
                                for kk in range(8):
                                    S.op("pe", lambda e, s=s, kk=kk, sl=sl, hid=hid: e.matmul(tm[s].t[:], lhsT=hid.t[:, kk, s * 128:(s + 1) * 128],
                                                                                             rhs=sl.t[:, kk * 512:(kk + 1) * 512], start=(kk == 0), stop=(kk == 7)),
                                         reads=[hid, sl], writes=[tm[s]])
                                add_into(s, g, tm[s], tm[s].t[:])

                    w_up_block(0)
                    for fb in range(16):
                        if fb + 1 < 16:
                            w_up_block(fb + 1)
                        w_down_block(fb)
                    S.dma("sp", gbc.t[:], g_post_d, writes=[gbc], sem_buf=gbc)
                    for s in range(4):
                        yb = ybs[s % 2]
                        norm_rows(hacc[s], hacc[s].t[:], 4096, yb, sss[s % 2], rstds[s % 2])
                        transpose_out(yb, 32, tps, lambda c, s=s: actT.t[:, c, s * 128:(s + 1) * 128], g_ple, lambda c: [aTp[c // 8]])
                        r0 = tok0 + s * 128
                        S.dma("sp", pf.t[:], pp[r0:r0 + 128, :], writes=[pf], sem_buf=pf)
                        S.op("dve", lambda e: e.tensor_copy(out=pbf.t[:], in_=pf.t[:]), reads=[pf], writes=[pbf])
                        for kk in range(2):
                            S.op("pe", lambda e, kk=kk: e.transpose(tps[0].t[:, kk * 128:(kk + 1) * 128], pbf.t[:, kk * 128:(kk + 1) * 128], ident.t[:]),
                                 reads=[pbf, ident], writes=[tps[0]])
                        S.op("act", lambda e, s=s: e.copy(out=pT.t[:, :, s * 128:(s + 1) * 128],
                                                          in_=tps[0].t[:, 0:256].rearrange("p (a b) -> p a b", b=128)), reads=[tps[0]], writes=[pT])

                    def emm(s, g, f):
                        for kk in range(2):
                            S.op("pe", lambda e, kk=kk: e.matmul(f.t[:], lhsT=pT.t[:, kk, s * 128:(s + 1) * 128],
                                                                 rhs=wpe.t[:, g * 1024 + kk * 512: g * 1024 + (kk + 1) * 512], start=(kk == 0), stop=(kk == 1)),
                                 reads=[pT, wpe], writes=[f])
                    for s in range(4):
                        S.op("dve", lambda e, s=s: e.memset(sse[s].t[:], 0.0), writes=[sse[s]])
                        for g in range(8):
                            f = fm[g % 2]
                            emm(s, g, f)
                            S.op("act", lambda e, s=s, g=g, f=f: e.activation(out=tmpg.t[:], in_=f.t[:], func=AF.Square, accum_out=sse[s].t[:, g:g + 1]),
                                 reads=[f, sse[s]], writes=[tmpg, sse[s]])
                        S.op("dve", lambda e, s=s: e.tensor_reduce(out=rse[s].t[:], in_=sse[s].t[:], axis=AX.X, op=ALU.add), reads=[sse[s]], writes=[rse[s]])
                        rstd_from_ss(rse[s], rse[s], 4096)

                    for g in range(8):
                        def evg(s, g=g):
                            f = fm[s % 2]
                            emm(s, g, f)
                            S.op("act", lambda e: e.activation(out=tmpg.t[:], in_=tm[s].t[:], func=AF.Sigmoid), reads=[tm[s]], writes=[tmpg])
                            S.op("dve", lambda e: e.scalar_tensor_tensor(out=tmpe.t[:], in0=f.t[:], scalar=rse[s].t[:], in1=g_post.t[:, g * 512:(g + 1) * 512],
                                                                         op0=ALU.mult, op1=ALU.mult), reads=[f, rse[s], g_post], writes=[tmpe])
                            S.op("dve", lambda e: e.tensor_tensor(out=tmpe.t[:], in0=tmpe.t[:], in1=tmpg.t[:], op=ALU.mult), reads=[tmpe, tmpg], writes=[tmpe])
                            add_into(s, g, tmpe, tmpe.t[:])
                        gemm_tm_c([w_pg_tm[g * 4 + q] for q in range(4)], evg)
                    if it + 1 < NTILE_OWN:
                        load_mix(tok0 + 512)
                    S.dma("sp", gbc.t[:], g_fin_d, writes=[gbc], sem_buf=gbc)
                    for s in range(4):
                        yb, ss, rstd = ybs[s % 2], sss[s % 2], rstds[s % 2]
                        S.op("dve", lambda e, ss=ss: e.memset(ss.t[:], 0.0), writes=[ss])
                        S.op("act", lambda e, s=s, yb=yb, ss=ss: e.activation(out=yb.t[:], in_=hacc[s].t[:], func=AF.Square, accum_out=ss.t[:]),
                             reads=[hacc[s], ss], writes=[yb, ss])
                        rstd_from_ss(rstd, ss, 4096)
                        S.op("dve", lambda e, s=s, rstd=rstd: e.scalar_tensor_tensor(out=hacc[s].t[:], in0=hacc[s].t[:], scalar=rstd.t[:], in1=g_fin.t[:],
                                                                          op0=ALU.mult, op1=ALU.mult), reads=[hacc[s], rstd, g_fin], writes=[hacc[s]])
                        r0 = tok0 + s * 128
                        finals.append(S.dma("sp", out_d[r0:r0 + 128, :], hacc[s].t[:], reads=[hacc[s]], sem_buf=hacc[s], kind="r"))
                print("phase C", S.flush(final_waits=finals))
    return nc


def _fm_tiles(w):
    K, N = w.shape
    kc = K // 128
    t = w.reshape(kc, 128, N // 128, 128).transpose(2, 1, 0, 3)
    return np.ascontiguousarray(t).reshape(N // 128, 128, kc * 128)


def _tm_tiles(w, width=512):
    K, N = w.shape
    nq = K // 1024
    t = w.reshape(nq, 8, 128, N // width, width).transpose(3, 0, 2, 1, 4)
    return np.ascontiguousarray(t).reshape((N // width) * nq, 128, 8 * width)


_CACHE = {}


def _prep_shared(inp):
    f32 = np.float32
    w_in = np.asarray(inp["w_in"], f32)[0]
    QL, KVL = 768, 512
    o_kr = QL + KVL
    o_hq = o_kr + 64
    o_hf = o_hq + 2048
    o_hi = o_hf + 2048
    o_hg = o_hi + 2048
    kr = w_in[:, o_kr:o_kr + 64]
    krs = np.concatenate([kr[:, 32:64], kr[:, 0:32]], axis=1)
    fm_cols = [np.concatenate([kr, krs], axis=1)]
    for h in range(16):
        fm_cols.append(w_in[:, o_hq + h * 128: o_hq + (h + 1) * 128])
        fm_cols.append(w_in[:, o_hf + h * 128: o_hf + (h + 1) * 128])
    w_in_fm = _fm_tiles(np.concatenate(fm_cols, axis=1))
    def pad_tiles(t):
        o = np.zeros((t.shape[0], 128, 4096), f32)
        o[:, :, :t.shape[2]] = t
        return o
    tm_list = [_tm_tiles(w_in[:, 0:512]), pad_tiles(_tm_tiles(w_in[:, 512:768], 256)),
               pad_tiles(_tm_tiles(w_in[:, 768:1024], 256)), pad_tiles(_tm_tiles(w_in[:, 1024:1280], 256))]
    w_in_tm = np.concatenate(tm_list, axis=0)
    w_in_tmh = np.concatenate([_tm_tiles(np.concatenate([w_in[:, o_hi + h * 128: o_hi + (h + 1) * 128],
                                                         w_in[:, o_hg + h * 128: o_hg + (h + 1) * 128]], axis=1), 256)
                               for h in range(16)], axis=0)
    w_in_tmp = np.concatenate([_tm_tiles(w_in[:, o_hi + h * 128: o_hi + (h + 1) * 128], 128) for h in range(16)], axis=0)
    w_uq = np.asarray(inp["w_uq"], f32)[0]
    w_ukv = np.asarray(inp["w_ukv"], f32)[0]
    w_attn = np.zeros((16, 128, 4096), f32)
    for h in range(16):
        qn = w_uq[:, h * 192: h * 192 + 128]
        qr = w_uq[:, h * 192 + 128: h * 192 + 192]
        qrs = np.concatenate([qr[:, 32:64], qr[:, 0:32]], axis=1)
        w_attn[h, :, 0:768] = _fm_tiles(qn)[0]
        w_attn[h, :, 768:1536] = _fm_tiles(np.concatenate([qr, qrs], axis=1))[0]
        uk = w_ukv[:, h * 256: h * 256 + 128]
        uv = w_ukv[:, h * 256 + 128: h * 256 + 256]
        w_attn[h, :, 1536:2048] = _fm_tiles(uk)[0]
        w_attn[h, :, 2048:2560] = _fm_tiles(uv)[0]
    w_pe = np.asarray(inp["w_ple"], f32)[0]
    t = w_pe.reshape(2, 128, 8, 512).transpose(1, 2, 0, 3)
    w_pe_t = np.ascontiguousarray(t).reshape(128, 2, 4096).transpose(1, 0, 2)
    def col(v, n):
        return np.ascontiguousarray(np.asarray(v, f32).reshape(n, 128).T)
    lbr = np.asarray(inp["hg_lower_bound"], f32)
    k = np.arange(128)
    maskbd = ((k[:, None] // 64 == k[None, :] // 64) & (k[:, None] <= k[None, :])).astype(ml_dtypes.bfloat16)
    t512 = np.arange(512)
    cme = np.broadcast_to(((t512 // 64) % 2 == 0)[None, :], (128, 512)).astype(ml_dtypes.bfloat16)
    cmo = np.broadcast_to(((t512 // 64) % 2 == 1)[None, :], (128, 512)).astype(ml_dtypes.bfloat16)
    rme = np.stack([(k < 64), (k >= 64)], axis=1).astype(f32)
    rmask = np.broadcast_to((t512 % 64 != 0)[None, :], (128, 512)).astype(f32)
    cm = np.zeros((128, 4, 512), ml_dtypes.bfloat16)
    for j in range(4):
        cm[:, j, :] = ((j * 128 + k)[:, None] <= t512[None, :])
    inv = (10000.0 ** (-np.arange(0, 64, 2, dtype=f32) / 64)).astype(f32)
    invf = np.stack([np.concatenate([inv, inv]), np.concatenate([-np.ones(32, f32), np.ones(32, f32)])], axis=1).astype(f32)
    sh = dict(
        w_in_fm=w_in_fm, w_in_tm=w_in_tm, w_in_tmh=w_in_tmh, w_in_tmp=w_in_tmp, w_attn=w_attn,
        w_o_tm=_tm_tiles(np.asarray(inp["w_o"], f32)[0]),
        w_up_fm=_fm_tiles(np.asarray(inp["w_up"], f32)[0]),
        w_dn_tm=None, w_pg_tm=_tm_tiles(np.asarray(inp["w_ple_gate"], f32)[0]),
        w_pe_t=np.ascontiguousarray(w_pe_t),
        g_mix=col(inp["norm_mix"][0], 32), g_mlp=col(inp["norm_mlp"][0], 32), g_ple=col(inp["norm_ple"][0], 32),
        g_qa=col(inp["q_a_norm"][0], 6), g_kva=col(inp["kv_a_norm"][0], 4), g_hg=col(inp["hg_out_norm"][0], 16),
        lbraw=np.ascontiguousarray(np.concatenate([lbr[0].reshape(16, 128).T, lbr[1].reshape(16, 128).T], axis=1)),
        g_post=np.ascontiguousarray(np.broadcast_to(np.asarray(inp["ple_post_norm"], f32)[0][None, :], (128, 4096))),
        g_fin=np.ascontiguousarray(np.broadcast_to(np.asarray(inp["final_norm"], f32)[None, :], (128, 4096))),
        ident=np.eye(128).astype(ml_dtypes.bfloat16), ones=np.ones((128, 128), ml_dtypes.bfloat16),
        maskbd=maskbd, cme=np.ascontiguousarray(cme), cmo=np.ascontiguousarray(cmo), rme=rme, rmask=np.ascontiguousarray(rmask),
        cmask=np.ascontiguousarray(cm.reshape(128, 2048)), invf=invf,
    )
    wd = np.asarray(inp["w_down"], f32)[0]
    t = wd.reshape(16, 8, 128, 8, 512).transpose(0, 3, 2, 1, 4)
    sh["w_dn_tm"] = np.ascontiguousarray(t).reshape(128, 128, 4096)
    return sh


def _per_core(inp, c):
    b, half = c // 2, c % 2
    x = np.asarray(inp["x"], np.float32)
    pos = np.asarray(inp["positions"], np.int32)
    own = slice(half * 2048, half * 2048 + 2048)
    xs = np.zeros((4096, 4096), np.float32)
    ps = np.zeros((4096,), np.int32)
    if half == 1:
        xs[0:2048] = x[b, 0:2048]
        ps[0:2048] = pos[b, 0:2048]
    xs[2048:] = x[b, own]
    ps[2048:] = pos[b, own]
    return dict(
        xs=xs, posr=np.ascontiguousarray(np.broadcast_to(ps[None, :], (64, 4096))),
        pp=np.ascontiguousarray(np.asarray(inp["p"], np.float32)[0, b, own]),
        pbias=np.full((128, 1), 0.0 if half == 1 else -30000.0, np.float32),
    )


def kernel(**inputs):
    sh = _prep_shared(inputs)
    nc = build_program()
    in_maps = []
    for c in range(8):
        m = dict(sh)
        m.update(_per_core(inputs, c))
        in_maps.append(m)
    res = run_bass_kernel_spmd(nc, in_maps, core_ids=list(range(8)))
    out = np.zeros((4, 4096, 4096), np.float32)
    for c in range(8):
        b, half = c // 2, c % 2
        out[b, half * 2048:(half + 1) * 2048] = res.results[c]["out"]
    return out
```
